# Optimizing a Trainium2 kernel written in Bass

```python
import numpy as np
import jax
import jax.numpy as jnp
from jax import lax

D_MODEL = 1024
BATCH = 8
SEQ = 2048
DEPTH = 1
DEC_BATCH = 8
DEC_SEQ = 32
PAST_LEN = 4096

CHUNK = 64
N_MEM = 256
EPS = 1e-6
A_HEADS = 4
A_HEAD_DIM = 128
A_WIDTH = A_HEADS * A_HEAD_DIM
B_WIDTH = D_MODEL
B_BLOCKS = 16
B_BLOCK_DIM = B_WIDTH // B_BLOCKS
CONV_W = 4
LRU_C = 8.0
C_HEADS = 4
C_HEAD_DIM = 128
C_WIDTH = C_HEADS * C_HEAD_DIM
N_BRANCH = 3
IN_COLS = 4 * A_WIDTH + 2 * B_WIDTH + 2 * C_WIDTH + N_BRANCH * D_MODEL

kernel_name = 'hybrid_hgrn2_rglru_memxattn_stream_step'


def _rmsnorm(x, g):
    xf = x.astype(jnp.float32)
    y = xf * lax.rsqrt(jnp.mean(xf * xf, axis=-1, keepdims=True) + EPS) * g.astype(jnp.float32)
    return y.astype(x.dtype)


def _hgrn2(q, logf, k, v, s0):
    bsz, seq, nh, _ = q.shape
    dv = v.shape[-1]
    c = min(CHUNK, seq)
    n = seq // c

    def blocks(t):
        return t.reshape(bsz, n, c, nh, t.shape[-1]).transpose(1, 0, 3, 2, 4)

    mask = jnp.tril(jnp.ones((c, c), dtype=bool))

    def step(s, inp):
        qc, lc, kc, vc = inp
        b = jnp.cumsum(lc, axis=2)
        qg = qc * jnp.exp(b)
        kg = kc * jnp.exp(-b)
        att = jnp.where(mask, jnp.einsum('bhtk,bhsk->bhts', qg, kg), 0.0)
        o = jnp.einsum('bhts,bhsv->bhtv', att, vc) + jnp.einsum('bhtk,bhkv->bhtv', qg, s)
        b_last = b[:, :, -1:, :]
        s = jnp.exp(b_last[:, :, 0, :])[..., None] * s + jnp.einsum('bhsk,bhsv->bhkv', kc * jnp.exp(b_last - b), vc)
        return s, o

    s_last, o = lax.scan(step, s0, (blocks(q), blocks(logf), blocks(k), blocks(v)))
    o = o.transpose(1, 0, 3, 2, 4).reshape(bsz, seq, nh, dv)
    return o, s_last


def _lru_scan(a, u, h0):
    u = u.at[:, 0].add(a[:, 0] * h0)

    def comb(left, right):
        al, ul = left
        ar, ur = right
        return al * ar, ar * ul + ur

    _, hs = lax.associative_scan(comb, (a, u), axis=1)
    return hs, hs[:, -1]


def _memory_kv(mem, g_mem, w_mem_k, w_mem_v):
    bsz = mem.shape[0]
    hm = _rmsnorm(mem, g_mem)
    mk = (hm @ w_mem_k).reshape(bsz, N_MEM, C_HEADS, C_HEAD_DIM)
    mv = (hm @ w_mem_v).reshape(bsz, N_MEM, C_HEADS, C_HEAD_DIM)
    return mk, mv


def _mixer_layer(x, mem_k, mem_v, s_hgrn, conv_ctx, h_lru, first_is_start, lb,
                 g_mix, w_in, g_a_out, w_a_down, w_conv, b_conv, w_lru_r, b_lru_r,
                 w_lru_i, b_lru_i, lru_lambda, w_b_down, w_c_down, w_out):
    f32 = jnp.float32
    bsz, seq, _ = x.shape
    h = _rmsnorm(x, g_mix)
    z = h @ w_in
    sizes = (A_WIDTH, A_WIDTH, A_WIDTH, A_WIDTH, B_WIDTH, B_WIDTH, C_WIDTH, C_WIDTH, D_MODEL, D_MODEL, D_MODEL)
    offs = [int(o) for o in np.cumsum(sizes)[:-1]]
    qa, fa, va, ga, xb, gb, qc, gc, za, zb, zc = jnp.split(z, offs, axis=-1)

    head4 = (bsz, seq, A_HEADS, A_HEAD_DIM)
    f = lb + (1.0 - lb) * jax.nn.sigmoid(fa.astype(f32))
    oa, s_hgrn_new = _hgrn2(jax.nn.silu(qa.astype(f32)).reshape(head4), jnp.log(f).reshape(head4),
                            (1.0 - f).reshape(head4), va.astype(f32).reshape(head4), s_hgrn.astype(f32))
    oa = _rmsnorm(oa, g_a_out.reshape(A_HEADS, A_HEAD_DIM)).reshape(bsz, seq, A_WIDTH).astype(x.dtype)
    pa = (oa * jax.nn.silu(ga)) @ w_a_down

    xp = jnp.concatenate([conv_ctx.astype(xb.dtype), xb], axis=1)
    xc = b_conv + sum(w_conv[j] * xp[:, j:j + seq] for j in range(CONV_W))
    conv_new = xp[:, seq:]
    xcb = xc.astype(f32).reshape(bsz, seq, B_BLOCKS, B_BLOCK_DIM)
    r = jax.nn.sigmoid(jnp.einsum('blnd,nde->blne', xcb, w_lru_r.astype(f32)) + b_lru_r.astype(f32).reshape(B_BLOCKS, B_BLOCK_DIM))
    ig = jax.nn.sigmoid(jnp.einsum('blnd,nde->blne', xcb, w_lru_i.astype(f32)) + b_lru_i.astype(f32).reshape(B_BLOCKS, B_BLOCK_DIM))
    log_a = -LRU_C * r * jax.nn.softplus(-lru_lambda.astype(f32).reshape(B_BLOCKS, B_BLOCK_DIM))
    mult = jnp.sqrt(-jnp.expm1(2.0 * log_a))
    if first_is_start:
        mult = jnp.where(jnp.arange(seq)[None, :, None, None] == 0, 1.0, mult)
    u = (mult * ig * xcb).reshape(bsz, seq, B_WIDTH)
    hb, h_lru_new = _lru_scan(jnp.exp(log_a).reshape(bsz, seq, B_WIDTH), u, h_lru.astype(f32))
    pb = (hb.astype(x.dtype) * jax.nn.silu(gb)) @ w_b_down

    qch = qc.reshape(bsz, seq, C_HEADS, C_HEAD_DIM)
    sc = jnp.einsum('blhd,bmhd->bhlm', qch, mem_k).astype(f32) * (C_HEAD_DIM ** -0.5)
    pr = jax.nn.softmax(sc, axis=-1).astype(x.dtype)
    oc = jnp.einsum('bhlm,bmhd->blhd', pr, mem_v).reshape(bsz, seq, C_WIDTH)
    pc = (oc * jax.nn.silu(gc)) @ w_c_down

    merged = jax.nn.sigmoid(za) * pa + jax.nn.sigmoid(zb) * pb + jax.nn.sigmoid(zc) * pc
    y = x + merged @ w_out
    return y, s_hgrn_new, conv_new, h_lru_new


def setup_inputs(seed: int = 0) -> dict:
    key = jax.random.key(seed)
    ks = jax.random.split(key, 32)
    f32 = jnp.float32

    def nrm(k, shape, s):
        return jax.random.normal(k, shape, f32) * s

    u = jax.random.uniform(ks[19], (DEPTH, B_WIDTH), f32, 0.9, 0.999)
    sg = u ** (1.0 / LRU_C)
    return {
        'x_prompt': nrm(ks[0], (BATCH, SEQ, D_MODEL), 1.0),
        'x_sample': nrm(ks[1], (DEC_BATCH, DEC_SEQ, D_MODEL), 1.0),
        'mem_prompt': nrm(ks[2], (BATCH, N_MEM, D_MODEL), 1.0),
        'cache_mem_k': nrm(ks[3], (DEPTH, DEC_BATCH, N_MEM, C_HEADS, C_HEAD_DIM), 1.0),
        'cache_mem_v': nrm(ks[4], (DEPTH, DEC_BATCH, N_MEM, C_HEADS, C_HEAD_DIM), 1.0),
        'state_hgrn': nrm(ks[5], (DEPTH, DEC_BATCH, A_HEADS, A_HEAD_DIM, A_HEAD_DIM), 0.5),
        'state_conv': nrm(ks[6], (DEPTH, DEC_BATCH, CONV_W - 1, B_WIDTH), 1.0),
        'state_lru': nrm(ks[7], (DEPTH, DEC_BATCH, B_WIDTH), 0.5),
        'g_mix': 1.0 + nrm(ks[8], (DEPTH, D_MODEL), 0.05),
        'w_in': nrm(ks[9], (DEPTH, D_MODEL, IN_COLS), D_MODEL ** -0.5),
        'lb_logits': nrm(ks[10], (DEPTH + 1, A_WIDTH), 0.1),
        'g_a_out': 1.0 + nrm(ks[11], (DEPTH, A_WIDTH), 0.05),
        'w_a_down': nrm(ks[12], (DEPTH, A_WIDTH, D_MODEL), A_WIDTH ** -0.5),
        'w_conv': nrm(ks[13], (DEPTH, CONV_W, B_WIDTH), CONV_W ** -0.5),
        'b_conv': nrm(ks[14], (DEPTH, B_WIDTH), 0.01),
        'w_lru_r': nrm(ks[15], (DEPTH, B_BLOCKS, B_BLOCK_DIM, B_BLOCK_DIM), B_BLOCK_DIM ** -0.5),
        'b_lru_r': nrm(ks[16], (DEPTH, B_WIDTH), 0.01),
        'w_lru_i': nrm(ks[17], (DEPTH, B_BLOCKS, B_BLOCK_DIM, B_BLOCK_DIM), B_BLOCK_DIM ** -0.5),
        'b_lru_i': nrm(ks[18], (DEPTH, B_WIDTH), 0.01),
        'lru_lambda': jnp.log(sg) - jnp.log1p(-sg),
        'w_b_down': nrm(ks[20], (DEPTH, B_WIDTH, D_MODEL), B_WIDTH ** -0.5),
        'g_mem': 1.0 + nrm(ks[21], (DEPTH, D_MODEL), 0.05),
        'w_mem_k': nrm(ks[22], (DEPTH, D_MODEL, C_WIDTH), D_MODEL ** -0.5),
        'w_mem_v': nrm(ks[23], (DEPTH, D_MODEL, C_WIDTH), D_MODEL ** -0.5),
        'w_c_down': nrm(ks[24], (DEPTH, C_WIDTH, D_MODEL), C_WIDTH ** -0.5),
        'w_out': nrm(ks[25], (DEPTH, D_MODEL, D_MODEL), D_MODEL ** -0.5),
        'g_final': 1.0 + nrm(ks[26], (D_MODEL,), 0.05),
    }


def reference(x_prompt, x_sample, mem_prompt, cache_mem_k, cache_mem_v, state_hgrn, state_conv, state_lru,
              g_mix, w_in, lb_logits, g_a_out, w_a_down, w_conv, b_conv, w_lru_r, b_lru_r, w_lru_i, b_lru_i,
              lru_lambda, w_b_down, g_mem, w_mem_k, w_mem_v, w_c_down, w_out, g_final):
    f32 = jnp.float32
    lb_all = jnp.cumsum(jax.nn.softmax(lb_logits.astype(f32), axis=0), axis=0)
    bp = x_prompt.shape[0]
    hp, hs = x_prompt, x_sample
    p_hgrn, p_conv, p_lru, p_mk, p_mv = [], [], [], [], []
    s_hgrn, s_conv, s_lru = [], [], []
    for l in range(DEPTH):
        lw = (g_mix[l], w_in[l], g_a_out[l], w_a_down[l], w_conv[l], b_conv[l], w_lru_r[l], b_lru_r[l],
              w_lru_i[l], b_lru_i[l], lru_lambda[l], w_b_down[l], w_c_down[l], w_out[l])
        mk, mv = _memory_kv(mem_prompt, g_mem[l], w_mem_k[l], w_mem_v[l])
        hp, a1, c1, r1 = _mixer_layer(hp, mk, mv,
                                      jnp.zeros((bp, A_HEADS, A_HEAD_DIM, A_HEAD_DIM), f32),
                                      jnp.zeros((bp, CONV_W - 1, B_WIDTH), x_prompt.dtype),
                                      jnp.zeros((bp, B_WIDTH), f32),
                                      True, lb_all[l], *lw)
        hs, a2, c2, r2 = _mixer_layer(hs, cache_mem_k[l], cache_mem_v[l], state_hgrn[l], state_conv[l], state_lru[l],
                                      False, lb_all[l], *lw)
        p_hgrn.append(a1.astype(x_prompt.dtype))
        p_conv.append(c1.astype(x_prompt.dtype))
        p_lru.append(r1.astype(x_prompt.dtype))
        p_mk.append(mk)
        p_mv.append(mv)
        s_hgrn.append(a2.astype(state_hgrn.dtype))
        s_conv.append(c2.astype(state_conv.dtype))
        s_lru.append(r2.astype(state_lru.dtype))
    y_prompt = _rmsnorm(hp, g_final)
    y_sample = _rmsnorm(hs, g_final)
    return (y_prompt, y_sample, jnp.stack(p_hgrn), jnp.stack(p_conv), jnp.stack(p_lru), jnp.stack(p_mk),
            jnp.stack(p_mv), jnp.stack(s_hgrn), jnp.stack(s_conv), jnp.stack(s_lru))
```

```python
import numpy as np
from contextlib import ExitStack
import concourse.bass as bass
import concourse.mybir as mybir
from concourse.bass_utils import run_bass_kernel_spmd

F32 = mybir.dt.float32
BF16 = mybir.dt.bfloat16
AF = mybir.ActivationFunctionType
ALU = mybir.AluOpType

D = 1024
SEQ = 2048
DSEQ = 32
NMEM = 256
EPS = 1e-6
KC = 8
HCH = 128
TOK = SEQ + DSEQ

O_QA, O_FA, O_VA, O_GA, O_XB, O_GB, O_QC, O_GC, O_ZA, O_ZB, O_ZC = (
    0, 512, 1024, 1536, 2048, 3072, 4096, 4608, 5120, 6144, 7168)

PV_GMIX, PV_GMEM, PV_L0, PV_L1, PV_GA, PV_WC, PV_BC, PV_BR, PV_BI, PV_LAM = (
    0, 8, 16, 20, 24, 28, 60, 68, 76, 84)
NPV = 92


class Tok:
    __slots__ = ("name", "w", "rs", "rdma", "pre", "excl")

    def __init__(self, name):
        self.name = name
        self.w = None
        self.rs = {}
        self.rdma = []
        self.pre = []
        self.excl = False


class Op:
    __slots__ = ("idx", "eng", "fn", "deps", "is_dma", "slot", "cum", "need", "ticket", "nofuse", "vc", "waits")


ENGS = ("pe", "act", "dve", "pool", "sp")


WSTAT = {}


class Prog:
    def __init__(self, nc, es):
        self.nc = nc
        self.es = es
        self.ops = []
        self.eng_ops = {e: [] for e in ENGS}
        self.slot_cum = {}
        self.slot_hist = {}
        self.slot_sem = {}
        self.eng_sem = {}

    def add(self, eng, fn, reads=(), writes=(), slot=None, nofuse=False):
        op = Op()
        op.nofuse = nofuse
        op.idx = len(self.ops)
        op.eng = eng
        op.fn = fn
        op.is_dma = slot is not None
        op.slot = slot
        op.need = False
        op.ticket = None
        op.cum = None
        deps = {}

        def dep(o, kind):
            if o is None:
                return
            if kind == "raw" or o not in deps:
                deps[o] = kind

        for t in reads:
            dep(t.w, "raw")
            if t.excl:
                for e2, r in t.rs.items():
                    if e2 != eng:
                        dep(r, "raw")
        for t in writes:
            dep(t.w, "raw")
            for r in t.rs.values():
                dep(r, "war")
            for r in t.rdma:
                dep(r, "raw")
            for r in t.pre:
                dep(r, "raw")
        op.deps = deps
        for t in reads:
            if op.is_dma:
                t.rdma.append(op)
            else:
                t.rs[eng] = op
        for t in writes:
            t.w = op
            t.rs = {}
            t.rdma = []
        if op.is_dma:
            c = self.slot_cum.get(slot, 0) + 16
            self.slot_cum[slot] = c
            op.cum = c
            self.slot_hist.setdefault(slot, []).append((op.idx, c))
        self.ops.append(op)
        self.eng_ops[eng].append(op)
        return op

    def _resolve(self):
        for op in self.ops:
            for d, kind in op.deps.items():
                if d.is_dma:
                    continue
                if d.eng == op.eng and not op.is_dma:
                    if op.eng == "pe" or kind == "war":
                        continue
                d.need = True
        for e in ENGS:
            n = 0
            for op in self.eng_ops[e]:
                if op.need and not op.is_dma:
                    n += 1
                    op.ticket = n

    def _plan_waits(self):
        vcE = {e: {} for e in ENGS}
        for op in self.ops:
            K = vcE[op.eng]
            need = {}
            for d, kind in op.deps.items():
                if d.is_dma:
                    key = ("slot", d.slot)
                    val = self._slot_wait_value(d.slot, op.idx)
                else:
                    if d.eng == op.eng and not op.is_dma and (op.eng == "pe" or kind == "war"):
                        continue
                    key = ("eng", d.eng)
                    val = d.ticket
                if key not in need or val > need[key][0]:
                    need[key] = (val, d)
            out = []
            for key, (val, d) in sorted(need.items(), key=lambda kv: -kv[1][1].idx):
                if K.get(key, 0) >= val:
                    continue
                out.append((key, val))
                K[key] = val
                if d.vc is not None:
                    for k2, v2 in d.vc.items():
                        if v2 > K.get(k2, 0):
                            K[k2] = v2
            op.waits = out
            if op.is_dma:
                op.vc = dict(K)
                op.vc[("slot", op.slot)] = op.cum
            elif op.ticket is not None:
                op.vc = dict(K)
                op.vc[("eng", op.eng)] = op.ticket
            else:
                op.vc = None

    def _slot_wait_value(self, slot, before_idx):
        v = 0
        for idx, c in self.slot_hist[slot]:
            if idx < before_idx:
                v = c
            else:
                break
        return v

    def alloc_sems(self):
        nc = self.nc
        for e in ("pe", "act", "dve", "pool"):
            self.eng_sem[e] = self.es.enter_context(nc.semaphore("s_" + e))
        for i, s in enumerate(self.slot_hist):
            self.slot_sem[s] = self.es.enter_context(nc.semaphore("d%d" % i))

    def emit(self, e, eng):
        for op in self.eng_ops[e]:
            pend = []
            for key, val in op.waits:
                sem = self.slot_sem[key[1]] if key[0] == "slot" else self.eng_sem[key[1]]
                pend.append((sem, val))
                WSTAT[e] = WSTAT.get(e, 0) + 1
            fuse = None
            if pend and e in ("act", "dve", "pool") and not op.is_dma and not op.nofuse:
                fuse = pend.pop()
            for sem, val in pend:
                eng.wait_ge(sem, val)
            ins = op.fn(eng)
            if fuse is not None:
                ins._wait_ge(fuse[0], fuse[1])
            if op.is_dma:
                ins.then_inc(self.slot_sem[op.slot], 16)
            elif op.need:
                ins.then_inc(self.eng_sem[e], 1)
        if e == "sp":
            for s, c in self.slot_cum.items():
                eng.wait_ge(self.slot_sem[s], c)

    def run(self):
        self._resolve()
        self._plan_waits()
        self.alloc_sems()
        nc = self.nc
        with nc.Block() as block:
            @block.tensor
            def _(eng):
                self.emit("pe", eng)

            @block.scalar
            def _(eng):
                self.emit("act", eng)

            @block.vector
            def _(eng):
                self.emit("dve", eng)

            @block.gpsimd
            def _(eng):
                self.emit("pool", eng)

            @block.sync
            def _(eng):
                self.emit("sp", eng)


class Arena:
    def __init__(self, nc, base=16512, top=229344):
        self.nc = nc
        self.off = base
        self.top = top
        self.n = 0
        self.peak = base

    def alloc(self, name, shape, dtype):
        isz = 2 if dtype == BF16 else 4
        nb = isz
        for s in shape[1:]:
            nb *= s
        nb = (nb + 31) // 32 * 32
        off = self.off
        assert off + nb <= self.top, ("SBUF overflow", name, off + nb - self.top)
        self.off += nb
        self.peak = max(self.peak, self.off)
        self.n += 1
        return self.nc.alloc_sbuf_tensor_at("%s_%d" % (name, self.n), list(shape), dtype, offset=off)

    def mark(self):
        return self.off

    def release(self, m):
        self.off = m


class Buf:
    def __init__(self, t, name):
        self.t = t
        self.k = Tok(name)


def build(phases="0ABMCD", dbg=()):
    nc = bass.Bass("TRN2", target_bir_lowering=False)
    es = ExitStack()
    P = Prog(nc, es)
    ar = Arena(nc)

    def din(name, shape):
        return nc.dram_tensor(name, list(shape), F32, kind="ExternalInput").ap()

    def dout(name, shape):
        return nc.dram_tensor(name, list(shape), F32, kind="ExternalOutput").ap()

    xp = din("xp", [SEQ, D]); xsm = din("xs", [DSEQ, D]); memd = din("mem", [NMEM, D])
    cmk = din("cmk", [NMEM, 512]); cmv = din("cmv", [NMEM, 512])
    s_hg = din("s_hg", [4, 128, 128]); s_cv = din("s_cv", [128, 8, 3]); s_lr = din("s_lr", [128, 8])
    w_in = din("w_in", [D, 8192]); w_ad = din("w_ad", [512, D]); w_bd = din("w_bd", [D, D])
    w_cd = din("w_cd", [512, D]); w_o = din("w_o", [D, D]); w_mk = din("w_mk", [D, 512]); w_mv = din("w_mv", [D, 512])
    wr_d = din("wr_bd", [128, 8, 128]); wi_d = din("wi_bd", [128, 8, 128])
    pv_d = din("pv", [128, NPV]); gf_d = din("gfin", [128, D])
    id_d = din("ident", [128, 128]); mk_d = din("amask", [128, 128]); rm_d = din("rmask", [128, 512])

    y_p = dout("y_p", [SEQ, D]); y_s = dout("y_s", [DSEQ, D])
    o_hg_p = dout("o_hg_p", [4, 128, 128]); o_cv_p = dout("o_cv_p", [128, 8, 3]); o_lr_p = dout("o_lr_p", [128, 8])
    o_mk = dout("o_mk", [NMEM, 512]); o_mv = dout("o_mv", [NMEM, 512])
    o_hg_s = dout("o_hg_s", [4, 128, 128]); o_cv_s = dout("o_cv_s", [128, 8, 3]); o_lr_s = dout("o_lr_s", [128, 8])

    PB = [Buf(nc.alloc_psum_tensor("pb%d" % i, [128, 512], F32), "pb%d" % i) for i in range(8)]
    for b_ in PB:
        b_.k.excl = True
    zrot = [0]
    zlist = [PB[0], PB[1], PB[2]]

    def zbank():
        b = zlist[zrot[0] % len(zlist)]
        zrot[0] += 1
        return b
    TB = PB[3]

    all_bufs = []

    def sb(name, shape, dtype):
        o0 = ar.off
        b = Buf(ar.alloc(name, shape, dtype), name)
        o1 = ar.off
        for (p0, p1, ob) in all_bufs:
            if p0 < o1 and o0 < p1:
                k = ob.k
                for r in [k.w] + list(k.rs.values()) + k.rdma + k.pre:
                    if r is not None and r not in b.k.pre:
                        b.k.pre.append(r)
        all_bufs.append((o0, o1, b))
        return b

    hT = ar.alloc("hT", [128, KC, TOK], BF16)
    hTk = [Tok("hT%d" % i) for i in range(17)]
    mrg = ar.alloc("mrg", [128, KC, TOK], BF16)
    mrgk = [Tok("mrg%d" % i) for i in range(5)]
    ident = sb("ident", [128, 128], BF16)
    ones = sb("ones", [128, 128], BF16)
    amask = sb("amask", [128, 128], F32)
    rmask = sb("rmask", [128, 512], F32)
    pv = sb("pv", [128, NPV], F32)
    dv = sb("dv", [128, 80], F32)
    DEFER_A3 = ("A" in phases) and ("B" in phases)
    oagT_keep = sb("oagT1", [128, 4, 512], BF16) if "A" in phases else None
    DV_LB, DV_OML, DV_CL, DV_T, DV_HBR, DV_HBI, DV_HCL, DV_HOML, DV_LBH, DV_NHOML = 0, 4, 8, 16, 32, 40, 48, 56, 60, 64

    NR = 9
    ring = [sb("ring%d" % i, [128, 4096], BF16) for i in range(NR)]

    def pvc(col):
        return pv.t[:, col:col + 1]

    def dvc(col):
        return dv.t[:, col:col + 1]

    def wv(ap, kc):
        return ap.rearrange("(c p) n -> p c n", p=128)

    units = {
        "mk": wv(w_mk, 8), "mv": wv(w_mv, 8),
        "fa": wv(w_in[:, O_FA:O_FA + 512], 8), "qa": wv(w_in[:, O_QA:O_QA + 512], 8),
        "ga": wv(w_in[:, O_GA:O_GA + 512], 8), "va": wv(w_in[:, O_VA:O_VA + 512], 8),
        "za0": wv(w_in[:, O_ZA:O_ZA + 512], 8), "za1": wv(w_in[:, O_ZA + 512:O_ZA + 1024], 8),
        "ad": wv(w_ad, 4),
        "xb0": wv(w_in[:, O_XB:O_XB + 512], 8), "xb1": wv(w_in[:, O_XB + 512:O_XB + 1024], 8),
        "gb0": wv(w_in[:, O_GB:O_GB + 512], 8), "gb1": wv(w_in[:, O_GB + 512:O_GB + 1024], 8),
        "zb0": wv(w_in[:, O_ZB:O_ZB + 512], 8), "zb1": wv(w_in[:, O_ZB + 512:O_ZB + 1024], 8),
        "bd0": wv(w_bd[:, 0:512], 8), "bd1": wv(w_bd[:, 512:1024], 8),
        "qc": wv(w_in[:, O_QC:O_QC + 512], 8), "gc": wv(w_in[:, O_GC:O_GC + 512], 8),
        "zc0": wv(w_in[:, O_ZC:O_ZC + 512], 8), "zc1": wv(w_in[:, O_ZC + 512:O_ZC + 1024], 8),
        "cd": wv(w_cd, 4),
        "o0": wv(w_o[:, 0:512], 8), "o1": wv(w_o[:, 512:1024], 8),
    }
    plan = []
    if "A" in phases:
        plan += ["fa", "qa", "ga", "va", "za0", "za1", "ad"]
    if "B" in phases:
        plan += ["xb0", "xb1", "gb0", "gb1", "zb0", "zb1", "bd0", "bd1"]
    if "M" in phases:
        plan += ["mk", "mv"]
    if "C" in phases:
        plan += ["qc", "gc", "zc0", "zc1", "cd"]
    if "D" in phases:
        plan += ["o0", "o1"]
    free_slots = list(range(NR))
    where = {}

    def pump(limit=None, gate=()):
        n_ = 0
        while plan and free_slots and (limit is None or n_ < limit):
            n_ += 1
            u = plan.pop(0)
            s = free_slots.pop(0)
            where[u] = s
            src = units[u]
            a = src.shape[1]
            dst = ring[s].t[:, :].rearrange("p (a b) -> p a b", a=a)
            P.add("pool", lambda e, dst=dst, src=src: e.dma_start(out=dst, in_=src),
                  reads=list(gate), writes=[ring[s].k], slot=("ring", s))

    def W(u, a=8):
        s = where[u]
        return ring[s].t[:, :].rearrange("p (a b) -> p a b", a=a), ring[s].k

    def wfree(u):
        free_slots.append(where.pop(u))
        pump()

    def ACT(out, in_, func, reads, writes, bias=0.0, scale=1.0, accum=None):
        if accum is None:
            P.add("act", lambda e: e.activation(out=out, in_=in_, func=func, bias=bias, scale=scale),
                  reads=reads, writes=writes)
        else:
            P.add("act", lambda e: e.activation(out=out, in_=in_, func=func, bias=bias, scale=scale,
                                               accum_out=accum), reads=reads, writes=writes, nofuse=True)

    def TS(out, in0, s1, s2, op0, op1, reads, writes, eng="dve"):
        if s2 is None:
            P.add(eng, lambda e: e.tensor_scalar(out=out, in0=in0, scalar1=s1, scalar2=None, op0=op0),
                  reads=reads, writes=writes)
        else:
            P.add(eng, lambda e: e.tensor_scalar(out=out, in0=in0, scalar1=s1, scalar2=s2, op0=op0, op1=op1),
                  reads=reads, writes=writes)

    def TT(out, in0, in1, op, reads, writes, eng="dve"):
        P.add(eng, lambda e: e.tensor_tensor(out=out, in0=in0, in1=in1, op=op), reads=reads, writes=writes)

    def STT(out, in0, sc, in1, op0, op1, reads, writes, eng="dve"):
        if eng == "pool":
            P.add("pool", lambda e: e.tensor_scalar(out=out, in0=in0, scalar1=sc, scalar2=1.0, op0=op0, op1=ALU.mult),
                  reads=reads, writes=writes)
            P.add("pool", lambda e: e.tensor_tensor(out=out, in0=out, in1=in1, op=op1), reads=reads + writes, writes=writes)
            return
        P.add("dve", lambda e: e.scalar_tensor_tensor(out=out, in0=in0, scalar=sc, in1=in1, op0=op0, op1=op1),
              reads=reads, writes=writes)

    def CP(out, in_, reads, writes, eng="dve"):
        if eng == "act":
            P.add(eng, lambda e: e.activation(out=out, in_=in_, func=AF.Copy), reads=reads, writes=writes)
        else:
            P.add(eng, lambda e: e.tensor_copy(out=out, in_=in_), reads=reads, writes=writes)

    def MM(out, lhsT, rhs, start, stop, reads, writes):
        P.add("pe", lambda e: e.matmul(out, lhsT, rhs, start=start, stop=stop, skip_group_check=True),
              reads=reads, writes=writes)

    def TR(out, in_, idn, reads, writes):
        P.add("pe", lambda e: e.transpose(out, in_, idn), reads=reads, writes=writes)

    def DMA(out, in_, reads, writes, slot, q="sp"):
        P.add(q, lambda e: e.dma_start(out=out, in_=in_), reads=reads, writes=writes, slot=slot)

    def hTtoks(t0, T):
        return [hTk[i] for i in range(t0 // 128, (t0 + T + 127) // 128)]

    def zmm(u, col0, t0, T, bank, ncol=128):
        wt, wk = W(u)
        rd = [wk] + hTtoks(t0, T)
        for kc in range(KC):
            MM(bank.t[:ncol, :T], wt[:, kc, col0:col0 + ncol], hT[:, kc, t0:t0 + T],
               kc == 0, kc == KC - 1, rd, [bank.k])

    dbg_list = []

    def DBG(name, ap, shape, toks, dtype=BF16):
        if name in dbg:
            d = nc.dram_tensor("dbg_" + name, list(shape), dtype, kind="ExternalOutput").ap()
            DMA(d, ap, toks, [], ("dbg", name))

    DMA(pv.t[:, :], pv_d, [], [pv.k], "c_pv")
    DMA(amask.t[:, :], mk_d, [], [amask.k], "c_am")
    DMA(rmask.t[:, :], rm_d, [], [rmask.k], "c_rm")
    P.add("pool", lambda e: e.dma_start(out=ident.t[:, :], in_=id_d), writes=[ident.k], slot="c_id")
    P.add("dve", lambda e: e.memset(ones.t[:, :], 1.0), writes=[ones.k])
    pump(limit=3)
    TT(dv.t[:, DV_T:DV_T + 4], pv.t[:, PV_L0:PV_L0 + 4], pv.t[:, PV_L1:PV_L1 + 4], ALU.subtract, [pv.k], [dv.k])
    ACT(dv.t[:, DV_LB:DV_LB + 4], dv.t[:, DV_T:DV_T + 4], AF.Sigmoid, [dv.k], [dv.k])
    TS(dv.t[:, DV_OML:DV_OML + 4], dv.t[:, DV_LB:DV_LB + 4], -1.0, 1.0, ALU.mult, ALU.add, [dv.k], [dv.k])
    TS(dv.t[:, DV_HOML:DV_HOML + 4], dv.t[:, DV_LB:DV_LB + 4], -0.5, 0.5, ALU.mult, ALU.add, [dv.k], [dv.k])
    TS(dv.t[:, DV_LBH:DV_LBH + 4], dv.t[:, DV_LB:DV_LB + 4], 0.5, 0.5, ALU.mult, ALU.add, [dv.k], [dv.k])
    TS(dv.t[:, DV_NHOML:DV_NHOML + 4], dv.t[:, DV_LB:DV_LB + 4], 0.5, -0.5, ALU.mult, ALU.add, [dv.k], [dv.k])
    ACT(dv.t[:, DV_T:DV_T + 8], pv.t[:, PV_LAM:PV_LAM + 8], AF.Exp, [pv.k, dv.k], [dv.k], scale=-1.0)
    ACT(dv.t[:, DV_T + 8:DV_T + 16], dv.t[:, DV_T:DV_T + 8], AF.Ln, [dv.k], [dv.k], bias=1.0)
    TS(dv.t[:, DV_CL:DV_CL + 8], dv.t[:, DV_T + 8:DV_T + 16], -8.0, None, ALU.mult, None, [dv.k], [dv.k])
    TS(dv.t[:, DV_HCL:DV_HCL + 8], dv.t[:, DV_T + 8:DV_T + 16], -4.0, None, ALU.mult, None, [dv.k], [dv.k])
    TS(dv.t[:, DV_HBR:DV_HBR + 8], pv.t[:, PV_BR:PV_BR + 8], 0.5, None, ALU.mult, None, [pv.k, dv.k], [dv.k])
    TS(dv.t[:, DV_HBI:DV_HBI + 8], pv.t[:, PV_BI:PV_BI + 8], 0.5, None, ALU.mult, None, [pv.k, dv.k], [dv.k])

    groups = [(0, 512), (512, 512), (1024, 512), (1536, 512), (2048, 32)]

    m0 = ar.mark()
    NB = {}

    def alloc_norm(tag):
        NB["xt"] = [sb("xt%s%d" % (tag, i), [128, D], F32) for i in range(2)]
        NB["junk"] = sb("junk" + tag, [128, D], BF16)
        NB["xn"] = [sb("xn%s%d" % (tag, i), [128, D], BF16) for i in range(2)]
        NB["st8"] = [sb("st8%s_%d" % (tag, i), [128, 8], F32) for i in range(2)]
        NB["tag"] = tag

    alloc_norm("0")
    TBv = TB.t[:, :].bitcast(BF16).rearrange("p (a b) -> p a b", a=8)
    cnt = [0]

    def norm_a(src, nrows):
        i = cnt[0] % 2
        cnt[0] += 1
        x, xb_, s8, junk = NB["xt"][i], NB["xn"][i], NB["st8"][i], NB["junk"]
        DMA(x.t[:nrows, :], src, [], [x.k], ("xt" + NB["tag"], i))
        ACT(junk.t[:nrows, :], x.t[:nrows, :], AF.Square, [x.k], [junk.k, s8.k], accum=s8.t[:nrows, 0:1])
        ACT(s8.t[:nrows, 1:2], s8.t[:nrows, 0:1], AF.Sqrt, [s8.k], [s8.k], bias=EPS, scale=1.0 / D)
        P.add("dve", lambda e: e.reciprocal(out=s8.t[:nrows, 2:3], in_=s8.t[:nrows, 1:2]), reads=[s8.k], writes=[s8.k])
        ACT(xb_.t[:nrows, :], x.t[:nrows, :], AF.Copy, [x.k, s8.k], [xb_.k], scale=s8.t[:nrows, 2:3])
        return xb_

    def norm_b(xb_, nrows, gcol, dstT, dtok):
        for kc in range(KC):
            TR(TBv[:, kc, 0:nrows], xb_.t[:nrows, kc * 128:(kc + 1) * 128], ident.t[:nrows, :nrows],
               [xb_.k, ident.k], [TB.k])
        gb_ = pv.t[:, gcol:gcol + 8].unsqueeze(2).to_broadcast([128, 8, nrows])
        TT(dstT, TBv[:, :, 0:nrows], gb_, ALU.mult, [TB.k, pv.k], [dtok])

    def norm_T(src, nrows, gcol, dstT, dtok):
        norm_b(norm_a(src, nrows), nrows, gcol, dstT, dtok)

    def norm_many(items):
        prev = None
        for it in items:
            xb_ = norm_a(it[0], it[1])
            if prev is not None:
                norm_b(*prev)
            prev = (xb_, it[1], it[2], it[3], it[4])
        norm_b(*prev)

    norm_many([(xp[i * 128:(i + 1) * 128, :], 128, PV_GMIX, hT[:, :, i * 128:(i + 1) * 128], hTk[i]) for i in range(16)]
              + [(xsm[:, :], DSEQ, PV_GMIX, hT[:, :, SEQ:SEQ + DSEQ], hTk[16])])
    DBG("hT", hT[:, :, :], [128, KC, TOK], hTk)
    pump(gate=[hTk[16]])

    ar.release(m0)

    MKV = {}
    def phase_M():
        alloc_norm("M")
        hmT = sb("hmT", [128, KC, NMEM], BF16)
        norm_many([(memd[j * 128:(j + 1) * 128, :], 128, PV_GMEM, hmT.t[:, :, j * 128:(j + 1) * 128], hmT.k)
                   for j in range(2)])
        wk_, wkk = W("mk")
        wv_, wvk = W("mv")
        for h in range(4):
            b = zbank()
            for kc in range(KC):
                MM(b.t[:, :NMEM], wk_[:, kc, h * 128:(h + 1) * 128], hmT.t[:, kc, :], kc == 0, kc == KC - 1,
                   [wkk, hmT.k], [b.k])
            CP(kT_p.t[:, h, :], b.t[:, :NMEM], [b.k], [kT_p.k], eng="act")
        stg = [sb("stg%d" % i, [128, 512], F32) for i in range(2)]
        n = 0
        for (wt_, wtk, od, isv) in ((wk_, wkk, o_mk, False), (wv_, wvk, o_mv, True)):
            for j in range(2):
                b = zbank()
                for kc in range(KC):
                    MM(b.t[:, :], hmT.t[:, kc, j * 128:(j + 1) * 128], wt_[:, kc, :], kc == 0, kc == KC - 1,
                       [wtk, hmT.k], [b.k])
                s = stg[n % 2]
                n += 1
                CP(s.t[:, :], b.t[:, :], [b.k], [s.k], eng="act")
                if isv:
                    CP(v_p.t[:, j, :], b.t[:, :], [b.k], [v_p.k], eng="dve")
                DMA(od[j * 128:(j + 1) * 128, :], s.t[:, :], [s.k], [], ("stg", n % 2))
        wfree("mk")
        wfree("mv")
        cmkb = sb("cmkb", [128, 2, 512], BF16)
        P.add("pool", lambda e: e.dma_start(out=cmkb.t[:, :, :], in_=cmk.rearrange("(j p) n -> p j n", p=128)),
              writes=[cmkb.k], slot="c_cmk")
        P.add("pool", lambda e: e.dma_start(out=v_s.t[:, :, :], in_=cmv.rearrange("(j p) n -> p j n", p=128)),
              writes=[v_s.k], slot="c_cmv")
        for h in range(4):
            for j in range(2):
                TR(TBv[:, h * 2 + j, :], cmkb.t[:, j, h * 128:(h + 1) * 128], ident.t[:, :], [cmkb.k, ident.k], [TB.k])
        CP(kT_s.t[:, :, :], TB.t[:, :].bitcast(BF16).rearrange("p (a b) -> p a b", a=4), [TB.k], [kT_s.k])
        DBG("kT_p", kT_p.t[:, :, :], [128, 4, 256], [kT_p.k])
        DBG("kT_s", kT_s.t[:, :, :], [128, 4, 256], [kT_s.k])


    if not any(p in phases for p in "ABC"):
        for g in range(5):
            t0, T = groups[g]
            P.add("dve", lambda e, t0=t0, T=T: e.memset(mrg[:, :, t0:t0 + T], 0.0), writes=[mrgk[g]])


    def mkpool(n, name):
        bufs = [sb("%s%d" % (name, i), [128, 512], F32) for i in range(n)]
        c = [0]

        def get():
            b = bufs[c[0] % n]
            c[0] += 1
            return b
        return get

    mrg_init = [False] * 5

    def merge(g, j, t0, T, pbank, tz):
        dst = mrg[:, j, t0:t0 + T]
        if not mrg_init[g]:
            STT(dst, tz.t[:, :T], 1.0, pbank.t[:, :T], ALU.add, ALU.mult, [pbank.k, tz.k], [mrgk[g]])
        else:
            STT(pbank.t[:, :T], tz.t[:, :T], 1.0, pbank.t[:, :T], ALU.add, ALU.mult, [pbank.k, tz.k], [pbank.k])
            TT(dst, dst, pbank.t[:, :T], ALU.add, [pbank.k, mrgk[g]], [mrgk[g]])

    if "A" in phases:
        mA = ar.mark()
        ATT, OB, KVB, XB = PB[4], PB[5], PB[6], PB[7]
        poolXa = mkpool(6, "tAx")
        poolYa = mkpool(2, "tAy")
        SETS = []
        for s_ in range(2):
            SETS.append(dict(
                qgT=sb("qgT%d" % s_, [128, 4, 512], BF16), kgT=sb("kgT%d" % s_, [128, 4, 512], BF16),
                sga=sb("sga%d" % s_, [128, 4, 512], BF16), dec=sb("dec%d" % s_, [128, 4, 8], F32)))
        kd_tm1 = sb("kd_tm", [128, 4, 512], BF16); v_tm1 = sb("v_tm", [128, 4, 512], BF16)
        for s_ in range(2):
            SETS[s_]["kd_tm"] = kd_tm1
            SETS[s_]["v_tm"] = v_tm1
        kdT = sb("kdT", [128, 4, 512], BF16)
        attm = [sb("attm0", [128, 4, 128], BF16)] * 2
        S_p = sb("S_p", [128, 4, 128], F32); S_s = S_p
        Sb = [sb("Sb%d" % i, [128, 4, 128], BF16) for i in range(2)]
        sq = sb("sqA", [128, 512], BF16)
        oagTs = [sb("oagT0", [128, 4, 512], BF16), oagT_keep]
        P.add("dve", lambda e: e.memset(S_p.t[:, :, :], 0.0), writes=[S_p.k])
        sbi = [0]

        def geo(g):
            t0, T = groups[g]
            TSZ = min(128, T); CS = min(HCH, T)
            return t0, T, TSZ, T // TSZ, CS, TSZ // CS, T // CS

        def XA(g, h):
            t0, T, TSZ, NT, CS, CPT, NCH = geo(g)
            st = SETS[g % 2]
            last = g == 4
            tmp = poolXa
            bf_ = zbank(); zmm("fa", h * 128, t0, T, bf_)
            A_ = tmp(); ACT(A_.t[:, :T], bf_.t[:, :T], AF.Tanh, [bf_.k], [A_.k], scale=0.5)
            yield
            bq_ = zbank(); zmm("qa", h * 128, t0, T, bq_)
            Q_ = tmp(); ACT(Q_.t[:, :T], bq_.t[:, :T], AF.Tanh, [bq_.k], [Q_.k], scale=0.5)
            STT(Q_.t[:, :T], Q_.t[:, :T], 1.0, bq_.t[:, :T], ALU.add, ALU.mult, [Q_.k, bq_.k], [Q_.k])
            yield
            bg_ = zbank(); zmm("ga", h * 128, t0, T, bg_)
            G_ = tmp(); ACT(G_.t[:, :T], bg_.t[:, :T], AF.Tanh, [bg_.k], [G_.k], scale=0.5)
            STT(st["sga"].t[:, h, :T], G_.t[:, :T], 1.0, bg_.t[:, :T], ALU.add, ALU.mult, [G_.k, bg_.k], [st["sga"].k])
            if last and h == 3:
                wfree("fa"); wfree("qa"); wfree("ga")
            yield
            C_ = tmp(); ACT(C_.t[:, :T], A_.t[:, :T], AF.Ln, [A_.k, dv.k], [C_.k], bias=dvc(DV_LBH + h), scale=dvc(DV_HOML + h))
            K_ = G_
            TS(K_.t[:, :T], A_.t[:, :T], dvc(DV_NHOML + h), dvc(DV_HOML + h), ALU.mult, ALU.add, [A_.k, G_.k, dv.k], [K_.k])
            yield
            D_ = tmp()
            P.add("dve", lambda e, D_=D_, C_=C_, T=T: e.tensor_tensor_scan(
                out=D_.t[:, :T], data0=rmask.t[:, :T], data1=C_.t[:, :T], initial=0.0, op0=ALU.mult, op1=ALU.add),
                reads=[rmask.k, C_.k], writes=[D_.k])
            yield
            E_ = A_
            ACT(E_.t[:, :T], D_.t[:, :T], AF.Exp, [D_.k, A_.k], [E_.k])
            F_ = tmp(); ACT(F_.t[:, :T], D_.t[:, :T], AF.Exp, [D_.k], [F_.k], scale=-1.0)
            yield
            e3 = E_.t[:, :T].rearrange("p (c s) -> p c s", s=CS)
            CP(st["dec"].t[:, h, 0:NCH].unsqueeze(2), e3[:, :, CS - 1:CS], [E_.k], [st["dec"].k])
            TT(F_.t[:, :T], K_.t[:, :T], F_.t[:, :T], ALU.mult, [K_.k, F_.k], [F_.k], eng="pool")
            TT(st["qgT"].t[:, h, :T], Q_.t[:, :T], E_.t[:, :T], ALU.mult, [Q_.k, E_.k], [st["qgT"].k], eng="pool")
            yield
            CP(st["kgT"].t[:, h, :T], F_.t[:, :T], [F_.k], [st["kgT"].k], eng="act")
            f3 = F_.t[:, :T].rearrange("p (c s) -> p c s", s=CS)
            TT(kdT.t[:, h, :T].rearrange("p (c s) -> p c s", s=CS), f3, e3[:, :, CS - 1:CS].to_broadcast([128, NCH, CS]),
               ALU.mult, [F_.k, E_.k], [kdT.k], eng="pool")
            yield
            if last and h == 3:
                pass

        def XA_fin(g):
            t0, T, TSZ, NT, CS, CPT, NCH = geo(g)
            st = SETS[g % 2]
            last = g == 4
            wva, wvak = W("va")
            for h in range(NT):
                b = zbank()
                c0 = t0 + h * TSZ
                for kc in range(KC):
                    MM(b.t[:TSZ, :], hT[:, kc, c0:c0 + TSZ], wva[:, kc, :], kc == 0, kc == KC - 1,
                       [wvak] + hTtoks(c0, TSZ), [b.k])
                CP(st["v_tm"].t[:TSZ, h, :], b.t[:TSZ, :], [b.k], [st["v_tm"].k], eng="act")
            if last:
                wfree("va")
            for i in range(NT):
                for h in range(4):
                    TR(TBv[:TSZ, h, :], kdT.t[:, h, i * TSZ:(i + 1) * TSZ], ident.t[:, :], [kdT.k, ident.k], [TB.k])
                CP(st["kd_tm"].t[:TSZ, i, :].rearrange("p (a b) -> p a b", a=4), TBv[:TSZ, 0:4, :], [TB.k], [st["kd_tm"].k])

        def YA(g, i):
            t0, T, TSZ, NT, CS, CPT, NCH = geo(g)
            st = SETS[g % 2]
            qgT, kgT, kd_tm, v_tm, sga, dec = st["qgT"], st["kgT"], st["kd_tm"], st["v_tm"], st["sga"], st["dec"]
            S = S_p if g < 4 else S_s
            oagT = oagTs[g % 2]
            if i == 0 and (g == 0 or g == 4):
                CP(Sb[sbi[0]].t[:, :, :], S.t[:, :, :], [S.k], [Sb[sbi[0]].k])
            am = attm[i % 2]
            c0 = i * TSZ
            for h in range(4):
                MM(ATT.t[:TSZ, h * 128:h * 128 + TSZ], kgT.t[:, h, c0:c0 + TSZ], qgT.t[:, h, c0:c0 + TSZ], True, True,
                   [kgT.k, qgT.k], [ATT.k])
            for h in range(4):
                TT(am.t[:TSZ, h, :TSZ], ATT.t[:TSZ, h * 128:h * 128 + TSZ], amask.t[:TSZ, :TSZ], ALU.mult,
                   [ATT.k, amask.k], [am.k])
            yield
            yield
            for h in range(4):
                MM(OB.t[:, h * TSZ:(h + 1) * TSZ], v_tm.t[:TSZ, i, h * 128:(h + 1) * 128], am.t[:TSZ, h, :TSZ],
                   h == 0, False, [v_tm.k, am.k], [OB.k])
            for c in range(CPT):
                gc = i * CPT + c
                cur = Sb[sbi[0]]
                for h in range(4):
                    MM(OB.t[:, h * TSZ + c * CS:h * TSZ + (c + 1) * CS], cur.t[:, h, :],
                       qgT.t[:, h, c0 + c * CS:c0 + (c + 1) * CS], False, (c == CPT - 1 and h == 3),
                       [cur.k, qgT.k], [OB.k])
                for h in range(4):
                    MM(KVB.t[:, h * 128:(h + 1) * 128], kd_tm.t[c * CS:(c + 1) * CS, i, h * 128:(h + 1) * 128],
                       v_tm.t[c * CS:(c + 1) * CS, i, h * 128:(h + 1) * 128], True, True,
                       [kd_tm.k, v_tm.k], [KVB.k])
                for h in range(4):
                    STT(S.t[:, h, :], S.t[:, h, :], dec.t[:, h, gc:gc + 1],
                        KVB.t[:, h * 128:(h + 1) * 128], ALU.mult, ALU.add, [S.k, dec.k, KVB.k], [S.k])
                sbi[0] ^= 1
                CP(Sb[sbi[0]].t[:, :, :], S.t[:, :, :], [S.k], [Sb[sbi[0]].k])
                yield
                yield
            W4 = 4 * TSZ
            ACT(sq.t[:, :W4], OB.t[:, :W4], AF.Square, [OB.k], [sq.k])
            yield
            MM(XB.t[:, :W4], ones.t[:, :], sq.t[:, :W4], True, True, [ones.k, sq.k], [XB.k])
            l_ = poolYa(); ACT(l_.t[:, :W4], XB.t[:, :W4], AF.Ln, [XB.k], [l_.k], bias=4.0 * EPS, scale=1.0 / 128)
            ACT(l_.t[:, :W4], l_.t[:, :W4], AF.Exp, [l_.k], [l_.k], scale=-0.5)
            yield
            for h in range(4):
                STT(OB.t[:, h * TSZ:(h + 1) * TSZ], OB.t[:, h * TSZ:(h + 1) * TSZ], pvc(PV_GA + h),
                    l_.t[:, h * TSZ:(h + 1) * TSZ], ALU.mult, ALU.mult, [OB.k, pv.k, l_.k], [OB.k])
            TT(oagT.t[:, :, c0:c0 + TSZ], OB.t[:, :W4].rearrange("p (a b) -> p a b", a=4), sga.t[:, :, c0:c0 + TSZ],
               ALU.mult, [OB.k, sga.k], [oagT.k])

        def downA(g, j, pool=None):
            t0, T = groups[g]
            last = g == (3 if DEFER_A3 else 4)
            wad, wadk = W("ad", 4)
            oagT = oagTs[g % 2]
            b = zbank()
            for c in range(4):
                MM(b.t[:, :T], wad[:, c, j * 128:(j + 1) * 128], oagT.t[:, c, :T], c == 0, c == 3, [wadk, oagT.k], [b.k])
            bz = zbank(); zmm("za%d" % (j // 4), (j % 4) * 128, t0, T, bz)
            sgz = (pool or poolYa)(); ACT(sgz.t[:, :T], bz.t[:, :T], AF.Tanh, [bz.k], [sgz.k], scale=0.5)
            merge(g, j, t0, T, b, sgz)
            if last and j == 3:
                wfree("za0")
            if last and j == 7:
                wfree("za1"); wfree("ad")
            if j == 7:
                mrg_init[g] = True

        def run2(*gs):
            gs = [x for x in gs if x is not None]
            while gs:
                for x in list(gs):
                    try:
                        next(x)
                    except StopIteration:
                        gs.remove(x)

        def Y2(g, p):
            NT = geo(g)[3]
            for i in (2 * p, 2 * p + 1):
                if i < NT:
                    yield from YA(g, i)

        def DA(g, js):
            for j in js:
                downA(g, j)
                yield

        for h in range(4):
            run2(XA(0, h))
        XA_fin(0)
        for g in range(6):
            NT = geo(g)[3] if g < 5 else 0
            for i in range(4):
                run2(XA(g + 1, i) if g + 1 < 5 else None,
                     YA(g, i) if i < NT else None,
                     DA(g - 1, (2 * i, 2 * i + 1)) if (g >= 1 and not (DEFER_A3 and g - 1 == 3)) else None)
            if g + 1 < 5:
                XA_fin(g + 1)
            if g == 3:
                DMA(o_hg_p.rearrange("h k v -> k h v"), S_p.t[:, :, :], [S_p.k], [], "o_hgp")
                DMA(S_p.t[:, :, :], s_hg.rearrange("h k v -> k h v"), [S_p.k], [S_p.k], "c_shg")
            if g == 4:
                DMA(o_hg_s.rearrange("h k v -> k h v"), S_s.t[:, :, :], [S_s.k], [], "o_hgs")
        ar.release(mA)
    else:
        pass

    if "B" in phases:
        mB = ar.mark()
        wr = sb("wr", [128, 8, 128], BF16)
        wi = sb("wi", [128, 8, 128], BF16)
        P.add("pool", lambda e: e.dma_start(out=wr.t[:, :, :], in_=wr_d), writes=[wr.k], slot="c_wr")
        P.add("pool", lambda e: e.dma_start(out=wi.t[:, :, :], in_=wi_d), writes=[wi.k], slot="c_wi")
        def mkpool2(n, name, dtype):
            bufs = [sb("%s%d" % (name, i), [128, 512], dtype) for i in range(n)]
            c = [0]

            def get():
                b_ = bufs[c[0] % n]
                c[0] += 1
                return b_
            return get
        poolX = mkpool2(4, "pX", F32)
        poolG = mkpool2(4, "pG", BF16)
        poolY = mkpool2(8, "pY", F32)
        poolD = mkpool2(2, "pD", BF16)
        xsb = [sb("xsb%d" % i, [128, 3 + 512], F32) for i in range(2)]
        xcb = [sb("xcb%d" % i, [128, 512], BF16) for i in range(4)]
        hbgs = [sb("hbg%d" % i, [128, 8, 512], BF16) for i in range(2)]
        cst_p = sb("cst_p", [128, 8, 3], F32); cst_s = sb("cst_s", [128, 8, 3], F32)
        hl_p = sb("hl_p", [128, 8], F32); hl_s = sb("hl_s", [128, 8], F32)
        P.add("dve", lambda e: e.memset(cst_p.t[:, :, :], 0.0), writes=[cst_p.k])
        P.add("dve", lambda e: e.memset(hl_p.t[:, :], 0.0), writes=[hl_p.k])
        DMA(cst_s.t[:, :, :], s_cv, [], [cst_s.k], "c_scv")
        DMA(hl_s.t[:, :], s_lr, [], [hl_s.k], "c_slr")
        zlist[:] = [PB[2], PB[3]]
        XS = {}

        def stageX(g, pr):
            t0, T = groups[g]
            last = g == 4
            cst = cst_p if g < 4 else cst_s
            cs = (2 * pr, 2 * pr + 1)
            Zx = {}; Zg = {}; X = {}; G = {}; XC = {}
            for c in cs:
                Zx[c] = PB[c % 2]; zmm("xb%d" % (c // 4), (c % 4) * 128, t0, T, Zx[c])
            if last and pr % 2 == 1:
                wfree("xb%d" % (pr // 2))
            for c in cs:
                xs_ = xsb[c % 2]
                CP(xs_.t[:, 0:3], cst.t[:, c, :], [cst.k], [xs_.k])
                CP(xs_.t[:, 3:3 + T], Zx[c].t[:, :T], [Zx[c].k], [xs_.k], eng="act")
                CP(cst.t[:, c, :], xs_.t[:, T:T + 3], [xs_.k], [cst.k])
            yield
            for c in cs:
                Zg[c] = zbank(); zmm("gb%d" % (c // 4), (c % 4) * 128, t0, T, Zg[c])
                tg = poolY(); ACT(tg.t[:, :T], Zg[c].t[:, :T], AF.Tanh, [Zg[c].k], [tg.k], scale=0.5)
                G[c] = poolG()
                STT(G[c].t[:, :T], tg.t[:, :T], 1.0, Zg[c].t[:, :T], ALU.add, ALU.mult, [tg.k, Zg[c].k], [G[c].k])
            if last and pr % 2 == 1:
                wfree("gb%d" % (pr // 2))
            yield
            for c in cs:
                Z = Zx[c]
                TS(Z.t[:, :T], Z.t[:, :T], pvc(PV_WC + 24 + c), pvc(PV_BC + c), ALU.mult, ALU.add, [Z.k, pv.k], [Z.k])
            yield
            for tap, off in ((2, 16), (1, 8)):
                for c in cs:
                    Z = Zx[c]; xs_ = xsb[c % 2]
                    STT(Z.t[:, :T], xs_.t[:, tap:tap + T], pvc(PV_WC + off + c), Z.t[:, :T], ALU.mult, ALU.add,
                        [xs_.k, Z.k, pv.k], [Z.k])
                yield
            for c in cs:
                Z = Zx[c]; xs_ = xsb[c % 2]
                X[c] = poolX()
                STT(X[c].t[:, :T], xs_.t[:, 0:T], pvc(PV_WC + c), Z.t[:, :T], ALU.mult, ALU.add, [xs_.k, Z.k, pv.k], [X[c].k])
            for c in cs:
                XC[c] = xcb[(2 * pr + (c % 2)) % 4]
                CP(XC[c].t[:, :T], X[c].t[:, :T], [X[c].k], [XC[c].k], eng="act")
            XS[(g, pr)] = (X, G, XC)
            yield

        def stageY(g, pr):
            t0, T = groups[g]
            hbg = hbgs[g % 2]
            hl = hl_p if g < 4 else hl_s
            cs = (2 * pr, 2 * pr + 1)
            X, G, XC = XS.pop((g, pr))
            R = {}; A2 = {}; I_ = {}
            for c in cs:
                Rb, Ib = PB[4 + 2 * (c % 2)], PB[5 + 2 * (c % 2)]
                MM(Rb.t[:, :T], wr.t[:, c, :], XC[c].t[:, :T], True, True, [wr.k, XC[c].k], [Rb.k])
                MM(Ib.t[:, :T], wi.t[:, c, :], XC[c].t[:, :T], True, True, [wi.k, XC[c].k], [Ib.k])
            for c in cs:
                Rb, Ib = PB[4 + 2 * (c % 2)], PB[5 + 2 * (c % 2)]
                R[c] = poolY(); ACT(R[c].t[:, :T], Rb.t[:, :T], AF.Tanh, [Rb.k, dv.k], [R[c].k], bias=dvc(DV_HBR + c), scale=0.5)
                I_[c] = poolY(); ACT(I_[c].t[:, :T], Ib.t[:, :T], AF.Tanh, [Ib.k, dv.k], [I_[c].k], bias=dvc(DV_HBI + c), scale=0.5)
            yield
            for c in cs:
                ACT(R[c].t[:, :T], R[c].t[:, :T], AF.Exp, [R[c].k, dv.k], [R[c].k], bias=dvc(DV_HCL + c), scale=dvc(DV_HCL + c))
            for c in cs:
                TS(I_[c].t[:, :T], I_[c].t[:, :T], 1.0, 1.0, ALU.add, ALU.mult, [I_[c].k], [I_[c].k], eng="pool")
            yield
            for c in cs:
                A2[c] = poolY()
                TT(A2[c].t[:, :T], R[c].t[:, :T], R[c].t[:, :T], ALU.mult, [R[c].k], [A2[c].k], eng="pool")
            yield
            for c in cs:
                TT(I_[c].t[:, :T], I_[c].t[:, :T], X[c].t[:, :T], ALU.mult, [I_[c].k, X[c].k], [I_[c].k], eng="pool")
            for c in cs:
                ACT(A2[c].t[:, :T], A2[c].t[:, :T], AF.Sqrt, [A2[c].k], [A2[c].k], bias=0.25, scale=-0.25)
                if g == 0:
                    P.add("dve", lambda e, b_=A2[c]: e.memset(b_.t[:, 0:1], 0.5), writes=[A2[c].k])
            yield
            for c in cs:
                TT(I_[c].t[:, :T], I_[c].t[:, :T], A2[c].t[:, :T], ALU.mult, [I_[c].k, A2[c].k], [I_[c].k])
            yield
            for c in cs:
                P.add("dve", lambda e, hb=A2[c], a=R[c], u=I_[c], hl=hl, c=c, T=T: e.tensor_tensor_scan(
                    out=hb.t[:, :T], data0=a.t[:, :T], data1=u.t[:, :T], initial=hl.t[:, c:c + 1],
                    op0=ALU.mult, op1=ALU.add), reads=[R[c].k, I_[c].k, hl.k], writes=[A2[c].k])
                CP(hl.t[:, c:c + 1], A2[c].t[:, T - 1:T], [A2[c].k], [hl.k])
            yield
            for c in cs:
                TT(hbg.t[:, c, :T], A2[c].t[:, :T], G[c].t[:, :T], ALU.mult, [A2[c].k, G[c].k], [hbg.k], eng="pool")
            yield

        def downB(g, j):
            t0, T = groups[g]
            last = g == 4
            hbg = hbgs[g % 2]
            b = zbank()
            wbd, wbdk = W("bd%d" % (j // 4))
            jc = (j % 4) * 128
            for c in range(8):
                MM(b.t[:, :T], wbd[:, c, jc:jc + 128], hbg.t[:, c, :T], c == 0, c == 7, [wbdk, hbg.k], [b.k])
            bz = zbank(); zmm("zb%d" % (j // 4), jc, t0, T, bz)
            sgz = poolD(); ACT(sgz.t[:, :T], bz.t[:, :T], AF.Tanh, [bz.k], [sgz.k], scale=0.5)
            merge(g, j, t0, T, b, sgz)
            if last and j % 4 == 3:
                wfree("zb%d" % (j // 4)); wfree("bd%d" % (j // 4))
            if j == 7:
                mrg_init[g] = True

        def runB(*gs):
            gs = [x for x in gs if x is not None]
            while gs:
                for x in list(gs):
                    try:
                        next(x)
                    except StopIteration:
                        gs.remove(x)

        def DB(g, pr):
            yield
            yield
            yield
            yield
            downB(g, 2 * pr + 0)
            yield
            yield
            yield
            downB(g, 2 * pr + 1)
            yield

        def DA3(k):
            yield
            yield
            yield
            yield
            downA(3, 2 * k, poolD)
            yield
            yield
            yield
            downA(3, 2 * k + 1, poolD)
            yield

        seq = [(g, pr) for g in range(5) for pr in range(4)]
        runB(stageX(*seq[0]))
        for k in range(len(seq) + 4):
            runB(stageX(*seq[k + 1]) if k + 1 < len(seq) else None,
                 stageY(*seq[k]) if k < len(seq) else None,
                 DB(*seq[k - 4]) if k >= 4 else (DA3(k) if DEFER_A3 else None))
            if k < len(seq):
                g, pr = seq[k]
                if pr == 3 and g == 3:
                    DMA(o_cv_p, cst_p.t[:, :, :], [cst_p.k], [], "o_cvp")
                    DMA(o_lr_p, hl_p.t[:, :], [hl_p.k], [], "o_lrp")
                if pr == 3 and g == 4:
                    DMA(o_cv_s, cst_s.t[:, :, :], [cst_s.k], [], "o_cvs")
                    DMA(o_lr_s, hl_s.t[:, :], [hl_s.k], [], "o_lrs")
        zlist[:] = [PB[0], PB[1], PB[2]]
        ar.release(mB)

    if "M" in phases:
        mC = ar.mark()
        kT_p = sb("kT_p", [128, 4, 256], BF16); v_p = sb("v_p", [128, 2, 512], BF16)
        kT_s = sb("kT_s", [128, 4, 256], BF16); v_s = sb("v_s", [128, 2, 512], BF16)
        mM = ar.mark()
        phase_M()
        ar.release(mM)
    if "C" in phases:
        USE_RCP = False
        poolCx = mkpool(2, "tCx")
        poolCy = mkpool(4, "tCy")
        poolCd = mkpool(2, "tCd")
        qcTs = [sb("qcT%d" % i, [128, 4, 512], BF16) for i in range(2)]
        sgcs = [sb("sgc%d" % i, [128, 4, 512], BF16) for i in range(2)]
        ocgs = [sb("ocg%d" % i, [128, 4, 512], BF16) for i in range(2)]
        eT = [sb("eT%d" % i, [128, 2, 512], BF16) for i in range(2)]
        SC = [PB[4], PB[5]]; OC = PB[6]; DEN = PB[7]

        def XC(g, h):
            t0, T = groups[g]
            last = g == 4
            Z = zbank(); zmm("qc", h * 128, t0, T, Z)
            CP(qcTs[g % 2].t[:, h, :T], Z.t[:, :T], [Z.k], [qcTs[g % 2].k], eng="act")
            if last and h == 3:
                wfree("qc")
            yield
            Z = zbank(); zmm("gc", h * 128, t0, T, Z)
            tg = poolCx(); ACT(tg.t[:, :T], Z.t[:, :T], AF.Tanh, [Z.k], [tg.k], scale=0.5)
            STT(sgcs[g % 2].t[:, h, :T], tg.t[:, :T], 1.0, Z.t[:, :T], ALU.add, ALU.mult, [tg.k, Z.k], [sgcs[g % 2].k])
            if last and h == 3:
                wfree("gc")
            yield

        def YC(g, h):
            t0, T = groups[g]
            kT, vv = (kT_p, v_p) if g < 4 else (kT_s, v_s)
            qcT, sgc, ocg = qcTs[g % 2], sgcs[g % 2], ocgs[g % 2]
            e_ = eT[h % 2]
            for mj in range(2):
                s = SC[mj]
                MM(s.t[:, :T], kT.t[:, h, mj * 128:(mj + 1) * 128], qcT.t[:, h, :T], True, True, [kT.k, qcT.k], [s.k])
                ACT(e_.t[:, mj, :T], s.t[:, :T], AF.Exp, [s.k], [e_.k], scale=float(128 ** -0.5))
            yield
            for mj in range(2):
                MM(OC.t[:, :T], vv.t[:, mj, h * 128:(h + 1) * 128], e_.t[:, mj, :T], mj == 0, mj == 1, [vv.k, e_.k], [OC.k])
            for mj in range(2):
                MM(DEN.t[:, :T], ones.t[:, :], e_.t[:, mj, :T], mj == 0, mj == 1, [ones.k, e_.k], [DEN.k])
            rd = poolCy()
            if USE_RCP:
                P.add("dve", lambda e, rd=rd, T=T: e.reciprocal_approx_fast(out=rd.t[:, :T], in_=DEN.t[:, :T]),
                      reads=[DEN.k], writes=[rd.k])
            else:
                ACT(rd.t[:, :T], DEN.t[:, :T], AF.Ln, [DEN.k], [rd.k])
                ACT(rd.t[:, :T], rd.t[:, :T], AF.Exp, [rd.k], [rd.k], scale=-1.0)
            yield
            t_ = poolCy(); TT(t_.t[:, :T], OC.t[:, :T], rd.t[:, :T], ALU.mult, [OC.k, rd.k], [t_.k])
            TT(ocg.t[:, h, :T], t_.t[:, :T], sgc.t[:, h, :T], ALU.mult, [t_.k, sgc.k], [ocg.k], eng="pool")
            yield

        def downC(g, j):
            t0, T = groups[g]
            last = g == 4
            ocg = ocgs[g % 2]
            wcd, wcdk = W("cd", 4)
            b = zbank()
            for c in range(4):
                MM(b.t[:, :T], wcd[:, c, j * 128:(j + 1) * 128], ocg.t[:, c, :T], c == 0, c == 3, [wcdk, ocg.k], [b.k])
            jc = (j % 4) * 128
            bz = zbank(); zmm("zc%d" % (j // 4), jc, t0, T, bz)
            sgz = poolCd(); ACT(sgz.t[:, :T], bz.t[:, :T], AF.Tanh, [bz.k], [sgz.k], scale=0.5)
            merge(g, j, t0, T, b, sgz)
            if last and j % 4 == 3:
                wfree("zc%d" % (j // 4))
            if last and j == 7:
                wfree("cd")
            if j == 7:
                mrg_init[g] = True

        def DC(g, js):
            downC(g, js[0])
            yield
            yield
            downC(g, js[1])
            yield

        def runC(*gs):
            gs = [x for x in gs if x is not None]
            while gs:
                for x in list(gs):
                    try:
                        next(x)
                    except StopIteration:
                        gs.remove(x)

        for h in range(4):
            runC(XC(0, h))
        for g in range(6):
            for h in range(4):
                runC(XC(g + 1, h) if g + 1 < 5 else None,
                     YC(g, h) if g < 5 else None,
                     DC(g - 1, (2 * h, 2 * h + 1)) if g >= 1 else None)
            if g < 5:
                DBG("ocg%d" % g, ocgs[g % 2].t[:, :, :], [128, 4, 512], [ocgs[g % 2].k])
    if "M" in phases:
        ar.release(mC)

    if "D" in phases:
        mD = ar.mark()
        xt = [sb("xtD%d" % i, [128, D], F32) for i in range(2)]
        yb = [sb("ybD%d" % i, [128, D], F32) for i in range(2)]
        yo = [sb("yoD%d" % i, [128, D], F32) for i in range(2)]
        junk = sb("junkD", [128, D], BF16)
        gfin = sb("gfin", [128, D], F32)
        DMA(gfin.t[:, :], gf_d, [], [gfin.k], "c_gf")
        st8 = [sb("st8D%d" % i, [128, 8], F32) for i in range(2)]
        wo = [W("o0"), W("o1")]
        def D_a(i):
            nrows = 128 if i < 16 else DSEQ
            t0 = i * 128
            src = xp[t0:t0 + 128, :] if i < 16 else xsm[:, :]
            g = min(i // 4, 4)
            x, y_ = xt[i % 2], yb[i % 2]
            DMA(x.t[:nrows, :], src, [], [x.k], ("xtD", i % 2))
            for n in range(2):
                b = zbank()
                wt_, wtk = wo[n]
                for kc in range(KC):
                    MM(b.t[:nrows, :], mrg[:, kc, t0:t0 + nrows], wt_[:, kc, :], kc == 0, kc == KC - 1,
                       [mrgk[g], wtk], [b.k])
                STT(y_.t[:nrows, n * 512:(n + 1) * 512], b.t[:nrows, :], 0.25, x.t[:nrows, n * 512:(n + 1) * 512],
                    ALU.mult, ALU.add, [b.k, x.k], [y_.k])

        def D_b(i):
            nrows = 128 if i < 16 else DSEQ
            t0 = i * 128
            dst = y_p[t0:t0 + 128, :] if i < 16 else y_s[:, :]
            y_, yo_, s8 = yb[i % 2], yo[i % 2], st8[i % 2]
            ACT(junk.t[:nrows, :], y_.t[:nrows, :], AF.Square, [y_.k], [junk.k, s8.k], accum=s8.t[:nrows, 0:1])
            ACT(s8.t[:nrows, 1:2], s8.t[:nrows, 0:1], AF.Sqrt, [s8.k], [s8.k], bias=EPS, scale=1.0 / D)
            P.add("dve", lambda e, s8=s8, nrows=nrows: e.reciprocal(out=s8.t[:nrows, 2:3], in_=s8.t[:nrows, 1:2]),
                  reads=[s8.k], writes=[s8.k])
            STT(yo_.t[:nrows, :], y_.t[:nrows, :], s8.t[:nrows, 2:3], gfin.t[:nrows, :], ALU.mult, ALU.mult,
                [y_.k, s8.k, gfin.k], [yo_.k], eng="pool" if i % 2 else "dve")
            DMA(dst, yo_.t[:nrows, :], [yo_.k], [], ("yoD", i % 2), q="act")

        for i in range(18):
            if i < 17:
                D_a(i)
            if i > 0:
                D_b(i - 1)
        ar.release(mD)

    P.run()
    es.close()
    return nc, ar


def _bd(w):
    o = np.zeros((128, 8, 128), np.float32)
    for n in range(16):
        c, q = divmod(n, 2)
        o[q * 64:(q + 1) * 64, c, q * 64:(q + 1) * 64] = w[n]
    return o


def _pm(v):
    return np.ascontiguousarray(np.asarray(v, np.float32).reshape(-1, 128).T)


def make_in_maps(inp):
    f = lambda a: np.ascontiguousarray(np.asarray(a, dtype=np.float32))
    pvec = np.zeros((128, NPV), np.float32)
    pvec[:, PV_GMIX:PV_GMIX + 8] = _pm(inp["g_mix"][0])
    pvec[:, PV_GMEM:PV_GMEM + 8] = _pm(inp["g_mem"][0])
    pvec[:, PV_L0:PV_L0 + 4] = _pm(inp["lb_logits"][0])
    pvec[:, PV_L1:PV_L1 + 4] = _pm(inp["lb_logits"][1])
    pvec[:, PV_GA:PV_GA + 4] = _pm(inp["g_a_out"][0])
    for j in range(4):
        pvec[:, PV_WC + 8 * j:PV_WC + 8 * j + 8] = _pm(inp["w_conv"][0][j])
    pvec[:, PV_BC:PV_BC + 8] = _pm(inp["b_conv"][0])
    pvec[:, PV_BR:PV_BR + 8] = _pm(inp["b_lru_r"][0])
    pvec[:, PV_BI:PV_BI + 8] = _pm(inp["b_lru_i"][0])
    pvec[:, PV_LAM:PV_LAM + 8] = _pm(inp["lru_lambda"][0])
    gfin = np.ascontiguousarray(np.broadcast_to(f(inp["g_final"])[None, :], (128, D)))
    ident = np.eye(128, dtype=np.float32)
    s = np.arange(128)[:, None]
    t = np.arange(128)[None, :]
    amask = ((s // HCH == t // HCH) & (t >= s)).astype(np.float32)
    rmask = np.ones((128, 512), np.float32)
    rmask[:, ::HCH] = 0.0
    shared = {
        "w_in": f(inp["w_in"][0]), "w_ad": f(inp["w_a_down"][0]), "w_bd": f(inp["w_b_down"][0]),
        "w_cd": f(inp["w_c_down"][0]), "w_o": f(inp["w_out"][0]), "w_mk": f(inp["w_mem_k"][0]),
        "w_mv": f(inp["w_mem_v"][0]), "wr_bd": _bd(f(inp["w_lru_r"][0])), "wi_bd": _bd(f(inp["w_lru_i"][0])),
        "pv": pvec, "gfin": gfin, "ident": ident, "amask": amask, "rmask": rmask,
    }
    maps = []
    for b in range(8):
        m = dict(shared)
        m["xp"] = f(inp["x_prompt"][b])
        m["xs"] = f(inp["x_sample"][b])
        m["mem"] = f(inp["mem_prompt"][b])
        m["cmk"] = f(inp["cache_mem_k"][0, b]).reshape(NMEM, 512)
        m["cmv"] = f(inp["cache_mem_v"][0, b]).reshape(NMEM, 512)
        m["s_hg"] = f(inp["state_hgrn"][0, b])
        m["s_cv"] = np.ascontiguousarray(f(inp["state_conv"][0, b]).reshape(3, 8, 128).transpose(2, 1, 0))
        m["s_lr"] = _pm(inp["state_lru"][0, b])
        maps.append(m)
    return maps


_CACHE = {}


def kernel(**inputs):
    if "nc" not in _CACHE:
        _CACHE["nc"] = build()[0]
    nc = _CACHE["nc"]
    maps = make_in_maps(inputs)
    res = run_bass_kernel_spmd(nc, maps, core_ids=list(range(8)))
    R = res.results
    st = lambda k: np.stack([np.asarray(r[k], np.float32) for r in R])
    y_p = st("y_p")
    y_s = st("y_s")
    hg_p = st("o_hg_p")[None]
    cv_p = st("o_cv_p").transpose(0, 3, 2, 1).reshape(8, 3, 1024)[None]
    lr_p = st("o_lr_p").transpose(0, 2, 1).reshape(8, 1024)[None]
    mk = st("o_mk").reshape(8, NMEM, 4, 128)[None]
    mv = st("o_mv").reshape(8, NMEM, 4, 128)[None]
    hg_s = st("o_hg_s")[None]
    cv_s = st("o_cv_s").transpose(0, 3, 2, 1).reshape(8, 3, 1024)[None]
    lr_s = st("o_lr_s").transpose(0, 2, 1).reshape(8, 1024)[None]
    return (y_p, y_s, hg_p, np.ascontiguousarray(cv_p), np.ascontiguousarray(lr_p), mk, mv,
            hg_s, np.ascontiguousarray(cv_s), np.ascontiguousarray(lr_s))
```

```python
import numpy as np
from contextlib import ExitStack
import concourse.bass as bass
import concourse.mybir as mybir
from concourse.bass_utils import run_bass_kernel_spmd

F32 = mybir.dt.float32
BF16 = mybir.dt.bfloat16
AF = mybir.ActivationFunctionType
ALU = mybir.AluOpType

D = 1024
SEQ = 2048
DSEQ = 32
NMEM = 256
EPS = 1e-6
KC = 8
HCH = 128
TOK = SEQ + DSEQ

O_QA, O_FA, O_VA, O_GA, O_XB, O_GB, O_QC, O_GC, O_ZA, O_ZB, O_ZC = (
    0, 512, 1024, 1536, 2048, 3072, 4096, 4608, 5120, 6144, 7168)

PV_GMIX, PV_GMEM, PV_L0, PV_L1, PV_GA, PV_WC, PV_BC, PV_BR, PV_BI, PV_LAM = (
    0, 8, 16, 20, 24, 28, 60, 68, 76, 84)
NPV = 92


class Tok:
    __slots__ = ("name", "w", "rs", "rdma", "pre", "excl")

    def __init__(self, name):
        self.name = name
        self.w = None
        self.rs = {}
        self.rdma = []
        self.pre = []
        self.excl = False


class Op:
    __slots__ = ("idx", "eng", "fn", "deps", "is_dma", "slot", "cum", "need", "ticket", "nofuse", "vc", "waits")


ENGS = ("pe", "act", "dve", "pool", "sp")


WSTAT = {}


class Prog:
    def __init__(self, nc, es):
        self.nc = nc
        self.es = es
        self.ops = []
        self.eng_ops = {e: [] for e in ENGS}
        self.slot_cum = {}
        self.slot_hist = {}
        self.slot_sem = {}
        self.eng_sem = {}

    def add(self, eng, fn, reads=(), writes=(), slot=None, nofuse=False):
        op = Op()
        op.nofuse = nofuse
        op.idx = len(self.ops)
        op.eng = eng
        op.fn = fn
        op.is_dma = slot is not None
        op.slot = slot
        op.need = False
        op.ticket = None
        op.cum = None
        deps = {}

        def dep(o, kind):
            if o is None:
                return
            if kind == "raw" or o not in deps:
                deps[o] = kind

        for t in reads:
            dep(t.w, "raw")
            if t.excl:
                for e2, r in t.rs.items():
                    if e2 != eng:
                        dep(r, "raw")
        for t in writes:
            dep(t.w, "raw")
            for r in t.rs.values():
                dep(r, "war")
            for r in t.rdma:
                dep(r, "raw")
            for r in t.pre:
                dep(r, "raw")
        op.deps = deps
        for t in reads:
            if op.is_dma:
                t.rdma.append(op)
            else:
                t.rs[eng] = op
        for t in writes:
            t.w = op
            t.rs = {}
            t.rdma = []
        if op.is_dma:
            c = self.slot_cum.get(slot, 0) + 16
            self.slot_cum[slot] = c
            op.cum = c
            self.slot_hist.setdefault(slot, []).append((op.idx, c))
        self.ops.append(op)
        self.eng_ops[eng].append(op)
        return op

    def _resolve(self):
        for op in self.ops:
            for d, kind in op.deps.items():
                if d.is_dma:
                    continue
                if d.eng == op.eng and not op.is_dma:
                    if op.eng == "pe" or kind == "war":
                        continue
                d.need = True
        for e in ENGS:
            n = 0
            for op in self.eng_ops[e]:
                if op.need and not op.is_dma:
                    n += 1
                    op.ticket = n

    def _plan_waits(self):
        vcE = {e: {} for e in ENGS}
        for op in self.ops:
            K = vcE[op.eng]
            need = {}
            for d, kind in op.deps.items():
                if d.is_dma:
                    key = ("slot", d.slot)
                    val = self._slot_wait_value(d.slot, op.idx)
                else:
                    if d.eng == op.eng and not op.is_dma and (op.eng == "pe" or kind == "war"):
                        continue
                    key = ("eng", d.eng)
                    val = d.ticket
                if key not in need or val > need[key][0]:
                    need[key] = (val, d)
            out = []
            for key, (val, d) in sorted(need.items(), key=lambda kv: -kv[1][1].idx):
                if K.get(key, 0) >= val:
                    continue
                out.append((key, val))
                K[key] = val
                if d.vc is not None:
                    for k2, v2 in d.vc.items():
                        if v2 > K.get(k2, 0):
                            K[k2] = v2
            op.waits = out
            if op.is_dma:
                op.vc = dict(K)
                op.vc[("slot", op.slot)] = op.cum
            elif op.ticket is not None:
                op.vc = dict(K)
                op.vc[("eng", op.eng)] = op.ticket
            else:
                op.vc = None

    def _slot_wait_value(self, slot, before_idx):
        v = 0
        for idx, c in self.slot_hist[slot]:
            if idx < before_idx:
                v = c
            else:
                break
        return v

    def alloc_sems(self):
        nc = self.nc
        for e in ("pe", "act", "dve", "pool"):
            self.eng_sem[e] = self.es.enter_context(nc.semaphore("s_" + e))
        for i, s in enumerate(self.slot_hist):
            self.slot_sem[s] = self.es.enter_context(nc.semaphore("d%d" % i))

    def emit(self, e, eng):
        for op in self.eng_ops[e]:
            pend = []
            for key, val in op.waits:
                sem = self.slot_sem[key[1]] if key[0] == "slot" else self.eng_sem[key[1]]
                pend.append((sem, val))
                WSTAT[e] = WSTAT.get(e, 0) + 1
            fuse = None
            if pend and e in ("act", "dve", "pool", "pe") and not op.is_dma and not op.nofuse:
                fuse = pend.pop()
            for sem, val in pend:
                eng.wait_ge(sem, val)
            ins = op.fn(eng)
            if fuse is not None:
                ins._wait_ge(fuse[0], fuse[1])
            if op.is_dma:
                ins.then_inc(self.slot_sem[op.slot], 16)
            elif op.need:
                ins.then_inc(self.eng_sem[e], 1)
        if e == "sp":
            for s, c in self.slot_cum.items():
                eng.wait_ge(self.slot_sem[s], c)

    def run(self):
        self._resolve()
        self._plan_waits()
        self.alloc_sems()
        nc = self.nc
        with nc.Block() as block:
            @block.tensor
            def _(eng):
                self.emit("pe", eng)

            @block.scalar
            def _(eng):
                self.emit("act", eng)

            @block.vector
            def _(eng):
                self.emit("dve", eng)

            @block.gpsimd
            def _(eng):
                self.emit("pool", eng)

            @block.sync
            def _(eng):
                self.emit("sp", eng)


class Arena:
    def __init__(self, nc, base=16512, top=229344):
        self.nc = nc
        self.off = base
        self.top = top
        self.n = 0
        self.peak = base

    def alloc(self, name, shape, dtype):
        isz = 2 if dtype == BF16 else 4
        nb = isz
        for s in shape[1:]:
            nb *= s
        nb = (nb + 31) // 32 * 32
        off = self.off
        assert off + nb <= self.top, ("SBUF overflow", name, off + nb - self.top)
        self.off += nb
        self.peak = max(self.peak, self.off)
        self.n += 1
        return self.nc.alloc_sbuf_tensor_at("%s_%d" % (name, self.n), list(shape), dtype, offset=off)

    def mark(self):
        return self.off

    def release(self, m):
        self.off = m


class Buf:
    def __init__(self, t, name):
        self.t = t
        self.k = Tok(name)


def build(phases="0ABMCD", dbg=()):
    nc = bass.Bass("TRN2", target_bir_lowering=False)
    es = ExitStack()
    P = Prog(nc, es)
    ar = Arena(nc)

    def din(name, shape):
        return nc.dram_tensor(name, list(shape), F32, kind="ExternalInput").ap()

    def dout(name, shape):
        return nc.dram_tensor(name, list(shape), F32, kind="ExternalOutput").ap()

    xp = din("xp", [SEQ, D]); xsm = din("xs", [DSEQ, D]); memd = din("mem", [NMEM, D])
    cmk = din("cmk", [NMEM, 512]); cmv = din("cmv", [NMEM, 512])
    s_hg = din("s_hg", [4, 128, 128]); s_cv = din("s_cv", [128, 8, 3]); s_lr = din("s_lr", [128, 8])
    w_in = din("w_in", [D, 8192]); w_ad = din("w_ad", [512, D]); w_bd = din("w_bd", [D, D])
    w_cd = din("w_cd", [512, D]); w_o = din("w_o", [D, D]); w_mk = din("w_mk", [D, 512]); w_mv = din("w_mv", [D, 512])
    wr_d = din("wr_bd", [128, 8, 128]); wi_d = din("wi_bd", [128, 8, 128])
    pv_d = din("pv", [128, NPV]); gf_d = din("gfin", [128, D])
    id_d = din("ident", [128, 128]); mk_d = din("amask", [128, 128]); rm_d = din("rmask", [128, 512])

    y_p = dout("y_p", [SEQ, D]); y_s = dout("y_s", [DSEQ, D])
    o_hg_p = dout("o_hg_p", [4, 128, 128]); o_cv_p = dout("o_cv_p", [128, 8, 3]); o_lr_p = dout("o_lr_p", [128, 8])
    o_mk = dout("o_mk", [NMEM, 512]); o_mv = dout("o_mv", [NMEM, 512])
    o_hg_s = dout("o_hg_s", [4, 128, 128]); o_cv_s = dout("o_cv_s", [128, 8, 3]); o_lr_s = dout("o_lr_s", [128, 8])

    PB = [Buf(nc.alloc_psum_tensor("pb%d" % i, [128, 512], F32), "pb%d" % i) for i in range(8)]
    for b_ in PB:
        b_.k.excl = True
    zrot = [0]
    zlist = [PB[0], PB[1], PB[2]]

    def zbank():
        b = zlist[zrot[0] % len(zlist)]
        zrot[0] += 1
        return b
    TB = PB[3]

    all_bufs = []

    def sb(name, shape, dtype):
        o0 = ar.off
        b = Buf(ar.alloc(name, shape, dtype), name)
        o1 = ar.off
        for (p0, p1, ob) in all_bufs:
            if p0 < o1 and o0 < p1:
                k = ob.k
                for r in [k.w] + list(k.rs.values()) + k.rdma + k.pre:
                    if r is not None and r not in b.k.pre:
                        b.k.pre.append(r)
        all_bufs.append((o0, o1, b))
        return b

    hT = ar.alloc("hT", [128, KC, TOK], BF16)
    hTk = [Tok("hT%d" % i) for i in range(17)]
    mrg = ar.alloc("mrg", [128, KC, TOK], BF16)
    mrgk = [Tok("mrg%d" % i) for i in range(5)]
    ident = sb("ident", [128, 128], BF16)
    ones = sb("ones", [128, 128], BF16)
    amask = sb("amask", [128, 128], F32)
    rmask = sb("rmask", [128, 512], F32)
    pv = sb("pv", [128, NPV], F32)
    dv = sb("dv", [128, 80], F32)
    DEFER_A3 = ("A" in phases) and ("B" in phases)
    oagT_keep = sb("oagT1", [128, 4, 512], BF16) if "A" in phases else None
    DV_LB, DV_OML, DV_CL, DV_T, DV_HBR, DV_HBI, DV_HCL, DV_HOML, DV_LBH, DV_NHOML = 0, 4, 8, 16, 32, 40, 48, 56, 60, 64

    NR = 9
    ring = [sb("ring%d" % i, [128, 4096], BF16) for i in range(NR)]

    def pvc(col):
        return pv.t[:, col:col + 1]

    def dvc(col):
        return dv.t[:, col:col + 1]

    def wv(ap, kc):
        return ap.rearrange("(c p) n -> p c n", p=128)

    units = {
        "mk": wv(w_mk, 8), "mv": wv(w_mv, 8),
        "fa": wv(w_in[:, O_FA:O_FA + 512], 8), "qa": wv(w_in[:, O_QA:O_QA + 512], 8),
        "ga": wv(w_in[:, O_GA:O_GA + 512], 8), "va": wv(w_in[:, O_VA:O_VA + 512], 8),
        "za0": wv(w_in[:, O_ZA:O_ZA + 512], 8), "za1": wv(w_in[:, O_ZA + 512:O_ZA + 1024], 8),
        "ad": wv(w_ad, 4),
        "xb0": wv(w_in[:, O_XB:O_XB + 512], 8), "xb1": wv(w_in[:, O_XB + 512:O_XB + 1024], 8),
        "gb0": wv(w_in[:, O_GB:O_GB + 512], 8), "gb1": wv(w_in[:, O_GB + 512:O_GB + 1024], 8),
        "zb0": wv(w_in[:, O_ZB:O_ZB + 512], 8), "zb1": wv(w_in[:, O_ZB + 512:O_ZB + 1024], 8),
        "bd0": wv(w_bd[:, 0:512], 8), "bd1": wv(w_bd[:, 512:1024], 8),
        "qc": wv(w_in[:, O_QC:O_QC + 512], 8), "gc": wv(w_in[:, O_GC:O_GC + 512], 8),
        "zc0": wv(w_in[:, O_ZC:O_ZC + 512], 8), "zc1": wv(w_in[:, O_ZC + 512:O_ZC + 1024], 8),
        "cd": wv(w_cd, 4),
        "o0": wv(w_o[:, 0:512], 8), "o1": wv(w_o[:, 512:1024], 8),
    }
    plan = []
    if "A" in phases:
        plan += ["fa", "qa", "ga", "va", "za0", "za1", "ad"]
    if "B" in phases:
        plan += ["xb0", "xb1", "gb0", "gb1", "zb0", "zb1", "bd0", "bd1"]
    if "M" in phases:
        plan += ["mk", "mv"]
    if "C" in phases:
        plan += ["qc", "gc", "zc0", "zc1", "cd"]
    if "D" in phases:
        plan += ["o0", "o1"]
    free_slots = list(range(NR))
    where = {}

    def pump(limit=None, gate=()):
        n_ = 0
        while plan and free_slots and (limit is None or n_ < limit):
            n_ += 1
            u = plan.pop(0)
            s = free_slots.pop(0)
            where[u] = s
            src = units[u]
            a = src.shape[1]
            dst = ring[s].t[:, :].rearrange("p (a b) -> p a b", a=a)
            P.add("pool", lambda e, dst=dst, src=src: e.dma_start(out=dst, in_=src),
                  reads=list(gate), writes=[ring[s].k], slot=("ring", s))

    def W(u, a=8):
        s = where[u]
        return ring[s].t[:, :].rearrange("p (a b) -> p a b", a=a), ring[s].k

    def wfree(u):
        free_slots.append(where.pop(u))
        pump()

    def ACT(out, in_, func, reads, writes, bias=0.0, scale=1.0, accum=None):
        if accum is None:
            P.add("act", lambda e: e.activation(out=out, in_=in_, func=func, bias=bias, scale=scale),
                  reads=reads, writes=writes)
        else:
            P.add("act", lambda e: e.activation(out=out, in_=in_, func=func, bias=bias, scale=scale,
                                               accum_out=accum), reads=reads, writes=writes, nofuse=True)

    def TS(out, in0, s1, s2, op0, op1, reads, writes, eng="dve"):
        if s2 is None:
            P.add(eng, lambda e: e.tensor_scalar(out=out, in0=in0, scalar1=s1, scalar2=None, op0=op0),
                  reads=reads, writes=writes)
        else:
            P.add(eng, lambda e: e.tensor_scalar(out=out, in0=in0, scalar1=s1, scalar2=s2, op0=op0, op1=op1),
                  reads=reads, writes=writes)

    def TT(out, in0, in1, op, reads, writes, eng="dve"):
        P.add(eng, lambda e: e.tensor_tensor(out=out, in0=in0, in1=in1, op=op), reads=reads, writes=writes)

    def STT(out, in0, sc, in1, op0, op1, reads, writes, eng="dve"):
        if eng == "pool":
            P.add("pool", lambda e: e.tensor_scalar(out=out, in0=in0, scalar1=sc, scalar2=1.0, op0=op0, op1=ALU.mult),
                  reads=reads, writes=writes)
            P.add("pool", lambda e: e.tensor_tensor(out=out, in0=out, in1=in1, op=op1), reads=reads + writes, writes=writes)
            return
        P.add("dve", lambda e: e.scalar_tensor_tensor(out=out, in0=in0, scalar=sc, in1=in1, op0=op0, op1=op1),
              reads=reads, writes=writes)

    def CP(out, in_, reads, writes, eng="dve"):
        if eng == "act":
            P.add(eng, lambda e: e.activation(out=out, in_=in_, func=AF.Copy), reads=reads, writes=writes)
        else:
            P.add(eng, lambda e: e.tensor_copy(out=out, in_=in_), reads=reads, writes=writes)

    def MM(out, lhsT, rhs, start, stop, reads, writes):
        P.add("pe", lambda e: e.matmul(out, lhsT, rhs, start=start, stop=stop, skip_group_check=True),
              reads=reads, writes=writes)

    def TR(out, in_, idn, reads, writes):
        P.add("pe", lambda e: e.transpose(out, in_, idn), reads=reads, writes=writes)

    def DMA(out, in_, reads, writes, slot, q="sp"):
        P.add(q, lambda e: e.dma_start(out=out, in_=in_), reads=reads, writes=writes, slot=slot)

    def hTtoks(t0, T):
        return [hTk[i] for i in range(t0 // 128, (t0 + T + 127) // 128)]

    def zmm(u, col0, t0, T, bank, ncol=128):
        wt, wk = W(u)
        rd = [wk] + hTtoks(t0, T)
        for kc in range(KC):
            MM(bank.t[:ncol, :T], wt[:, kc, col0:col0 + ncol], hT[:, kc, t0:t0 + T],
               kc == 0, kc == KC - 1, rd, [bank.k])

    dbg_list = []

    def DBG(name, ap, shape, toks, dtype=BF16):
        if name in dbg:
            d = nc.dram_tensor("dbg_" + name, list(shape), dtype, kind="ExternalOutput").ap()
            DMA(d, ap, toks, [], ("dbg", name))

    DMA(pv.t[:, :], pv_d, [], [pv.k], "c_pv")
    DMA(amask.t[:, :], mk_d, [], [amask.k], "c_am")
    DMA(rmask.t[:, :], rm_d, [], [rmask.k], "c_rm")
    P.add("pool", lambda e: e.dma_start(out=ident.t[:, :], in_=id_d), writes=[ident.k], slot="c_id")
    P.add("dve", lambda e: e.memset(ones.t[:, :], 1.0), writes=[ones.k])
    pump(limit=3)
    TT(dv.t[:, DV_T:DV_T + 4], pv.t[:, PV_L0:PV_L0 + 4], pv.t[:, PV_L1:PV_L1 + 4], ALU.subtract, [pv.k], [dv.k])
    ACT(dv.t[:, DV_LB:DV_LB + 4], dv.t[:, DV_T:DV_T + 4], AF.Sigmoid, [dv.k], [dv.k])
    TS(dv.t[:, DV_OML:DV_OML + 4], dv.t[:, DV_LB:DV_LB + 4], -1.0, 1.0, ALU.mult, ALU.add, [dv.k], [dv.k])
    TS(dv.t[:, DV_HOML:DV_HOML + 4], dv.t[:, DV_LB:DV_LB + 4], -0.5, 0.5, ALU.mult, ALU.add, [dv.k], [dv.k])
    TS(dv.t[:, DV_LBH:DV_LBH + 4], dv.t[:, DV_LB:DV_LB + 4], 0.5, 0.5, ALU.mult, ALU.add, [dv.k], [dv.k])
    TS(dv.t[:, DV_NHOML:DV_NHOML + 4], dv.t[:, DV_LB:DV_LB + 4], 0.5, -0.5, ALU.mult, ALU.add, [dv.k], [dv.k])
    ACT(dv.t[:, DV_T:DV_T + 8], pv.t[:, PV_LAM:PV_LAM + 8], AF.Exp, [pv.k, dv.k], [dv.k], scale=-1.0)
    ACT(dv.t[:, DV_T + 8:DV_T + 16], dv.t[:, DV_T:DV_T + 8], AF.Ln, [dv.k], [dv.k], bias=1.0)
    TS(dv.t[:, DV_CL:DV_CL + 8], dv.t[:, DV_T + 8:DV_T + 16], -8.0, None, ALU.mult, None, [dv.k], [dv.k])
    TS(dv.t[:, DV_HCL:DV_HCL + 8], dv.t[:, DV_T + 8:DV_T + 16], -4.0, None, ALU.mult, None, [dv.k], [dv.k])
    TS(dv.t[:, DV_HBR:DV_HBR + 8], pv.t[:, PV_BR:PV_BR + 8], 0.5, None, ALU.mult, None, [pv.k, dv.k], [dv.k])
    TS(dv.t[:, DV_HBI:DV_HBI + 8], pv.t[:, PV_BI:PV_BI + 8], 0.5, None, ALU.mult, None, [pv.k, dv.k], [dv.k])

    groups = [(0, 512), (512, 512), (1024, 512), (1536, 512), (2048, 32)]

    m0 = ar.mark()
    NB = {}

    def alloc_norm(tag, n=2):
        NB["xt"] = [sb("xt%s%d" % (tag, i), [128, D], F32) for i in range(n)]
        NB["junk"] = sb("junk" + tag, [128, D], BF16)
        NB["xn"] = [sb("xn%s%d" % (tag, i), [128, D], BF16) for i in range(n)]
        NB["st8"] = [sb("st8%s_%d" % (tag, i), [128, 8], F32) for i in range(n)]
        NB["tag"] = tag

    alloc_norm("0", 4)
    TBv = TB.t[:, :].bitcast(BF16).rearrange("p (a b) -> p a b", a=8)
    cnt = [0]

    def norm_a(src, nrows):
        i = cnt[0] % len(NB["xt"])
        cnt[0] += 1
        x, xb_, s8, junk = NB["xt"][i], NB["xn"][i], NB["st8"][i], NB["junk"]
        DMA(x.t[:nrows, :], src, [], [x.k], ("xt" + NB["tag"], i))
        ACT(junk.t[:nrows, :], x.t[:nrows, :], AF.Square, [x.k], [junk.k, s8.k], accum=s8.t[:nrows, 0:1])
        ACT(s8.t[:nrows, 1:2], s8.t[:nrows, 0:1], AF.Sqrt, [s8.k], [s8.k], bias=EPS, scale=1.0 / D)
        P.add("dve", lambda e: e.reciprocal(out=s8.t[:nrows, 2:3], in_=s8.t[:nrows, 1:2]), reads=[s8.k], writes=[s8.k])
        ACT(xb_.t[:nrows, :], x.t[:nrows, :], AF.Copy, [x.k, s8.k], [xb_.k], scale=s8.t[:nrows, 2:3])
        return xb_

    def norm_b(xb_, nrows, gcol, dstT, dtok):
        for kc in range(KC):
            TR(TBv[:, kc, 0:nrows], xb_.t[:nrows, kc * 128:(kc + 1) * 128], ident.t[:nrows, :nrows],
               [xb_.k, ident.k], [TB.k])
        gb_ = pv.t[:, gcol:gcol + 8].unsqueeze(2).to_broadcast([128, 8, nrows])
        TT(dstT, TBv[:, :, 0:nrows], gb_, ALU.mult, [TB.k, pv.k], [dtok])

    def norm_T(src, nrows, gcol, dstT, dtok):
        norm_b(norm_a(src, nrows), nrows, gcol, dstT, dtok)

    def norm_many(items):
        prev = None
        for it in items:
            xb_ = norm_a(it[0], it[1])
            if prev is not None:
                norm_b(*prev)
            prev = (xb_, it[1], it[2], it[3], it[4])
        norm_b(*prev)

    norm_many([(xp[i * 128:(i + 1) * 128, :], 128, PV_GMIX, hT[:, :, i * 128:(i + 1) * 128], hTk[i]) for i in range(16)]
              + [(xsm[:, :], DSEQ, PV_GMIX, hT[:, :, SEQ:SEQ + DSEQ], hTk[16])])
    DBG("hT", hT[:, :, :], [128, KC, TOK], hTk)
    pump(gate=[hTk[16]])

    ar.release(m0)

    MKV = {}
    def phase_M():
        alloc_norm("M")
        hmT = sb("hmT", [128, KC, NMEM], BF16)
        norm_many([(memd[j * 128:(j + 1) * 128, :], 128, PV_GMEM, hmT.t[:, :, j * 128:(j + 1) * 128], hmT.k)
                   for j in range(2)])
        wk_, wkk = W("mk")
        wv_, wvk = W("mv")
        for h in range(4):
            b = zbank()
            for kc in range(KC):
                MM(b.t[:, :NMEM], wk_[:, kc, h * 128:(h + 1) * 128], hmT.t[:, kc, :], kc == 0, kc == KC - 1,
                   [wkk, hmT.k], [b.k])
            CP(kT_p.t[:, h, :], b.t[:, :NMEM], [b.k], [kT_p.k], eng="act")
        stg = [sb("stg%d" % i, [128, 512], F32) for i in range(2)]
        n = 0
        for (wt_, wtk, od, isv) in ((wk_, wkk, o_mk, False), (wv_, wvk, o_mv, True)):
            for j in range(2):
                b = zbank()
                for kc in range(KC):
                    MM(b.t[:, :], hmT.t[:, kc, j * 128:(j + 1) * 128], wt_[:, kc, :], kc == 0, kc == KC - 1,
                       [wtk, hmT.k], [b.k])
                s = stg[n % 2]
                n += 1
                CP(s.t[:, :], b.t[:, :], [b.k], [s.k], eng="act")
                if isv:
                    CP(v_p.t[:, j, :], b.t[:, :], [b.k], [v_p.k], eng="dve")
                DMA(od[j * 128:(j + 1) * 128, :], s.t[:, :], [s.k], [], ("stg", n % 2))
        wfree("mk")
        wfree("mv")
        cmkb = sb("cmkb", [128, 2, 512], BF16)
        P.add("pool", lambda e: e.dma_start(out=cmkb.t[:, :, :], in_=cmk.rearrange("(j p) n -> p j n", p=128)),
              writes=[cmkb.k], slot="c_cmk")
        P.add("pool", lambda e: e.dma_start(out=v_s.t[:, :, :], in_=cmv.rearrange("(j p) n -> p j n", p=128)),
              writes=[v_s.k], slot="c_cmv")
        for h in range(4):
            for j in range(2):
                TR(TBv[:, h * 2 + j, :], cmkb.t[:, j, h * 128:(h + 1) * 128], ident.t[:, :], [cmkb.k, ident.k], [TB.k])
        CP(kT_s.t[:, :, :], TB.t[:, :].bitcast(BF16).rearrange("p (a b) -> p a b", a=4), [TB.k], [kT_s.k])
        DBG("kT_p", kT_p.t[:, :, :], [128, 4, 256], [kT_p.k])
        DBG("kT_s", kT_s.t[:, :, :], [128, 4, 256], [kT_s.k])


    if not any(p in phases for p in "ABC"):
        for g in range(5):
            t0, T = groups[g]
            P.add("dve", lambda e, t0=t0, T=T: e.memset(mrg[:, :, t0:t0 + T], 0.0), writes=[mrgk[g]])


    def mkpool(n, name):
        bufs = [sb("%s%d" % (name, i), [128, 512], F32) for i in range(n)]
        c = [0]

        def get():
            b = bufs[c[0] % n]
            c[0] += 1
            return b
        return get

    mrg_init = [False] * 5

    def merge(g, j, t0, T, pbank, tz):
        dst = mrg[:, j, t0:t0 + T]
        if not mrg_init[g]:
            STT(dst, tz.t[:, :T], 1.0, pbank.t[:, :T], ALU.add, ALU.mult, [pbank.k, tz.k], [mrgk[g]])
        else:
            STT(pbank.t[:, :T], tz.t[:, :T], 1.0, pbank.t[:, :T], ALU.add, ALU.mult, [pbank.k, tz.k], [pbank.k])
            TT(dst, dst, pbank.t[:, :T], ALU.add, [pbank.k, mrgk[g]], [mrgk[g]])

    if "A" in phases:
        mA = ar.mark()
        ATT, OB, KVB, XB = PB[4], PB[5], PB[6], PB[7]
        poolXa = mkpool(6, "tAx")
        poolYa = mkpool(2, "tAy")
        SETS = []
        for s_ in range(2):
            SETS.append(dict(
                qgT=sb("qgT%d" % s_, [128, 4, 512], BF16), kgT=sb("kgT%d" % s_, [128, 4, 512], BF16),
                sga=sb("sga%d" % s_, [128, 4, 512], BF16), dec=sb("dec%d" % s_, [128, 4, 8], F32)))
        kd_tm1 = sb("kd_tm", [128, 4, 512], BF16); v_tm1 = sb("v_tm", [128, 4, 512], BF16)
        for s_ in range(2):
            SETS[s_]["kd_tm"] = kd_tm1
            SETS[s_]["v_tm"] = v_tm1
        kdT = sb("kdT", [128, 4, 512], BF16)
        attm = [sb("attm0", [128, 4, 128], BF16)] * 2
        S_p = sb("S_p", [128, 4, 128], F32); S_s = S_p
        Sb = [sb("Sb%d" % i, [128, 4, 128], BF16) for i in range(2)]
        sq = sb("sqA", [128, 512], BF16)
        oagTs = [sb("oagT0", [128, 4, 512], BF16), oagT_keep]
        P.add("dve", lambda e: e.memset(S_p.t[:, :, :], 0.0), writes=[S_p.k])
        sbi = [0]

        def geo(g):
            t0, T = groups[g]
            TSZ = min(128, T); CS = min(HCH, T)
            return t0, T, TSZ, T // TSZ, CS, TSZ // CS, T // CS

        def XA(g, h):
            t0, T, TSZ, NT, CS, CPT, NCH = geo(g)
            st = SETS[g % 2]
            last = g == 4
            tmp = poolXa
            bf_ = zbank(); zmm("fa", h * 128, t0, T, bf_)
            A_ = tmp(); ACT(A_.t[:, :T], bf_.t[:, :T], AF.Tanh, [bf_.k], [A_.k], scale=0.5)
            yield
            bq_ = zbank(); zmm("qa", h * 128, t0, T, bq_)
            Q_ = tmp(); ACT(Q_.t[:, :T], bq_.t[:, :T], AF.Tanh, [bq_.k], [Q_.k], scale=0.5)
            STT(Q_.t[:, :T], Q_.t[:, :T], 1.0, bq_.t[:, :T], ALU.add, ALU.mult, [Q_.k, bq_.k], [Q_.k])
            yield
            bg_ = zbank(); zmm("ga", h * 128, t0, T, bg_)
            G_ = tmp(); ACT(G_.t[:, :T], bg_.t[:, :T], AF.Tanh, [bg_.k], [G_.k], scale=0.5)
            STT(st["sga"].t[:, h, :T], G_.t[:, :T], 1.0, bg_.t[:, :T], ALU.add, ALU.mult, [G_.k, bg_.k], [st["sga"].k])
            if last and h == 3:
                wfree("fa"); wfree("qa"); wfree("ga")
            yield
            C_ = tmp(); ACT(C_.t[:, :T], A_.t[:, :T], AF.Ln, [A_.k, dv.k], [C_.k], bias=dvc(DV_LBH + h), scale=dvc(DV_HOML + h))
            K_ = G_
            TS(K_.t[:, :T], A_.t[:, :T], dvc(DV_NHOML + h), dvc(DV_HOML + h), ALU.mult, ALU.add, [A_.k, G_.k, dv.k], [K_.k])
            yield
            D_ = tmp()
            P.add("dve", lambda e, D_=D_, C_=C_, T=T: e.tensor_tensor_scan(
                out=D_.t[:, :T], data0=rmask.t[:, :T], data1=C_.t[:, :T], initial=0.0, op0=ALU.mult, op1=ALU.add),
                reads=[rmask.k, C_.k], writes=[D_.k])
            yield
            E_ = A_
            ACT(E_.t[:, :T], D_.t[:, :T], AF.Exp, [D_.k, A_.k], [E_.k])
            F_ = tmp(); ACT(F_.t[:, :T], D_.t[:, :T], AF.Exp, [D_.k], [F_.k], scale=-1.0)
            yield
            e3 = E_.t[:, :T].rearrange("p (c s) -> p c s", s=CS)
            CP(st["dec"].t[:, h, 0:NCH].unsqueeze(2), e3[:, :, CS - 1:CS], [E_.k], [st["dec"].k])
            TT(F_.t[:, :T], K_.t[:, :T], F_.t[:, :T], ALU.mult, [K_.k, F_.k], [F_.k], eng="pool")
            TT(st["qgT"].t[:, h, :T], Q_.t[:, :T], E_.t[:, :T], ALU.mult, [Q_.k, E_.k], [st["qgT"].k], eng="pool")
            yield
            CP(st["kgT"].t[:, h, :T], F_.t[:, :T], [F_.k], [st["kgT"].k], eng="act")
            f3 = F_.t[:, :T].rearrange("p (c s) -> p c s", s=CS)
            TT(kdT.t[:, h, :T].rearrange("p (c s) -> p c s", s=CS), f3, e3[:, :, CS - 1:CS].to_broadcast([128, NCH, CS]),
               ALU.mult, [F_.k, E_.k], [kdT.k], eng="pool")
            yield
            if last and h == 3:
                pass

        def XA_fin(g):
            t0, T, TSZ, NT, CS, CPT, NCH = geo(g)
            st = SETS[g % 2]
            last = g == 4
            wva, wvak = W("va")
            for h in range(NT):
                b = zbank()
                c0 = t0 + h * TSZ
                for kc in range(KC):
                    MM(b.t[:TSZ, :], hT[:, kc, c0:c0 + TSZ], wva[:, kc, :], kc == 0, kc == KC - 1,
                       [wvak] + hTtoks(c0, TSZ), [b.k])
                CP(st["v_tm"].t[:TSZ, h, :], b.t[:TSZ, :], [b.k], [st["v_tm"].k], eng="act")
            if last:
                wfree("va")
            for i in range(NT):
                for h in range(4):
                    TR(TBv[:TSZ, h, :], kdT.t[:, h, i * TSZ:(i + 1) * TSZ], ident.t[:, :], [kdT.k, ident.k], [TB.k])
                CP(st["kd_tm"].t[:TSZ, i, :].rearrange("p (a b) -> p a b", a=4), TBv[:TSZ, 0:4, :], [TB.k], [st["kd_tm"].k])

        def YA(g, i):
            t0, T, TSZ, NT, CS, CPT, NCH = geo(g)
            st = SETS[g % 2]
            qgT, kgT, kd_tm, v_tm, sga, dec = st["qgT"], st["kgT"], st["kd_tm"], st["v_tm"], st["sga"], st["dec"]
            S = S_p if g < 4 else S_s
            oagT = oagTs[g % 2]
            if i == 0 and (g == 0 or g == 4):
                CP(Sb[sbi[0]].t[:, :, :], S.t[:, :, :], [S.k], [Sb[sbi[0]].k])
            am = attm[i % 2]
            c0 = i * TSZ
            for h in range(4):
                MM(ATT.t[:TSZ, h * 128:h * 128 + TSZ], kgT.t[:, h, c0:c0 + TSZ], qgT.t[:, h, c0:c0 + TSZ], True, True,
                   [kgT.k, qgT.k], [ATT.k])
            for h in range(4):
                TT(am.t[:TSZ, h, :TSZ], ATT.t[:TSZ, h * 128:h * 128 + TSZ], amask.t[:TSZ, :TSZ], ALU.mult,
                   [ATT.k, amask.k], [am.k])
            yield
            yield
            for h in range(4):
                MM(OB.t[:, h * TSZ:(h + 1) * TSZ], v_tm.t[:TSZ, i, h * 128:(h + 1) * 128], am.t[:TSZ, h, :TSZ],
                   h == 0, False, [v_tm.k, am.k], [OB.k])
            for c in range(CPT):
                gc = i * CPT + c
                cur = Sb[sbi[0]]
                for h in range(4):
                    MM(OB.t[:, h * TSZ + c * CS:h * TSZ + (c + 1) * CS], cur.t[:, h, :],
                       qgT.t[:, h, c0 + c * CS:c0 + (c + 1) * CS], False, (c == CPT - 1 and h == 3),
                       [cur.k, qgT.k], [OB.k])
                for h in range(4):
                    MM(KVB.t[:, h * 128:(h + 1) * 128], kd_tm.t[c * CS:(c + 1) * CS, i, h * 128:(h + 1) * 128],
                       v_tm.t[c * CS:(c + 1) * CS, i, h * 128:(h + 1) * 128], True, True,
                       [kd_tm.k, v_tm.k], [KVB.k])
                for h in range(4):
                    STT(S.t[:, h, :], S.t[:, h, :], dec.t[:, h, gc:gc + 1],
                        KVB.t[:, h * 128:(h + 1) * 128], ALU.mult, ALU.add, [S.k, dec.k, KVB.k], [S.k])
                sbi[0] ^= 1
                CP(Sb[sbi[0]].t[:, :, :], S.t[:, :, :], [S.k], [Sb[sbi[0]].k])
                yield
                yield
            W4 = 4 * TSZ
            ACT(sq.t[:, :W4], OB.t[:, :W4], AF.Square, [OB.k], [sq.k])
            yield
            MM(XB.t[:, :W4], ones.t[:, :], sq.t[:, :W4], True, True, [ones.k, sq.k], [XB.k])
            l_ = poolYa(); ACT(l_.t[:, :W4], XB.t[:, :W4], AF.Ln, [XB.k], [l_.k], bias=4.0 * EPS, scale=1.0 / 128)
            ACT(l_.t[:, :W4], l_.t[:, :W4], AF.Exp, [l_.k], [l_.k], scale=-0.5)
            yield
            for h in range(4):
                STT(OB.t[:, h * TSZ:(h + 1) * TSZ], OB.t[:, h * TSZ:(h + 1) * TSZ], pvc(PV_GA + h),
                    l_.t[:, h * TSZ:(h + 1) * TSZ], ALU.mult, ALU.mult, [OB.k, pv.k, l_.k], [OB.k])
            TT(oagT.t[:, :, c0:c0 + TSZ], OB.t[:, :W4].rearrange("p (a b) -> p a b", a=4), sga.t[:, :, c0:c0 + TSZ],
               ALU.mult, [OB.k, sga.k], [oagT.k])

        def downA(g, j, pool=None):
            t0, T = groups[g]
            last = g == (3 if DEFER_A3 else 4)
            wad, wadk = W("ad", 4)
            oagT = oagTs[g % 2]
            b = zbank()
            for c in range(4):
                MM(b.t[:, :T], wad[:, c, j * 128:(j + 1) * 128], oagT.t[:, c, :T], c == 0, c == 3, [wadk, oagT.k], [b.k])
            bz = zbank(); zmm("za%d" % (j // 4), (j % 4) * 128, t0, T, bz)
            sgz = (pool or poolYa)(); ACT(sgz.t[:, :T], bz.t[:, :T], AF.Tanh, [bz.k], [sgz.k], scale=0.5)
            merge(g, j, t0, T, b, sgz)
            if last and j == 3:
                wfree("za0")
            if last and j == 7:
                wfree("za1"); wfree("ad")
            if j == 7:
                mrg_init[g] = True

        def run2(*gs):
            gs = [x for x in gs if x is not None]
            while gs:
                for x in list(gs):
                    try:
                        next(x)
                    except StopIteration:
                        gs.remove(x)

        def Y2(g, p):
            NT = geo(g)[3]
            for i in (2 * p, 2 * p + 1):
                if i < NT:
                    yield from YA(g, i)

        def DA(g, js):
            for j in js:
                downA(g, j)
                yield

        for h in range(4):
            run2(XA(0, h))
        XA_fin(0)
        for g in range(6):
            NT = geo(g)[3] if g < 5 else 0
            for i in range(4):
                run2(XA(g + 1, i) if g + 1 < 5 else None,
                     YA(g, i) if i < NT else None,
                     DA(g - 1, (2 * i, 2 * i + 1)) if (g >= 1 and not (DEFER_A3 and g - 1 == 3)) else None)
            if g + 1 < 5:
                XA_fin(g + 1)
            if g == 3:
                DMA(o_hg_p.rearrange("h k v -> k h v"), S_p.t[:, :, :], [S_p.k], [], "o_hgp")
                DMA(S_p.t[:, :, :], s_hg.rearrange("h k v -> k h v"), [S_p.k], [S_p.k], "c_shg")
            if g == 4:
                DMA(o_hg_s.rearrange("h k v -> k h v"), S_s.t[:, :, :], [S_s.k], [], "o_hgs")
        ar.release(mA)
    else:
        pass

    if "B" in phases:
        mB = ar.mark()
        wr = sb("wr", [128, 8, 128], BF16)
        wi = sb("wi", [128, 8, 128], BF16)
        P.add("pool", lambda e: e.dma_start(out=wr.t[:, :, :], in_=wr_d), writes=[wr.k], slot="c_wr")
        P.add("pool", lambda e: e.dma_start(out=wi.t[:, :, :], in_=wi_d), writes=[wi.k], slot="c_wi")
        def mkpool2(n, name, dtype):
            bufs = [sb("%s%d" % (name, i), [128, 512], dtype) for i in range(n)]
            c = [0]

            def get():
                b_ = bufs[c[0] % n]
                c[0] += 1
                return b_
            return get
        poolX = mkpool2(4, "pX", F32)
        poolG = mkpool2(4, "pG", BF16)
        poolY = mkpool2(8, "pY", F32)
        poolD = mkpool2(2, "pD", BF16)
        xsb = [sb("xsb%d" % i, [128, 3 + 512], F32) for i in range(2)]
        xcb = [sb("xcb%d" % i, [128, 512], BF16) for i in range(4)]
        hbgs = [sb("hbg%d" % i, [128, 8, 512], BF16) for i in range(2)]
        cst_p = sb("cst_p", [128, 8, 3], F32); cst_s = sb("cst_s", [128, 8, 3], F32)
        hl_p = sb("hl_p", [128, 8], F32); hl_s = sb("hl_s", [128, 8], F32)
        P.add("dve", lambda e: e.memset(cst_p.t[:, :, :], 0.0), writes=[cst_p.k])
        P.add("dve", lambda e: e.memset(hl_p.t[:, :], 0.0), writes=[hl_p.k])
        DMA(cst_s.t[:, :, :], s_cv, [], [cst_s.k], "c_scv")
        DMA(hl_s.t[:, :], s_lr, [], [hl_s.k], "c_slr")
        zlist[:] = [PB[2], PB[3]]
        XS = {}

        def stageX(g, pr):
            t0, T = groups[g]
            last = g == 4
            cst = cst_p if g < 4 else cst_s
            cs = (2 * pr, 2 * pr + 1)
            Zx = {}; Zg = {}; X = {}; G = {}; XC = {}
            for c in cs:
                Zx[c] = PB[c % 2]; zmm("xb%d" % (c // 4), (c % 4) * 128, t0, T, Zx[c])
            if last and pr % 2 == 1:
                wfree("xb%d" % (pr // 2))
            for c in cs:
                xs_ = xsb[c % 2]
                CP(xs_.t[:, 0:3], cst.t[:, c, :], [cst.k], [xs_.k])
                CP(xs_.t[:, 3:3 + T], Zx[c].t[:, :T], [Zx[c].k], [xs_.k], eng="act")
                CP(cst.t[:, c, :], xs_.t[:, T:T + 3], [xs_.k], [cst.k])
            yield
            for c in cs:
                Zg[c] = zbank(); zmm("gb%d" % (c // 4), (c % 4) * 128, t0, T, Zg[c])
                tg = poolY(); ACT(tg.t[:, :T], Zg[c].t[:, :T], AF.Tanh, [Zg[c].k], [tg.k], scale=0.5)
                G[c] = poolG()
                STT(G[c].t[:, :T], tg.t[:, :T], 1.0, Zg[c].t[:, :T], ALU.add, ALU.mult, [tg.k, Zg[c].k], [G[c].k])
            if last and pr % 2 == 1:
                wfree("gb%d" % (pr // 2))
            yield
            for c in cs:
                Z = Zx[c]
                TS(Z.t[:, :T], Z.t[:, :T], pvc(PV_WC + 24 + c), pvc(PV_BC + c), ALU.mult, ALU.add, [Z.k, pv.k], [Z.k])
            yield
            for tap, off in ((2, 16), (1, 8)):
                for c in cs:
                    Z = Zx[c]; xs_ = xsb[c % 2]
                    STT(Z.t[:, :T], xs_.t[:, tap:tap + T], pvc(PV_WC + off + c), Z.t[:, :T], ALU.mult, ALU.add,
                        [xs_.k, Z.k, pv.k], [Z.k])
                yield
            for c in cs:
                Z = Zx[c]; xs_ = xsb[c % 2]
                X[c] = poolX()
                STT(X[c].t[:, :T], xs_.t[:, 0:T], pvc(PV_WC + c), Z.t[:, :T], ALU.mult, ALU.add, [xs_.k, Z.k, pv.k], [X[c].k])
            for c in cs:
                XC[c] = xcb[(2 * pr + (c % 2)) % 4]
                CP(XC[c].t[:, :T], X[c].t[:, :T], [X[c].k], [XC[c].k], eng="act")
            XS[(g, pr)] = (X, G, XC)
            yield

        def stageY(g, pr):
            t0, T = groups[g]
            hbg = hbgs[g % 2]
            hl = hl_p if g < 4 else hl_s
            cs = (2 * pr, 2 * pr + 1)
            X, G, XC = XS.pop((g, pr))
            R = {}; A2 = {}; I_ = {}
            for c in cs:
                Rb, Ib = PB[4 + 2 * (c % 2)], PB[5 + 2 * (c % 2)]
                MM(Rb.t[:, :T], wr.t[:, c, :], XC[c].t[:, :T], True, True, [wr.k, XC[c].k], [Rb.k])
                MM(Ib.t[:, :T], wi.t[:, c, :], XC[c].t[:, :T], True, True, [wi.k, XC[c].k], [Ib.k])
            for c in cs:
                Rb, Ib = PB[4 + 2 * (c % 2)], PB[5 + 2 * (c % 2)]
                R[c] = poolY(); ACT(R[c].t[:, :T], Rb.t[:, :T], AF.Tanh, [Rb.k, dv.k], [R[c].k], bias=dvc(DV_HBR + c), scale=0.5)
                I_[c] = poolY(); ACT(I_[c].t[:, :T], Ib.t[:, :T], AF.Tanh, [Ib.k, dv.k], [I_[c].k], bias=dvc(DV_HBI + c), scale=0.5)
            yield
            for c in cs:
                ACT(R[c].t[:, :T], R[c].t[:, :T], AF.Exp, [R[c].k, dv.k], [R[c].k], bias=dvc(DV_HCL + c), scale=dvc(DV_HCL + c))
            for c in cs:
                TS(I_[c].t[:, :T], I_[c].t[:, :T], 1.0, 1.0, ALU.add, ALU.mult, [I_[c].k], [I_[c].k], eng="pool")
            yield
            for c in cs:
                A2[c] = poolY()
                TT(A2[c].t[:, :T], R[c].t[:, :T], R[c].t[:, :T], ALU.mult, [R[c].k], [A2[c].k], eng="pool")
            yield
            for c in cs:
                TT(I_[c].t[:, :T], I_[c].t[:, :T], X[c].t[:, :T], ALU.mult, [I_[c].k, X[c].k], [I_[c].k], eng="pool")
            for c in cs:
                ACT(A2[c].t[:, :T], A2[c].t[:, :T], AF.Sqrt, [A2[c].k], [A2[c].k], bias=0.25, scale=-0.25)
                if g == 0:
                    P.add("dve", lambda e, b_=A2[c]: e.memset(b_.t[:, 0:1], 0.5), writes=[A2[c].k])
            yield
            for c in cs:
                TT(I_[c].t[:, :T], I_[c].t[:, :T], A2[c].t[:, :T], ALU.mult, [I_[c].k, A2[c].k], [I_[c].k])
            yield
            for c in cs:
                P.add("dve", lambda e, hb=A2[c], a=R[c], u=I_[c], hl=hl, c=c, T=T: e.tensor_tensor_scan(
                    out=hb.t[:, :T], data0=a.t[:, :T], data1=u.t[:, :T], initial=hl.t[:, c:c + 1],
                    op0=ALU.mult, op1=ALU.add), reads=[R[c].k, I_[c].k, hl.k], writes=[A2[c].k])
                CP(hl.t[:, c:c + 1], A2[c].t[:, T - 1:T], [A2[c].k], [hl.k])
            yield
            for c in cs:
                TT(hbg.t[:, c, :T], A2[c].t[:, :T], G[c].t[:, :T], ALU.mult, [A2[c].k, G[c].k], [hbg.k], eng="pool")
            yield

        def downB(g, j):
            t0, T = groups[g]
            last = g == 4
            hbg = hbgs[g % 2]
            b = zbank()
            wbd, wbdk = W("bd%d" % (j // 4))
            jc = (j % 4) * 128
            for c in range(8):
                MM(b.t[:, :T], wbd[:, c, jc:jc + 128], hbg.t[:, c, :T], c == 0, c == 7, [wbdk, hbg.k], [b.k])
            bz = zbank(); zmm("zb%d" % (j // 4), jc, t0, T, bz)
            sgz = poolD(); ACT(sgz.t[:, :T], bz.t[:, :T], AF.Tanh, [bz.k], [sgz.k], scale=0.5)
            merge(g, j, t0, T, b, sgz)
            if last and j % 4 == 3:
                wfree("zb%d" % (j // 4)); wfree("bd%d" % (j // 4))
            if j == 7:
                mrg_init[g] = True

        def runB(*gs):
            gs = [x for x in gs if x is not None]
            while gs:
                for x in list(gs):
                    try:
                        next(x)
                    except StopIteration:
                        gs.remove(x)

        def DB(g, pr):
            yield
            yield
            yield
            yield
            downB(g, 2 * pr + 0)
            yield
            yield
            yield
            downB(g, 2 * pr + 1)
            yield

        def DA3(k):
            yield
            yield
            yield
            yield
            downA(3, 2 * k, poolD)
            yield
            yield
            yield
            downA(3, 2 * k + 1, poolD)
            yield

        seq = [(g, pr) for g in range(5) for pr in range(4)]
        runB(stageX(*seq[0]))
        for k in range(len(seq) + 4):
            runB(stageX(*seq[k + 1]) if k + 1 < len(seq) else None,
                 stageY(*seq[k]) if k < len(seq) else None,
                 DB(*seq[k - 4]) if k >= 4 else (DA3(k) if DEFER_A3 else None))
            if k < len(seq):
                g, pr = seq[k]
                if pr == 3 and g == 3:
                    DMA(o_cv_p, cst_p.t[:, :, :], [cst_p.k], [], "o_cvp")
                    DMA(o_lr_p, hl_p.t[:, :], [hl_p.k], [], "o_lrp")
                if pr == 3 and g == 4:
                    DMA(o_cv_s, cst_s.t[:, :, :], [cst_s.k], [], "o_cvs")
                    DMA(o_lr_s, hl_s.t[:, :], [hl_s.k], [], "o_lrs")
        zlist[:] = [PB[0], PB[1], PB[2]]
        ar.release(mB)

    if "M" in phases:
        mC = ar.mark()
        kT_p = sb("kT_p", [128, 4, 256], BF16); v_p = sb("v_p", [128, 2, 512], BF16)
        kT_s = sb("kT_s", [128, 4, 256], BF16); v_s = sb("v_s", [128, 2, 512], BF16)
        mM = ar.mark()
        phase_M()
        ar.release(mM)
    if "C" in phases:
        USE_RCP = False
        poolCx = mkpool(2, "tCx")
        poolCy = mkpool(4, "tCy")
        poolCd = mkpool(2, "tCd")
        qcTs = [sb("qcT%d" % i, [128, 4, 512], BF16) for i in range(2)]
        sgcs = [sb("sgc%d" % i, [128, 4, 512], BF16) for i in range(2)]
        ocgs = [sb("ocg%d" % i, [128, 4, 512], BF16) for i in range(2)]
        eT = [sb("eT%d" % i, [128, 2, 512], BF16) for i in range(2)]
        SC = [PB[4], PB[5]]; OC = PB[6]; DEN = PB[7]

        def XC(g, h):
            t0, T = groups[g]
            last = g == 4
            Z = zbank(); zmm("qc", h * 128, t0, T, Z)
            CP(qcTs[g % 2].t[:, h, :T], Z.t[:, :T], [Z.k], [qcTs[g % 2].k], eng="act")
            if last and h == 3:
                wfree("qc")
            yield
            Z = zbank(); zmm("gc", h * 128, t0, T, Z)
            tg = poolCx(); ACT(tg.t[:, :T], Z.t[:, :T], AF.Tanh, [Z.k], [tg.k], scale=0.5)
            STT(sgcs[g % 2].t[:, h, :T], tg.t[:, :T], 1.0, Z.t[:, :T], ALU.add, ALU.mult, [tg.k, Z.k], [sgcs[g % 2].k])
            if last and h == 3:
                wfree("gc")
            yield

        def YC(g, h):
            t0, T = groups[g]
            kT, vv = (kT_p, v_p) if g < 4 else (kT_s, v_s)
            qcT, sgc, ocg = qcTs[g % 2], sgcs[g % 2], ocgs[g % 2]
            e_ = eT[h % 2]
            for mj in range(2):
                s = SC[mj]
                MM(s.t[:, :T], kT.t[:, h, mj * 128:(mj + 1) * 128], qcT.t[:, h, :T], True, True, [kT.k, qcT.k], [s.k])
                ACT(e_.t[:, mj, :T], s.t[:, :T], AF.Exp, [s.k], [e_.k], scale=float(128 ** -0.5))
            yield
            for mj in range(2):
                MM(OC.t[:, :T], vv.t[:, mj, h * 128:(h + 1) * 128], e_.t[:, mj, :T], mj == 0, mj == 1, [vv.k, e_.k], [OC.k])
            for mj in range(2):
                MM(DEN.t[:, :T], ones.t[:, :], e_.t[:, mj, :T], mj == 0, mj == 1, [ones.k, e_.k], [DEN.k])
            rd = poolCy()
            if USE_RCP:
                P.add("dve", lambda e, rd=rd, T=T: e.reciprocal_approx_fast(out=rd.t[:, :T], in_=DEN.t[:, :T]),
                      reads=[DEN.k], writes=[rd.k])
            else:
                ACT(rd.t[:, :T], DEN.t[:, :T], AF.Ln, [DEN.k], [rd.k])
                ACT(rd.t[:, :T], rd.t[:, :T], AF.Exp, [rd.k], [rd.k], scale=-1.0)
            yield
            t_ = poolCy(); TT(t_.t[:, :T], OC.t[:, :T], rd.t[:, :T], ALU.mult, [OC.k, rd.k], [t_.k])
            TT(ocg.t[:, h, :T], t_.t[:, :T], sgc.t[:, h, :T], ALU.mult, [t_.k, sgc.k], [ocg.k], eng="pool")
            yield

        def downC(g, j):
            t0, T = groups[g]
            last = g == 4
            ocg = ocgs[g % 2]
            wcd, wcdk = W("cd", 4)
            b = zbank()
            for c in range(4):
                MM(b.t[:, :T], wcd[:, c, j * 128:(j + 1) * 128], ocg.t[:, c, :T], c == 0, c == 3, [wcdk, ocg.k], [b.k])
            jc = (j % 4) * 128
            bz = zbank(); zmm("zc%d" % (j // 4), jc, t0, T, bz)
            sgz = poolCd(); ACT(sgz.t[:, :T], bz.t[:, :T], AF.Tanh, [bz.k], [sgz.k], scale=0.5)
            merge(g, j, t0, T, b, sgz)
            if last and j % 4 == 3:
                wfree("zc%d" % (j // 4))
            if last and j == 7:
                wfree("cd")
            if j == 7:
                mrg_init[g] = True

        def DC(g, js):
            downC(g, js[0])
            yield
            yield
            downC(g, js[1])
            yield

        def runC(*gs):
            gs = [x for x in gs if x is not None]
            while gs:
                for x in list(gs):
                    try:
                        next(x)
                    except StopIteration:
                        gs.remove(x)

        for h in range(4):
            runC(XC(0, h))
        for g in range(6):
            for h in range(4):
                runC(XC(g + 1, h) if g + 1 < 5 else None,
                     YC(g, h) if g < 5 else None,
                     DC(g - 1, (2 * h, 2 * h + 1)) if g >= 1 else None)
            if g < 5:
                DBG("ocg%d" % g, ocgs[g % 2].t[:, :, :], [128, 4, 512], [ocgs[g % 2].k])
    if "M" in phases:
        ar.release(mC)

    if "D" in phases:
        mD = ar.mark()
        xt = [sb("xtD%d" % i, [128, D], F32) for i in range(2)]
        yb = [sb("ybD%d" % i, [128, D], F32) for i in range(2)]
        yo = [sb("yoD%d" % i, [128, D], F32) for i in range(2)]
        junk = sb("junkD", [128, D], BF16)
        gfin = sb("gfin", [128, D], F32)
        DMA(gfin.t[:, :], gf_d, [], [gfin.k], "c_gf")
        st8 = [sb("st8D%d" % i, [128, 8], F32) for i in range(2)]
        wo = [W("o0"), W("o1")]
        def D_a(i):
            nrows = 128 if i < 16 else DSEQ
            t0 = i * 128
            src = xp[t0:t0 + 128, :] if i < 16 else xsm[:, :]
            g = min(i // 4, 4)
            x, y_ = xt[i % 2], yb[i % 2]
            DMA(x.t[:nrows, :], src, [], [x.k], ("xtD", i % 2))
            for n in range(2):
                b = zbank()
                wt_, wtk = wo[n]
                for kc in range(KC):
                    MM(b.t[:nrows, :], mrg[:, kc, t0:t0 + nrows], wt_[:, kc, :], kc == 0, kc == KC - 1,
                       [mrgk[g], wtk], [b.k])
                STT(y_.t[:nrows, n * 512:(n + 1) * 512], b.t[:nrows, :], 0.25, x.t[:nrows, n * 512:(n + 1) * 512],
                    ALU.mult, ALU.add, [b.k, x.k], [y_.k])

        def D_b(i):
            nrows = 128 if i < 16 else DSEQ
            t0 = i * 128
            dst = y_p[t0:t0 + 128, :] if i < 16 else y_s[:, :]
            y_, yo_, s8 = yb[i % 2], yo[i % 2], st8[i % 2]
            ACT(junk.t[:nrows, :], y_.t[:nrows, :], AF.Square, [y_.k], [junk.k, s8.k], accum=s8.t[:nrows, 0:1])
            ACT(s8.t[:nrows, 1:2], s8.t[:nrows, 0:1], AF.Sqrt, [s8.k], [s8.k], bias=EPS, scale=1.0 / D)
            P.add("dve", lambda e, s8=s8, nrows=nrows: e.reciprocal(out=s8.t[:nrows, 2:3], in_=s8.t[:nrows, 1:2]),
                  reads=[s8.k], writes=[s8.k])
            STT(yo_.t[:nrows, :], y_.t[:nrows, :], s8.t[:nrows, 2:3], gfin.t[:nrows, :], ALU.mult, ALU.mult,
                [y_.k, s8.k, gfin.k], [yo_.k], eng="pool" if i % 2 else "dve")
            DMA(dst, yo_.t[:nrows, :], [yo_.k], [], ("yoD", i % 2), q="act")

        for i in range(18):
            if i < 17:
                D_a(i)
            if i > 0:
                D_b(i - 1)
        ar.release(mD)

    P.run()
    es.close()
    return nc, ar


def _bd(w):
    o = np.zeros((128, 8, 128), np.float32)
    for n in range(16):
        c, q = divmod(n, 2)
        o[q * 64:(q + 1) * 64, c, q * 64:(q + 1) * 64] = w[n]
    return o


def _pm(v):
    return np.ascontiguousarray(np.asarray(v, np.float32).reshape(-1, 128).T)


def make_in_maps(inp):
    f = lambda a: np.ascontiguousarray(np.asarray(a, dtype=np.float32))
    pvec = np.zeros((128, NPV), np.float32)
    pvec[:, PV_GMIX:PV_GMIX + 8] = _pm(inp["g_mix"][0])
    pvec[:, PV_GMEM:PV_GMEM + 8] = _pm(inp["g_mem"][0])
    pvec[:, PV_L0:PV_L0 + 4] = _pm(inp["lb_logits"][0])
    pvec[:, PV_L1:PV_L1 + 4] = _pm(inp["lb_logits"][1])
    pvec[:, PV_GA:PV_GA + 4] = _pm(inp["g_a_out"][0])
    for j in range(4):
        pvec[:, PV_WC + 8 * j:PV_WC + 8 * j + 8] = _pm(inp["w_conv"][0][j])
    pvec[:, PV_BC:PV_BC + 8] = _pm(inp["b_conv"][0])
    pvec[:, PV_BR:PV_BR + 8] = _pm(inp["b_lru_r"][0])
    pvec[:, PV_BI:PV_BI + 8] = _pm(inp["b_lru_i"][0])
    pvec[:, PV_LAM:PV_LAM + 8] = _pm(inp["lru_lambda"][0])
    gfin = np.ascontiguousarray(np.broadcast_to(f(inp["g_final"])[None, :], (128, D)))
    ident = np.eye(128, dtype=np.float32)
    s = np.arange(128)[:, None]
    t = np.arange(128)[None, :]
    amask = ((s // HCH == t // HCH) & (t >= s)).astype(np.float32)
    rmask = np.ones((128, 512), np.float32)
    rmask[:, ::HCH] = 0.0
    shared = {
        "w_in": f(inp["w_in"][0]), "w_ad": f(inp["w_a_down"][0]), "w_bd": f(inp["w_b_down"][0]),
        "w_cd": f(inp["w_c_down"][0]), "w_o": f(inp["w_out"][0]), "w_mk": f(inp["w_mem_k"][0]),
        "w_mv": f(inp["w_mem_v"][0]), "wr_bd": _bd(f(inp["w_lru_r"][0])), "wi_bd": _bd(f(inp["w_lru_i"][0])),
        "pv": pvec, "gfin": gfin, "ident": ident, "amask": amask, "rmask": rmask,
    }
    maps = []
    for b in range(8):
        m = dict(shared)
        m["xp"] = f(inp["x_prompt"][b])
        m["xs"] = f(inp["x_sample"][b])
        m["mem"] = f(inp["mem_prompt"][b])
        m["cmk"] = f(inp["cache_mem_k"][0, b]).reshape(NMEM, 512)
        m["cmv"] = f(inp["cache_mem_v"][0, b]).reshape(NMEM, 512)
        m["s_hg"] = f(inp["state_hgrn"][0, b])
        m["s_cv"] = np.ascontiguousarray(f(inp["state_conv"][0, b]).reshape(3, 8, 128).transpose(2, 1, 0))
        m["s_lr"] = _pm(inp["state_lru"][0, b])
        maps.append(m)
    return maps


_CACHE = {}


def kernel(**inputs):
    if "nc" not in _CACHE:
        _CACHE["nc"] = build()[0]
    nc = _CACHE["nc"]
    maps = make_in_maps(inputs)
    res = run_bass_kernel_spmd(nc, maps, core_ids=list(range(8)))
    R = res.results
    st = lambda k: np.stack([np.asarray(r[k], np.float32) for r in R])
    y_p = st("y_p")
    y_s = st("y_s")
    hg_p = st("o_hg_p")[None]
    cv_p = st("o_cv_p").transpose(0, 3, 2, 1).reshape(8, 3, 1024)[None]
    lr_p = st("o_lr_p").transpose(0, 2, 1).reshape(8, 1024)[None]
    mk = st("o_mk").reshape(8, NMEM, 4, 128)[None]
    mv = st("o_mv").reshape(8, NMEM, 4, 128)[None]
    hg_s = st("o_hg_s")[None]
    cv_s = st("o_cv_s").transpose(0, 3, 2, 1).reshape(8, 3, 1024)[None]
    lr_s = st("o_lr_s").transpose(0, 2, 1).reshape(8, 1024)[None]
    return (y_p, y_s, hg_p, np.ascontiguousarray(cv_p), np.ascontiguousarray(lr_p), mk, mv,
            hg_s, np.ascontiguousarray(cv_s), np.ascontiguousarray(lr_s))
```

```python
import numpy as np
from contextlib import ExitStack
import concourse.bass as bass
import concourse.mybir as mybir
from concourse.bass_utils import run_bass_kernel_spmd

F32 = mybir.dt.float32
BF16 = mybir.dt.bfloat16
AF = mybir.ActivationFunctionType
ALU = mybir.AluOpType

D = 1024
SEQ = 2048
DSEQ = 32
NMEM = 256
EPS = 1e-6
KC = 8
HCH = 128
TOK = SEQ + DSEQ

O_QA, O_FA, O_VA, O_GA, O_XB, O_GB, O_QC, O_GC, O_ZA, O_ZB, O_ZC = (
    0, 512, 1024, 1536, 2048, 3072, 4096, 4608, 5120, 6144, 7168)

PV_GMIX, PV_GMEM, PV_L0, PV_L1, PV_GA, PV_WC, PV_BC, PV_BR, PV_BI, PV_LAM = (
    0, 8, 16, 20, 24, 28, 60, 68, 76, 84)
NPV = 92


class Tok:
    __slots__ = ("name", "w", "rs", "rdma", "pre", "excl")

    def __init__(self, name):
        self.name = name
        self.w = None
        self.rs = {}
        self.rdma = []
        self.pre = []
        self.excl = False


class Op:
    __slots__ = ("idx", "eng", "fn", "deps", "is_dma", "slot", "cum", "need", "ticket", "nofuse", "vc", "waits")


ENGS = ("pe", "act", "dve", "pool", "sp")


WSTAT = {}


class Prog:
    def __init__(self, nc, es):
        self.nc = nc
        self.es = es
        self.ops = []
        self.eng_ops = {e: [] for e in ENGS}
        self.slot_cum = {}
        self.slot_hist = {}
        self.slot_sem = {}
        self.eng_sem = {}

    def add(self, eng, fn, reads=(), writes=(), slot=None, nofuse=False):
        op = Op()
        op.nofuse = nofuse
        op.idx = len(self.ops)
        op.eng = eng
        op.fn = fn
        op.is_dma = slot is not None
        op.slot = slot
        op.need = False
        op.ticket = None
        op.cum = None
        deps = {}

        def dep(o, kind):
            if o is None:
                return
            if kind == "raw" or o not in deps:
                deps[o] = kind

        for t in reads:
            dep(t.w, "raw")
            if t.excl:
                for e2, r in t.rs.items():
                    if e2 != eng:
                        dep(r, "raw")
        for t in writes:
            dep(t.w, "raw")
            for r in t.rs.values():
                dep(r, "war")
            for r in t.rdma:
                dep(r, "raw")
            for r in t.pre:
                dep(r, "raw")
        op.deps = deps
        for t in reads:
            if op.is_dma:
                t.rdma.append(op)
            else:
                t.rs[eng] = op
        for t in writes:
            t.w = op
            t.rs = {}
            t.rdma = []
        if op.is_dma:
            c = self.slot_cum.get(slot, 0) + 16
            self.slot_cum[slot] = c
            op.cum = c
            self.slot_hist.setdefault(slot, []).append((op.idx, c))
        self.ops.append(op)
        self.eng_ops[eng].append(op)
        return op

    def _resolve(self):
        for op in self.ops:
            for d, kind in op.deps.items():
                if d.is_dma:
                    continue
                if d.eng == op.eng and not op.is_dma:
                    if op.eng == "pe" or kind == "war":
                        continue
                d.need = True
        for e in ENGS:
            n = 0
            for op in self.eng_ops[e]:
                if op.need and not op.is_dma:
                    n += 1
                    op.ticket = n

    def _plan_waits(self):
        vcE = {e: {} for e in ENGS}
        for op in self.ops:
            K = vcE[op.eng]
            need = {}
            for d, kind in op.deps.items():
                if d.is_dma:
                    key = ("slot", d.slot)
                    val = self._slot_wait_value(d.slot, op.idx)
                else:
                    if d.eng == op.eng and not op.is_dma and (op.eng == "pe" or kind == "war"):
                        continue
                    key = ("eng", d.eng)
                    val = d.ticket
                if key not in need or val > need[key][0]:
                    need[key] = (val, d)
            out = []
            for key, (val, d) in sorted(need.items(), key=lambda kv: -kv[1][1].idx):
                if K.get(key, 0) >= val:
                    continue
                out.append((key, val))
                K[key] = val
                if d.vc is not None:
                    for k2, v2 in d.vc.items():
                        if v2 > K.get(k2, 0):
                            K[k2] = v2
            op.waits = out
            if op.is_dma:
                op.vc = dict(K)
                op.vc[("slot", op.slot)] = op.cum
            elif op.ticket is not None:
                op.vc = dict(K)
                op.vc[("eng", op.eng)] = op.ticket
            else:
                op.vc = None

    def _slot_wait_value(self, slot, before_idx):
        v = 0
        for idx, c in self.slot_hist[slot]:
            if idx < before_idx:
                v = c
            else:
                break
        return v

    def alloc_sems(self):
        nc = self.nc
        for e in ("pe", "act", "dve", "pool"):
            self.eng_sem[e] = self.es.enter_context(nc.semaphore("s_" + e))
        for i, s in enumerate(self.slot_hist):
            self.slot_sem[s] = self.es.enter_context(nc.semaphore("d%d" % i))

    def emit(self, e, eng):
        for op in self.eng_ops[e]:
            pend = []
            for key, val in op.waits:
                sem = self.slot_sem[key[1]] if key[0] == "slot" else self.eng_sem[key[1]]
                pend.append((sem, val))
                WSTAT[e] = WSTAT.get(e, 0) + 1
            fuse = None
            if pend and e in ("act", "dve", "pool", "pe") and not op.is_dma and not op.nofuse:
                fuse = pend.pop()
            for sem, val in pend:
                eng.wait_ge(sem, val)
            ins = op.fn(eng)
            if fuse is not None:
                ins._wait_ge(fuse[0], fuse[1])
            if op.is_dma:
                ins.then_inc(self.slot_sem[op.slot], 16)
            elif op.need:
                ins.then_inc(self.eng_sem[e], 1)
        if e == "sp":
            for s, c in self.slot_cum.items():
                eng.wait_ge(self.slot_sem[s], c)

    def run(self):
        self._resolve()
        self._plan_waits()
        self.alloc_sems()
        nc = self.nc
        with nc.Block() as block:
            @block.tensor
            def _(eng):
                self.emit("pe", eng)

            @block.scalar
            def _(eng):
                self.emit("act", eng)

            @block.vector
            def _(eng):
                self.emit("dve", eng)

            @block.gpsimd
            def _(eng):
                self.emit("pool", eng)

            @block.sync
            def _(eng):
                self.emit("sp", eng)


class Arena:
    def __init__(self, nc, base=16512, top=229344):
        self.nc = nc
        self.off = base
        self.top = top
        self.n = 0
        self.peak = base

    def alloc(self, name, shape, dtype):
        isz = 2 if dtype == BF16 else 4
        nb = isz
        for s in shape[1:]:
            nb *= s
        nb = (nb + 31) // 32 * 32
        off = self.off
        assert off + nb <= self.top, ("SBUF overflow", name, off + nb - self.top)
        self.off += nb
        self.peak = max(self.peak, self.off)
        self.n += 1
        return self.nc.alloc_sbuf_tensor_at("%s_%d" % (name, self.n), list(shape), dtype, offset=off)

    def mark(self):
        return self.off

    def release(self, m):
        self.off = m


class Buf:
    def __init__(self, t, name):
        self.t = t
        self.k = Tok(name)


def build(phases="0ABMCD", dbg=()):
    nc = bass.Bass("TRN2", target_bir_lowering=False)
    es = ExitStack()
    P = Prog(nc, es)
    ar = Arena(nc)

    def din(name, shape):
        return nc.dram_tensor(name, list(shape), F32, kind="ExternalInput").ap()

    def dout(name, shape):
        return nc.dram_tensor(name, list(shape), F32, kind="ExternalOutput").ap()

    xp = din("xp", [SEQ, D]); xsm = din("xs", [DSEQ, D]); memd = din("mem", [NMEM, D])
    cmk = din("cmk", [NMEM, 512]); cmv = din("cmv", [NMEM, 512])
    s_hg = din("s_hg", [4, 128, 128]); s_cv = din("s_cv", [128, 8, 3]); s_lr = din("s_lr", [128, 8])
    w_in = din("w_in", [D, 8192]); w_ad = din("w_ad", [512, D]); w_bd = din("w_bd", [D, D])
    w_cd = din("w_cd", [512, D]); w_o = din("w_o", [D, D]); w_mk = din("w_mk", [D, 512]); w_mv = din("w_mv", [D, 512])
    wr_d = din("wr_bd", [128, 8, 128]); wi_d = din("wi_bd", [128, 8, 128])
    pv_d = din("pv", [128, NPV]); gf_d = din("gfin", [128, D])
    id_d = din("ident", [128, 128]); mk_d = din("amask", [128, 128]); rm_d = din("rmask", [128, 512])

    y_p = dout("y_p", [SEQ, D]); y_s = dout("y_s", [DSEQ, D])
    o_hg_p = dout("o_hg_p", [4, 128, 128]); o_cv_p = dout("o_cv_p", [128, 8, 3]); o_lr_p = dout("o_lr_p", [128, 8])
    o_mk = dout("o_mk", [NMEM, 512]); o_mv = dout("o_mv", [NMEM, 512])
    o_hg_s = dout("o_hg_s", [4, 128, 128]); o_cv_s = dout("o_cv_s", [128, 8, 3]); o_lr_s = dout("o_lr_s", [128, 8])

    PB = [Buf(nc.alloc_psum_tensor("pb%d" % i, [128, 512], F32), "pb%d" % i) for i in range(8)]
    for b_ in PB:
        b_.k.excl = True
    zrot = [0]
    zlist = [PB[0], PB[1], PB[2]]

    def zbank():
        b = zlist[zrot[0] % len(zlist)]
        zrot[0] += 1
        return b
    TB = PB[3]

    all_bufs = []

    def sb(name, shape, dtype):
        o0 = ar.off
        b = Buf(ar.alloc(name, shape, dtype), name)
        o1 = ar.off
        for (p0, p1, ob) in all_bufs:
            if p0 < o1 and o0 < p1:
                k = ob.k
                for r in [k.w] + list(k.rs.values()) + k.rdma + k.pre:
                    if r is not None and r not in b.k.pre:
                        b.k.pre.append(r)
        all_bufs.append((o0, o1, b))
        return b

    hT = ar.alloc("hT", [128, KC, TOK], BF16)
    hTk = [Tok("hT%d" % i) for i in range(17)]
    mrg = ar.alloc("mrg", [128, KC, TOK], BF16)
    mrgk = [Tok("mrg%d" % i) for i in range(5)]
    ident = sb("ident", [128, 128], BF16)
    ones = sb("ones", [128, 128], BF16)
    amask = sb("amask", [128, 128], F32)
    rmask = sb("rmask", [128, 512], F32)
    pv = sb("pv", [128, NPV], F32)
    dv = sb("dv", [128, 80], F32)
    DEFER_A3 = ("A" in phases) and ("B" in phases)
    oagT_keep = sb("oagT1", [128, 4, 512], BF16) if "A" in phases else None
    DV_LB, DV_OML, DV_CL, DV_T, DV_HBR, DV_HBI, DV_HCL, DV_HOML, DV_LBH, DV_NHOML = 0, 4, 8, 16, 32, 40, 48, 56, 60, 64

    NR = 9
    ring = [sb("ring%d" % i, [128, 4096], BF16) for i in range(NR)]

    def pvc(col):
        return pv.t[:, col:col + 1]

    def dvc(col):
        return dv.t[:, col:col + 1]

    def wv(ap, kc):
        return ap.rearrange("(c p) n -> p c n", p=128)

    units = {
        "mk": wv(w_mk, 8), "mv": wv(w_mv, 8),
        "fa": wv(w_in[:, O_FA:O_FA + 512], 8), "qa": wv(w_in[:, O_QA:O_QA + 512], 8),
        "ga": wv(w_in[:, O_GA:O_GA + 512], 8), "va": wv(w_in[:, O_VA:O_VA + 512], 8),
        "za0": wv(w_in[:, O_ZA:O_ZA + 512], 8), "za1": wv(w_in[:, O_ZA + 512:O_ZA + 1024], 8),
        "ad": wv(w_ad, 4),
        "xb0": wv(w_in[:, O_XB:O_XB + 512], 8), "xb1": wv(w_in[:, O_XB + 512:O_XB + 1024], 8),
        "gb0": wv(w_in[:, O_GB:O_GB + 512], 8), "gb1": wv(w_in[:, O_GB + 512:O_GB + 1024], 8),
        "zb0": wv(w_in[:, O_ZB:O_ZB + 512], 8), "zb1": wv(w_in[:, O_ZB + 512:O_ZB + 1024], 8),
        "bd0": wv(w_bd[:, 0:512], 8), "bd1": wv(w_bd[:, 512:1024], 8),
        "qc": wv(w_in[:, O_QC:O_QC + 512], 8), "gc": wv(w_in[:, O_GC:O_GC + 512], 8),
        "zc0": wv(w_in[:, O_ZC:O_ZC + 512], 8), "zc1": wv(w_in[:, O_ZC + 512:O_ZC + 1024], 8),
        "cd": wv(w_cd, 4),
        "o0": wv(w_o[:, 0:512], 8), "o1": wv(w_o[:, 512:1024], 8),
    }
    plan = []
    if "A" in phases:
        plan += ["fa", "qa", "ga", "va", "za0", "za1", "ad"]
    if "B" in phases:
        plan += ["xb0", "xb1", "gb0", "gb1", "zb0", "zb1", "bd0", "bd1"]
    if "M" in phases:
        plan += ["mk", "mv"]
    if "C" in phases:
        plan += ["qc", "gc", "zc0", "zc1", "cd"]
    if "D" in phases:
        plan += ["o0", "o1"]
    free_slots = list(range(NR))
    where = {}

    def pump(limit=None, gate=()):
        n_ = 0
        while plan and free_slots and (limit is None or n_ < limit):
            n_ += 1
            u = plan.pop(0)
            s = free_slots.pop(0)
            where[u] = s
            src = units[u]
            a = src.shape[1]
            dst = ring[s].t[:, :].rearrange("p (a b) -> p a b", a=a)
            P.add("pool", lambda e, dst=dst, src=src: e.dma_start(out=dst, in_=src),
                  reads=list(gate), writes=[ring[s].k], slot=("ring", s))

    def W(u, a=8):
        s = where[u]
        return ring[s].t[:, :].rearrange("p (a b) -> p a b", a=a), ring[s].k

    def wfree(u):
        free_slots.append(where.pop(u))
        pump()

    def ACT(out, in_, func, reads, writes, bias=0.0, scale=1.0, accum=None):
        if accum is None:
            P.add("act", lambda e: e.activation(out=out, in_=in_, func=func, bias=bias, scale=scale),
                  reads=reads, writes=writes)
        else:
            P.add("act", lambda e: e.activation(out=out, in_=in_, func=func, bias=bias, scale=scale,
                                               accum_out=accum), reads=reads, writes=writes, nofuse=True)

    def TS(out, in0, s1, s2, op0, op1, reads, writes, eng="dve"):
        if s2 is None:
            P.add(eng, lambda e: e.tensor_scalar(out=out, in0=in0, scalar1=s1, scalar2=None, op0=op0),
                  reads=reads, writes=writes)
        else:
            P.add(eng, lambda e: e.tensor_scalar(out=out, in0=in0, scalar1=s1, scalar2=s2, op0=op0, op1=op1),
                  reads=reads, writes=writes)

    def TT(out, in0, in1, op, reads, writes, eng="dve"):
        P.add(eng, lambda e: e.tensor_tensor(out=out, in0=in0, in1=in1, op=op), reads=reads, writes=writes)

    def STT(out, in0, sc, in1, op0, op1, reads, writes, eng="dve"):
        if eng == "pool":
            P.add("pool", lambda e: e.tensor_scalar(out=out, in0=in0, scalar1=sc, scalar2=1.0, op0=op0, op1=ALU.mult),
                  reads=reads, writes=writes)
            P.add("pool", lambda e: e.tensor_tensor(out=out, in0=out, in1=in1, op=op1), reads=reads + writes, writes=writes)
            return
        P.add("dve", lambda e: e.scalar_tensor_tensor(out=out, in0=in0, scalar=sc, in1=in1, op0=op0, op1=op1),
              reads=reads, writes=writes)

    def CP(out, in_, reads, writes, eng="dve"):
        if eng == "act":
            P.add(eng, lambda e: e.activation(out=out, in_=in_, func=AF.Copy), reads=reads, writes=writes)
        else:
            P.add(eng, lambda e: e.tensor_copy(out=out, in_=in_), reads=reads, writes=writes)

    def MM(out, lhsT, rhs, start, stop, reads, writes):
        P.add("pe", lambda e: e.matmul(out, lhsT, rhs, start=start, stop=stop, skip_group_check=True),
              reads=reads, writes=writes)

    def TR(out, in_, idn, reads, writes):
        P.add("pe", lambda e: e.transpose(out, in_, idn), reads=reads, writes=writes)

    def DMA(out, in_, reads, writes, slot, q="sp"):
        P.add(q, lambda e: e.dma_start(out=out, in_=in_), reads=reads, writes=writes, slot=slot)

    def hTtoks(t0, T):
        return [hTk[i] for i in range(t0 // 128, (t0 + T + 127) // 128)]

    def zmm(u, col0, t0, T, bank, ncol=128):
        wt, wk = W(u)
        rd = [wk] + hTtoks(t0, T)
        for kc in range(KC):
            MM(bank.t[:ncol, :T], wt[:, kc, col0:col0 + ncol], hT[:, kc, t0:t0 + T],
               kc == 0, kc == KC - 1, rd, [bank.k])

    dbg_list = []

    def DBG(name, ap, shape, toks, dtype=BF16):
        if name in dbg:
            d = nc.dram_tensor("dbg_" + name, list(shape), dtype, kind="ExternalOutput").ap()
            DMA(d, ap, toks, [], ("dbg", name))

    DMA(pv.t[:, :], pv_d, [], [pv.k], "c_pv")
    DMA(amask.t[:, :], mk_d, [], [amask.k], "c_am")
    DMA(rmask.t[:, :], rm_d, [], [rmask.k], "c_rm")
    P.add("pool", lambda e: e.dma_start(out=ident.t[:, :], in_=id_d), writes=[ident.k], slot="c_id")
    P.add("dve", lambda e: e.memset(ones.t[:, :], 1.0), writes=[ones.k])
    pump(limit=3)
    TT(dv.t[:, DV_T:DV_T + 4], pv.t[:, PV_L0:PV_L0 + 4], pv.t[:, PV_L1:PV_L1 + 4], ALU.subtract, [pv.k], [dv.k])
    ACT(dv.t[:, DV_LB:DV_LB + 4], dv.t[:, DV_T:DV_T + 4], AF.Sigmoid, [dv.k], [dv.k])
    TS(dv.t[:, DV_OML:DV_OML + 4], dv.t[:, DV_LB:DV_LB + 4], -1.0, 1.0, ALU.mult, ALU.add, [dv.k], [dv.k])
    TS(dv.t[:, DV_HOML:DV_HOML + 4], dv.t[:, DV_LB:DV_LB + 4], -0.5, 0.5, ALU.mult, ALU.add, [dv.k], [dv.k])
    TS(dv.t[:, DV_LBH:DV_LBH + 4], dv.t[:, DV_LB:DV_LB + 4], 0.5, 0.5, ALU.mult, ALU.add, [dv.k], [dv.k])
    TS(dv.t[:, DV_NHOML:DV_NHOML + 4], dv.t[:, DV_LB:DV_LB + 4], 0.5, -0.5, ALU.mult, ALU.add, [dv.k], [dv.k])
    ACT(dv.t[:, DV_T:DV_T + 8], pv.t[:, PV_LAM:PV_LAM + 8], AF.Exp, [pv.k, dv.k], [dv.k], scale=-1.0)
    ACT(dv.t[:, DV_T + 8:DV_T + 16], dv.t[:, DV_T:DV_T + 8], AF.Ln, [dv.k], [dv.k], bias=1.0)
    TS(dv.t[:, DV_CL:DV_CL + 8], dv.t[:, DV_T + 8:DV_T + 16], -8.0, None, ALU.mult, None, [dv.k], [dv.k])
    TS(dv.t[:, DV_HCL:DV_HCL + 8], dv.t[:, DV_T + 8:DV_T + 16], -4.0, None, ALU.mult, None, [dv.k], [dv.k])
    TS(dv.t[:, DV_HBR:DV_HBR + 8], pv.t[:, PV_BR:PV_BR + 8], 0.5, None, ALU.mult, None, [pv.k, dv.k], [dv.k])
    TS(dv.t[:, DV_HBI:DV_HBI + 8], pv.t[:, PV_BI:PV_BI + 8], 0.5, None, ALU.mult, None, [pv.k, dv.k], [dv.k])

    groups = [(0, 512), (512, 512), (1024, 512), (1536, 512), (2048, 32)]

    m0 = ar.mark()
    NB = {}

    def alloc_norm(tag, n=2):
        NB["xt"] = [sb("xt%s%d" % (tag, i), [128, D], F32) for i in range(n)]
        NB["junk"] = sb("junk" + tag, [128, D], BF16)
        NB["xn"] = [sb("xn%s%d" % (tag, i), [128, D], BF16) for i in range(n)]
        NB["st8"] = [sb("st8%s_%d" % (tag, i), [128, 8], F32) for i in range(n)]
        NB["tag"] = tag

    alloc_norm("0", 4)
    TBv = TB.t[:, :].bitcast(BF16).rearrange("p (a b) -> p a b", a=8)
    cnt = [0]

    def norm_a(src, nrows):
        i = cnt[0] % len(NB["xt"])
        cnt[0] += 1
        x, xb_, s8, junk = NB["xt"][i], NB["xn"][i], NB["st8"][i], NB["junk"]
        DMA(x.t[:nrows, :], src, [], [x.k], ("xt" + NB["tag"], i))
        ACT(junk.t[:nrows, :], x.t[:nrows, :], AF.Square, [x.k], [junk.k, s8.k], accum=s8.t[:nrows, 0:1])
        ACT(s8.t[:nrows, 1:2], s8.t[:nrows, 0:1], AF.Sqrt, [s8.k], [s8.k], bias=EPS, scale=1.0 / D)
        P.add("dve", lambda e: e.reciprocal(out=s8.t[:nrows, 2:3], in_=s8.t[:nrows, 1:2]), reads=[s8.k], writes=[s8.k])
        ACT(xb_.t[:nrows, :], x.t[:nrows, :], AF.Copy, [x.k, s8.k], [xb_.k], scale=s8.t[:nrows, 2:3])
        return xb_

    def norm_b(xb_, nrows, gcol, dstT, dtok):
        for kc in range(KC):
            TR(TBv[:, kc, 0:nrows], xb_.t[:nrows, kc * 128:(kc + 1) * 128], ident.t[:nrows, :nrows],
               [xb_.k, ident.k], [TB.k])
        gb_ = pv.t[:, gcol:gcol + 8].unsqueeze(2).to_broadcast([128, 8, nrows])
        TT(dstT, TBv[:, :, 0:nrows], gb_, ALU.mult, [TB.k, pv.k], [dtok])

    def norm_T(src, nrows, gcol, dstT, dtok):
        norm_b(norm_a(src, nrows), nrows, gcol, dstT, dtok)

    def norm_many(items):
        prev = None
        for it in items:
            xb_ = norm_a(it[0], it[1])
            if prev is not None:
                norm_b(*prev)
            prev = (xb_, it[1], it[2], it[3], it[4])
        norm_b(*prev)

    norm_many([(xp[i * 128:(i + 1) * 128, :], 128, PV_GMIX, hT[:, :, i * 128:(i + 1) * 128], hTk[i]) for i in range(16)]
              + [(xsm[:, :], DSEQ, PV_GMIX, hT[:, :, SEQ:SEQ + DSEQ], hTk[16])])
    DBG("hT", hT[:, :, :], [128, KC, TOK], hTk)
    pump(gate=[hTk[16]])

    ar.release(m0)

    MKV = {}
    def phase_M():
        alloc_norm("M")
        hmT = sb("hmT", [128, KC, NMEM], BF16)
        norm_many([(memd[j * 128:(j + 1) * 128, :], 128, PV_GMEM, hmT.t[:, :, j * 128:(j + 1) * 128], hmT.k)
                   for j in range(2)])
        wk_, wkk = W("mk")
        wv_, wvk = W("mv")
        for h in range(4):
            b = zbank()
            for kc in range(KC):
                MM(b.t[:, :NMEM], wk_[:, kc, h * 128:(h + 1) * 128], hmT.t[:, kc, :], kc == 0, kc == KC - 1,
                   [wkk, hmT.k], [b.k])
            CP(kT_p.t[:, h, :], b.t[:, :NMEM], [b.k], [kT_p.k], eng="act")
        stg = [sb("stg%d" % i, [128, 512], F32) for i in range(2)]
        n = 0
        for (wt_, wtk, od, isv) in ((wk_, wkk, o_mk, False), (wv_, wvk, o_mv, True)):
            for j in range(2):
                b = zbank()
                for kc in range(KC):
                    MM(b.t[:, :], hmT.t[:, kc, j * 128:(j + 1) * 128], wt_[:, kc, :], kc == 0, kc == KC - 1,
                       [wtk, hmT.k], [b.k])
                s = stg[n % 2]
                n += 1
                CP(s.t[:, :], b.t[:, :], [b.k], [s.k], eng="act")
                if isv:
                    CP(v_p.t[:, j, :], b.t[:, :], [b.k], [v_p.k], eng="dve")
                DMA(od[j * 128:(j + 1) * 128, :], s.t[:, :], [s.k], [], ("stg", n % 2))
        wfree("mk")
        wfree("mv")
        cmkb = sb("cmkb", [128, 2, 512], BF16)
        P.add("pool", lambda e: e.dma_start(out=cmkb.t[:, :, :], in_=cmk.rearrange("(j p) n -> p j n", p=128)),
              writes=[cmkb.k], slot="c_cmk")
        P.add("pool", lambda e: e.dma_start(out=v_s.t[:, :, :], in_=cmv.rearrange("(j p) n -> p j n", p=128)),
              writes=[v_s.k], slot="c_cmv")
        for h in range(4):
            for j in range(2):
                TR(TBv[:, h * 2 + j, :], cmkb.t[:, j, h * 128:(h + 1) * 128], ident.t[:, :], [cmkb.k, ident.k], [TB.k])
        CP(kT_s.t[:, :, :], TB.t[:, :].bitcast(BF16).rearrange("p (a b) -> p a b", a=4), [TB.k], [kT_s.k])
        DBG("kT_p", kT_p.t[:, :, :], [128, 4, 256], [kT_p.k])
        DBG("kT_s", kT_s.t[:, :, :], [128, 4, 256], [kT_s.k])


    if not any(p in phases for p in "ABC"):
        for g in range(5):
            t0, T = groups[g]
            P.add("dve", lambda e, t0=t0, T=T: e.memset(mrg[:, :, t0:t0 + T], 0.0), writes=[mrgk[g]])


    def mkpool(n, name):
        bufs = [sb("%s%d" % (name, i), [128, 512], F32) for i in range(n)]
        c = [0]

        def get():
            b = bufs[c[0] % n]
            c[0] += 1
            return b
        return get

    mrg_init = [False] * 5

    def merge(g, j, t0, T, pbank, tz):
        dst = mrg[:, j, t0:t0 + T]
        if not mrg_init[g]:
            STT(dst, tz.t[:, :T], 1.0, pbank.t[:, :T], ALU.add, ALU.mult, [pbank.k, tz.k], [mrgk[g]])
        else:
            STT(pbank.t[:, :T], tz.t[:, :T], 1.0, pbank.t[:, :T], ALU.add, ALU.mult, [pbank.k, tz.k], [pbank.k])
            TT(dst, dst, pbank.t[:, :T], ALU.add, [pbank.k, mrgk[g]], [mrgk[g]])

    if "A" in phases:
        mA = ar.mark()
        ATT, OB, KVB, XB = PB[4], PB[5], PB[6], PB[7]
        poolXa = mkpool(6, "tAx")
        poolYa = mkpool(2, "tAy")
        SETS = []
        for s_ in range(2):
            SETS.append(dict(
                qgT=sb("qgT%d" % s_, [128, 4, 512], BF16), kgT=sb("kgT%d" % s_, [128, 4, 512], BF16),
                sga=sb("sga%d" % s_, [128, 4, 512], BF16), dec=sb("dec%d" % s_, [128, 4, 8], F32)))
        kd_tm1 = sb("kd_tm", [128, 4, 512], BF16); v_tm1 = sb("v_tm", [128, 4, 512], BF16)
        for s_ in range(2):
            SETS[s_]["kd_tm"] = kd_tm1
            SETS[s_]["v_tm"] = v_tm1
        kdT = sb("kdT", [128, 4, 512], BF16)
        attm = [sb("attm0", [128, 4, 128], BF16)] * 2
        S_p = sb("S_p", [128, 4, 128], F32); S_s = S_p
        Sb = [sb("Sb%d" % i, [128, 4, 128], BF16) for i in range(2)]
        sq = sb("sqA", [128, 512], BF16)
        oagTs = [sb("oagT0", [128, 4, 512], BF16), oagT_keep]
        P.add("dve", lambda e: e.memset(S_p.t[:, :, :], 0.0), writes=[S_p.k])
        sbi = [0]

        def geo(g):
            t0, T = groups[g]
            TSZ = min(128, T); CS = min(HCH, T)
            return t0, T, TSZ, T // TSZ, CS, TSZ // CS, T // CS

        def XA(g, h):
            t0, T, TSZ, NT, CS, CPT, NCH = geo(g)
            st = SETS[g % 2]
            last = g == 4
            tmp = poolXa
            bf_ = zbank(); zmm("fa", h * 128, t0, T, bf_)
            A_ = tmp(); ACT(A_.t[:, :T], bf_.t[:, :T], AF.Tanh, [bf_.k], [A_.k], scale=0.5)
            yield
            bq_ = zbank(); zmm("qa", h * 128, t0, T, bq_)
            Q_ = tmp(); ACT(Q_.t[:, :T], bq_.t[:, :T], AF.Tanh, [bq_.k], [Q_.k], scale=0.5)
            STT(Q_.t[:, :T], Q_.t[:, :T], 1.0, bq_.t[:, :T], ALU.add, ALU.mult, [Q_.k, bq_.k], [Q_.k])
            yield
            bg_ = zbank(); zmm("ga", h * 128, t0, T, bg_)
            G_ = tmp(); ACT(G_.t[:, :T], bg_.t[:, :T], AF.Tanh, [bg_.k], [G_.k], scale=0.5)
            STT(st["sga"].t[:, h, :T], G_.t[:, :T], 1.0, bg_.t[:, :T], ALU.add, ALU.mult, [G_.k, bg_.k], [st["sga"].k])
            if last and h == 3:
                wfree("fa"); wfree("qa"); wfree("ga")
            yield
            C_ = tmp(); ACT(C_.t[:, :T], A_.t[:, :T], AF.Ln, [A_.k, dv.k], [C_.k], bias=dvc(DV_LBH + h), scale=dvc(DV_HOML + h))
            K_ = G_
            TS(K_.t[:, :T], A_.t[:, :T], dvc(DV_NHOML + h), dvc(DV_HOML + h), ALU.mult, ALU.add, [A_.k, G_.k, dv.k], [K_.k])
            yield
            D_ = tmp()
            P.add("dve", lambda e, D_=D_, C_=C_, T=T: e.tensor_tensor_scan(
                out=D_.t[:, :T], data0=rmask.t[:, :T], data1=C_.t[:, :T], initial=0.0, op0=ALU.mult, op1=ALU.add),
                reads=[rmask.k, C_.k], writes=[D_.k])
            yield
            E_ = A_
            ACT(E_.t[:, :T], D_.t[:, :T], AF.Exp, [D_.k, A_.k], [E_.k])
            F_ = tmp(); ACT(F_.t[:, :T], D_.t[:, :T], AF.Exp, [D_.k], [F_.k], scale=-1.0)
            yield
            e3 = E_.t[:, :T].rearrange("p (c s) -> p c s", s=CS)
            CP(st["dec"].t[:, h, 0:NCH].unsqueeze(2), e3[:, :, CS - 1:CS], [E_.k], [st["dec"].k])
            TT(F_.t[:, :T], K_.t[:, :T], F_.t[:, :T], ALU.mult, [K_.k, F_.k], [F_.k], eng="pool")
            TT(st["qgT"].t[:, h, :T], Q_.t[:, :T], E_.t[:, :T], ALU.mult, [Q_.k, E_.k], [st["qgT"].k], eng="pool")
            yield
            CP(st["kgT"].t[:, h, :T], F_.t[:, :T], [F_.k], [st["kgT"].k], eng="act")
            f3 = F_.t[:, :T].rearrange("p (c s) -> p c s", s=CS)
            TT(kdT.t[:, h, :T].rearrange("p (c s) -> p c s", s=CS), f3, e3[:, :, CS - 1:CS].to_broadcast([128, NCH, CS]),
               ALU.mult, [F_.k, E_.k], [kdT.k], eng="pool")
            yield
            if last and h == 3:
                pass

        def XA_fin(g):
            t0, T, TSZ, NT, CS, CPT, NCH = geo(g)
            st = SETS[g % 2]
            last = g == 4
            wva, wvak = W("va")
            for h in range(NT):
                b = zbank()
                c0 = t0 + h * TSZ
                for kc in range(KC):
                    MM(b.t[:TSZ, :], hT[:, kc, c0:c0 + TSZ], wva[:, kc, :], kc == 0, kc == KC - 1,
                       [wvak] + hTtoks(c0, TSZ), [b.k])
                CP(st["v_tm"].t[:TSZ, h, :], b.t[:TSZ, :], [b.k], [st["v_tm"].k], eng="act")
            if last:
                wfree("va")
            for i in range(NT):
                for h in range(4):
                    TR(TBv[:TSZ, h, :], kdT.t[:, h, i * TSZ:(i + 1) * TSZ], ident.t[:, :], [kdT.k, ident.k], [TB.k])
                CP(st["kd_tm"].t[:TSZ, i, :].rearrange("p (a b) -> p a b", a=4), TBv[:TSZ, 0:4, :], [TB.k], [st["kd_tm"].k])

        def YA(g, i):
            t0, T, TSZ, NT, CS, CPT, NCH = geo(g)
            st = SETS[g % 2]
            qgT, kgT, kd_tm, v_tm, sga, dec = st["qgT"], st["kgT"], st["kd_tm"], st["v_tm"], st["sga"], st["dec"]
            S = S_p if g < 4 else S_s
            oagT = oagTs[g % 2]
            if i == 0 and (g == 0 or g == 4):
                CP(Sb[sbi[0]].t[:, :, :], S.t[:, :, :], [S.k], [Sb[sbi[0]].k])
            am = attm[i % 2]
            c0 = i * TSZ
            for h in range(4):
                MM(ATT.t[:TSZ, h * 128:h * 128 + TSZ], kgT.t[:, h, c0:c0 + TSZ], qgT.t[:, h, c0:c0 + TSZ], True, True,
                   [kgT.k, qgT.k], [ATT.k])
            for h in range(4):
                TT(am.t[:TSZ, h, :TSZ], ATT.t[:TSZ, h * 128:h * 128 + TSZ], amask.t[:TSZ, :TSZ], ALU.mult,
                   [ATT.k, amask.k], [am.k])
            yield
            yield
            for h in range(4):
                MM(OB.t[:, h * TSZ:(h + 1) * TSZ], v_tm.t[:TSZ, i, h * 128:(h + 1) * 128], am.t[:TSZ, h, :TSZ],
                   h == 0, False, [v_tm.k, am.k], [OB.k])
            for c in range(CPT):
                gc = i * CPT + c
                cur = Sb[sbi[0]]
                for h in range(4):
                    MM(OB.t[:, h * TSZ + c * CS:h * TSZ + (c + 1) * CS], cur.t[:, h, :],
                       qgT.t[:, h, c0 + c * CS:c0 + (c + 1) * CS], False, (c == CPT - 1 and h == 3),
                       [cur.k, qgT.k], [OB.k])
                for h in range(4):
                    MM(KVB.t[:, h * 128:(h + 1) * 128], kd_tm.t[c * CS:(c + 1) * CS, i, h * 128:(h + 1) * 128],
                       v_tm.t[c * CS:(c + 1) * CS, i, h * 128:(h + 1) * 128], True, True,
                       [kd_tm.k, v_tm.k], [KVB.k])
                for h in range(4):
                    STT(S.t[:, h, :], S.t[:, h, :], dec.t[:, h, gc:gc + 1],
                        KVB.t[:, h * 128:(h + 1) * 128], ALU.mult, ALU.add, [S.k, dec.k, KVB.k], [S.k])
                sbi[0] ^= 1
                CP(Sb[sbi[0]].t[:, :, :], S.t[:, :, :], [S.k], [Sb[sbi[0]].k])
                yield
                yield
            W4 = 4 * TSZ
            ACT(sq.t[:, :W4], OB.t[:, :W4], AF.Square, [OB.k], [sq.k])
            yield
            MM(XB.t[:, :W4], ones.t[:, :], sq.t[:, :W4], True, True, [ones.k, sq.k], [XB.k])
            l_ = poolYa(); ACT(l_.t[:, :W4], XB.t[:, :W4], AF.Ln, [XB.k], [l_.k], bias=4.0 * EPS, scale=1.0 / 128)
            ACT(l_.t[:, :W4], l_.t[:, :W4], AF.Exp, [l_.k], [l_.k], scale=-0.5)
            yield
            for h in range(4):
                STT(OB.t[:, h * TSZ:(h + 1) * TSZ], OB.t[:, h * TSZ:(h + 1) * TSZ], pvc(PV_GA + h),
                    l_.t[:, h * TSZ:(h + 1) * TSZ], ALU.mult, ALU.mult, [OB.k, pv.k, l_.k], [OB.k])
            TT(oagT.t[:, :, c0:c0 + TSZ], OB.t[:, :W4].rearrange("p (a b) -> p a b", a=4), sga.t[:, :, c0:c0 + TSZ],
               ALU.mult, [OB.k, sga.k], [oagT.k])

        def downA(g, j, pool=None):
            t0, T = groups[g]
            last = g == (3 if DEFER_A3 else 4)
            wad, wadk = W("ad", 4)
            oagT = oagTs[g % 2]
            b = zbank()
            for c in range(4):
                MM(b.t[:, :T], wad[:, c, j * 128:(j + 1) * 128], oagT.t[:, c, :T], c == 0, c == 3, [wadk, oagT.k], [b.k])
            bz = zbank(); zmm("za%d" % (j // 4), (j % 4) * 128, t0, T, bz)
            sgz = (pool or poolYa)(); ACT(sgz.t[:, :T], bz.t[:, :T], AF.Tanh, [bz.k], [sgz.k], scale=0.5)
            merge(g, j, t0, T, b, sgz)
            if last and j == 3:
                wfree("za0")
            if last and j == 7:
                wfree("za1"); wfree("ad")
            if j == 7:
                mrg_init[g] = True

        def run2(*gs):
            gs = [x for x in gs if x is not None]
            while gs:
                for x in list(gs):
                    try:
                        next(x)
                    except StopIteration:
                        gs.remove(x)

        def Y2(g, p):
            NT = geo(g)[3]
            for i in (2 * p, 2 * p + 1):
                if i < NT:
                    yield from YA(g, i)

        def DA(g, js):
            yield
            downA(g, js[0])
            yield
            yield
            yield
            downA(g, js[1])
            yield

        for h in range(4):
            run2(XA(0, h))
        XA_fin(0)
        for g in range(6):
            NT = geo(g)[3] if g < 5 else 0
            for i in range(4):
                run2(XA(g + 1, i) if g + 1 < 5 else None,
                     YA(g, i) if i < NT else None,
                     DA(g - 1, (2 * i, 2 * i + 1)) if (g >= 1 and not (DEFER_A3 and g - 1 == 3)) else None)
            if g + 1 < 5:
                XA_fin(g + 1)
            if g == 3:
                DMA(o_hg_p.rearrange("h k v -> k h v"), S_p.t[:, :, :], [S_p.k], [], "o_hgp")
                DMA(S_p.t[:, :, :], s_hg.rearrange("h k v -> k h v"), [S_p.k], [S_p.k], "c_shg")
            if g == 4:
                DMA(o_hg_s.rearrange("h k v -> k h v"), S_s.t[:, :, :], [S_s.k], [], "o_hgs")
        ar.release(mA)
    else:
        pass

    if "B" in phases:
        mB = ar.mark()
        wr = sb("wr", [128, 8, 128], BF16)
        wi = sb("wi", [128, 8, 128], BF16)
        P.add("pool", lambda e: e.dma_start(out=wr.t[:, :, :], in_=wr_d), writes=[wr.k], slot="c_wr")
        P.add("pool", lambda e: e.dma_start(out=wi.t[:, :, :], in_=wi_d), writes=[wi.k], slot="c_wi")
        def mkpool2(n, name, dtype):
            bufs = [sb("%s%d" % (name, i), [128, 512], dtype) for i in range(n)]
            c = [0]

            def get():
                b_ = bufs[c[0] % n]
                c[0] += 1
                return b_
            return get
        poolX = mkpool2(4, "pX", F32)
        poolG = mkpool2(4, "pG", BF16)
        poolY = mkpool2(8, "pY", F32)
        poolD = mkpool2(2, "pD", BF16)
        xsb = [sb("xsb%d" % i, [128, 3 + 512], F32) for i in range(2)]
        xcb = [sb("xcb%d" % i, [128, 512], BF16) for i in range(4)]
        hbgs = [sb("hbg%d" % i, [128, 8, 512], BF16) for i in range(2)]
        cst_p = sb("cst_p", [128, 8, 3], F32); cst_s = sb("cst_s", [128, 8, 3], F32)
        hl_p = sb("hl_p", [128, 8], F32); hl_s = sb("hl_s", [128, 8], F32)
        P.add("dve", lambda e: e.memset(cst_p.t[:, :, :], 0.0), writes=[cst_p.k])
        P.add("dve", lambda e: e.memset(hl_p.t[:, :], 0.0), writes=[hl_p.k])
        DMA(cst_s.t[:, :, :], s_cv, [], [cst_s.k], "c_scv")
        DMA(hl_s.t[:, :], s_lr, [], [hl_s.k], "c_slr")
        zlist[:] = [PB[2], PB[3]]
        XS = {}

        def stageX(g, pr):
            t0, T = groups[g]
            last = g == 4
            cst = cst_p if g < 4 else cst_s
            cs = (2 * pr, 2 * pr + 1)
            Zx = {}; Zg = {}; X = {}; G = {}; XC = {}
            for c in cs:
                Zx[c] = PB[c % 2]; zmm("xb%d" % (c // 4), (c % 4) * 128, t0, T, Zx[c])
            if last and pr % 2 == 1:
                wfree("xb%d" % (pr // 2))
            for c in cs:
                xs_ = xsb[c % 2]
                CP(xs_.t[:, 0:3], cst.t[:, c, :], [cst.k], [xs_.k])
                CP(xs_.t[:, 3:3 + T], Zx[c].t[:, :T], [Zx[c].k], [xs_.k], eng="act")
                CP(cst.t[:, c, :], xs_.t[:, T:T + 3], [xs_.k], [cst.k])
            yield
            for c in cs:
                Zg[c] = zbank(); zmm("gb%d" % (c // 4), (c % 4) * 128, t0, T, Zg[c])
                tg = poolY(); ACT(tg.t[:, :T], Zg[c].t[:, :T], AF.Tanh, [Zg[c].k], [tg.k], scale=0.5)
                G[c] = poolG()
                STT(G[c].t[:, :T], tg.t[:, :T], 1.0, Zg[c].t[:, :T], ALU.add, ALU.mult, [tg.k, Zg[c].k], [G[c].k])
            if last and pr % 2 == 1:
                wfree("gb%d" % (pr // 2))
            yield
            for c in cs:
                Z = Zx[c]
                TS(Z.t[:, :T], Z.t[:, :T], pvc(PV_WC + 24 + c), pvc(PV_BC + c), ALU.mult, ALU.add, [Z.k, pv.k], [Z.k])
            yield
            for tap, off in ((2, 16), (1, 8)):
                for c in cs:
                    Z = Zx[c]; xs_ = xsb[c % 2]
                    STT(Z.t[:, :T], xs_.t[:, tap:tap + T], pvc(PV_WC + off + c), Z.t[:, :T], ALU.mult, ALU.add,
                        [xs_.k, Z.k, pv.k], [Z.k])
                yield
            for c in cs:
                Z = Zx[c]; xs_ = xsb[c % 2]
                X[c] = poolX()
                STT(X[c].t[:, :T], xs_.t[:, 0:T], pvc(PV_WC + c), Z.t[:, :T], ALU.mult, ALU.add, [xs_.k, Z.k, pv.k], [X[c].k])
            for c in cs:
                XC[c] = xcb[(2 * pr + (c % 2)) % 4]
                CP(XC[c].t[:, :T], X[c].t[:, :T], [X[c].k], [XC[c].k], eng="act")
            XS[(g, pr)] = (X, G, XC)
            yield

        def stageY(g, pr):
            t0, T = groups[g]
            hbg = hbgs[g % 2]
            hl = hl_p if g < 4 else hl_s
            cs = (2 * pr, 2 * pr + 1)
            X, G, XC = XS.pop((g, pr))
            R = {}; A2 = {}; I_ = {}
            for c in cs:
                Rb, Ib = PB[4 + 2 * (c % 2)], PB[5 + 2 * (c % 2)]
                MM(Rb.t[:, :T], wr.t[:, c, :], XC[c].t[:, :T], True, True, [wr.k, XC[c].k], [Rb.k])
                MM(Ib.t[:, :T], wi.t[:, c, :], XC[c].t[:, :T], True, True, [wi.k, XC[c].k], [Ib.k])
            for c in cs:
                Rb, Ib = PB[4 + 2 * (c % 2)], PB[5 + 2 * (c % 2)]
                R[c] = poolY(); ACT(R[c].t[:, :T], Rb.t[:, :T], AF.Tanh, [Rb.k, dv.k], [R[c].k], bias=dvc(DV_HBR + c), scale=0.5)
                I_[c] = poolY(); ACT(I_[c].t[:, :T], Ib.t[:, :T], AF.Tanh, [Ib.k, dv.k], [I_[c].k], bias=dvc(DV_HBI + c), scale=0.5)
            yield
            for c in cs:
                ACT(R[c].t[:, :T], R[c].t[:, :T], AF.Exp, [R[c].k, dv.k], [R[c].k], bias=dvc(DV_HCL + c), scale=dvc(DV_HCL + c))
            for c in cs:
                TS(I_[c].t[:, :T], I_[c].t[:, :T], 1.0, 1.0, ALU.add, ALU.mult, [I_[c].k], [I_[c].k], eng="pool")
            yield
            for c in cs:
                A2[c] = poolY()
                TT(A2[c].t[:, :T], R[c].t[:, :T], R[c].t[:, :T], ALU.mult, [R[c].k], [A2[c].k], eng="pool")
            yield
            for c in cs:
                TT(I_[c].t[:, :T], I_[c].t[:, :T], X[c].t[:, :T], ALU.mult, [I_[c].k, X[c].k], [I_[c].k], eng="pool")
            for c in cs:
                ACT(A2[c].t[:, :T], A2[c].t[:, :T], AF.Sqrt, [A2[c].k], [A2[c].k], bias=0.25, scale=-0.25)
                if g == 0:
                    P.add("dve", lambda e, b_=A2[c]: e.memset(b_.t[:, 0:1], 0.5), writes=[A2[c].k])
            yield
            for c in cs:
                TT(I_[c].t[:, :T], I_[c].t[:, :T], A2[c].t[:, :T], ALU.mult, [I_[c].k, A2[c].k], [I_[c].k])
            yield
            for c in cs:
                P.add("dve", lambda e, hb=A2[c], a=R[c], u=I_[c], hl=hl, c=c, T=T: e.tensor_tensor_scan(
                    out=hb.t[:, :T], data0=a.t[:, :T], data1=u.t[:, :T], initial=hl.t[:, c:c + 1],
                    op0=ALU.mult, op1=ALU.add), reads=[R[c].k, I_[c].k, hl.k], writes=[A2[c].k])
                CP(hl.t[:, c:c + 1], A2[c].t[:, T - 1:T], [A2[c].k], [hl.k])
            yield
            for c in cs:
                TT(hbg.t[:, c, :T], A2[c].t[:, :T], G[c].t[:, :T], ALU.mult, [A2[c].k, G[c].k], [hbg.k], eng="pool")
            yield

        def downB(g, j):
            t0, T = groups[g]
            last = g == 4
            hbg = hbgs[g % 2]
            b = zbank()
            wbd, wbdk = W("bd%d" % (j // 4))
            jc = (j % 4) * 128
            for c in range(8):
                MM(b.t[:, :T], wbd[:, c, jc:jc + 128], hbg.t[:, c, :T], c == 0, c == 7, [wbdk, hbg.k], [b.k])
            bz = zbank(); zmm("zb%d" % (j // 4), jc, t0, T, bz)
            sgz = poolD(); ACT(sgz.t[:, :T], bz.t[:, :T], AF.Tanh, [bz.k], [sgz.k], scale=0.5)
            merge(g, j, t0, T, b, sgz)
            if last and j % 4 == 3:
                wfree("zb%d" % (j // 4)); wfree("bd%d" % (j // 4))
            if j == 7:
                mrg_init[g] = True

        def runB(*gs):
            gs = [x for x in gs if x is not None]
            while gs:
                for x in list(gs):
                    try:
                        next(x)
                    except StopIteration:
                        gs.remove(x)

        def DB(g, pr):
            yield
            yield
            yield
            yield
            downB(g, 2 * pr + 0)
            yield
            yield
            yield
            downB(g, 2 * pr + 1)
            yield

        def DA3(k):
            yield
            yield
            yield
            yield
            downA(3, 2 * k, poolD)
            yield
            yield
            yield
            downA(3, 2 * k + 1, poolD)
            yield

        seq = [(g, pr) for g in range(5) for pr in range(4)]
        runB(stageX(*seq[0]))
        for k in range(len(seq) + 4):
            runB(stageX(*seq[k + 1]) if k + 1 < len(seq) else None,
                 stageY(*seq[k]) if k < len(seq) else None,
                 DB(*seq[k - 4]) if k >= 4 else (DA3(k) if DEFER_A3 else None))
            if k < len(seq):
                g, pr = seq[k]
                if pr == 3 and g == 3:
                    DMA(o_cv_p, cst_p.t[:, :, :], [cst_p.k], [], "o_cvp")
                    DMA(o_lr_p, hl_p.t[:, :], [hl_p.k], [], "o_lrp")
                if pr == 3 and g == 4:
                    DMA(o_cv_s, cst_s.t[:, :, :], [cst_s.k], [], "o_cvs")
                    DMA(o_lr_s, hl_s.t[:, :], [hl_s.k], [], "o_lrs")
        zlist[:] = [PB[0], PB[1], PB[2]]
        ar.release(mB)

    if "M" in phases:
        mC = ar.mark()
        kT_p = sb("kT_p", [128, 4, 256], BF16); v_p = sb("v_p", [128, 2, 512], BF16)
        kT_s = sb("kT_s", [128, 4, 256], BF16); v_s = sb("v_s", [128, 2, 512], BF16)
        mM = ar.mark()
        phase_M()
        ar.release(mM)
    if "C" in phases:
        USE_RCP = False
        poolCx = mkpool(2, "tCx")
        poolCy = mkpool(4, "tCy")
        poolCd = mkpool(2, "tCd")
        qcTs = [sb("qcT%d" % i, [128, 4, 512], BF16) for i in range(2)]
        sgcs = [sb("sgc%d" % i, [128, 4, 512], BF16) for i in range(2)]
        ocgs = [sb("ocg%d" % i, [128, 4, 512], BF16) for i in range(2)]
        eT = [sb("eT%d" % i, [128, 2, 512], BF16) for i in range(2)]
        SC = [PB[4], PB[5]]; OC = PB[6]; DEN = PB[7]

        def XC(g, h):
            t0, T = groups[g]
            last = g == 4
            Z = zbank(); zmm("qc", h * 128, t0, T, Z)
            CP(qcTs[g % 2].t[:, h, :T], Z.t[:, :T], [Z.k], [qcTs[g % 2].k], eng="act")
            if last and h == 3:
                wfree("qc")
            yield
            Z = zbank(); zmm("gc", h * 128, t0, T, Z)
            tg = poolCx(); ACT(tg.t[:, :T], Z.t[:, :T], AF.Tanh, [Z.k], [tg.k], scale=0.5)
            STT(sgcs[g % 2].t[:, h, :T], tg.t[:, :T], 1.0, Z.t[:, :T], ALU.add, ALU.mult, [tg.k, Z.k], [sgcs[g % 2].k])
            if last and h == 3:
                wfree("gc")
            yield

        def YC(g, h):
            t0, T = groups[g]
            kT, vv = (kT_p, v_p) if g < 4 else (kT_s, v_s)
            qcT, sgc, ocg = qcTs[g % 2], sgcs[g % 2], ocgs[g % 2]
            e_ = eT[h % 2]
            for mj in range(2):
                s = SC[mj]
                MM(s.t[:, :T], kT.t[:, h, mj * 128:(mj + 1) * 128], qcT.t[:, h, :T], True, True, [kT.k, qcT.k], [s.k])
                ACT(e_.t[:, mj, :T], s.t[:, :T], AF.Exp, [s.k], [e_.k], scale=float(128 ** -0.5))
            yield
            for mj in range(2):
                MM(OC.t[:, :T], vv.t[:, mj, h * 128:(h + 1) * 128], e_.t[:, mj, :T], mj == 0, mj == 1, [vv.k, e_.k], [OC.k])
            for mj in range(2):
                MM(DEN.t[:, :T], ones.t[:, :], e_.t[:, mj, :T], mj == 0, mj == 1, [ones.k, e_.k], [DEN.k])
            rd = poolCy()
            if USE_RCP:
                P.add("dve", lambda e, rd=rd, T=T: e.reciprocal_approx_fast(out=rd.t[:, :T], in_=DEN.t[:, :T]),
                      reads=[DEN.k], writes=[rd.k])
            else:
                ACT(rd.t[:, :T], DEN.t[:, :T], AF.Ln, [DEN.k], [rd.k])
                ACT(rd.t[:, :T], rd.t[:, :T], AF.Exp, [rd.k], [rd.k], scale=-1.0)
            yield
            t_ = poolCy(); TT(t_.t[:, :T], OC.t[:, :T], rd.t[:, :T], ALU.mult, [OC.k, rd.k], [t_.k])
            TT(ocg.t[:, h, :T], t_.t[:, :T], sgc.t[:, h, :T], ALU.mult, [t_.k, sgc.k], [ocg.k], eng="pool")
            yield

        def downC(g, j):
            t0, T = groups[g]
            last = g == 4
            ocg = ocgs[g % 2]
            wcd, wcdk = W("cd", 4)
            b = zbank()
            for c in range(4):
                MM(b.t[:, :T], wcd[:, c, j * 128:(j + 1) * 128], ocg.t[:, c, :T], c == 0, c == 3, [wcdk, ocg.k], [b.k])
            jc = (j % 4) * 128
            bz = zbank(); zmm("zc%d" % (j // 4), jc, t0, T, bz)
            sgz = poolCd(); ACT(sgz.t[:, :T], bz.t[:, :T], AF.Tanh, [bz.k], [sgz.k], scale=0.5)
            merge(g, j, t0, T, b, sgz)
            if last and j % 4 == 3:
                wfree("zc%d" % (j // 4))
            if last and j == 7:
                wfree("cd")
            if j == 7:
                mrg_init[g] = True

        def DC(g, js):
            downC(g, js[0])
            yield
            yield
            downC(g, js[1])
            yield

        def runC(*gs):
            gs = [x for x in gs if x is not None]
            while gs:
                for x in list(gs):
                    try:
                        next(x)
                    except StopIteration:
                        gs.remove(x)

        for h in range(4):
            runC(XC(0, h))
        for g in range(6):
            for h in range(4):
                runC(XC(g + 1, h) if g + 1 < 5 else None,
                     YC(g, h) if g < 5 else None,
                     DC(g - 1, (2 * h, 2 * h + 1)) if g >= 1 else None)
            if g < 5:
                DBG("ocg%d" % g, ocgs[g % 2].t[:, :, :], [128, 4, 512], [ocgs[g % 2].k])
    if "M" in phases:
        ar.release(mC)

    if "D" in phases:
        mD = ar.mark()
        xt = [sb("xtD%d" % i, [128, D], F32) for i in range(2)]
        yb = [sb("ybD%d" % i, [128, D], F32) for i in range(2)]
        yo = [sb("yoD%d" % i, [128, D], F32) for i in range(2)]
        junk = sb("junkD", [128, D], BF16)
        gfin = sb("gfin", [128, D], F32)
        DMA(gfin.t[:, :], gf_d, [], [gfin.k], "c_gf")
        st8 = [sb("st8D%d" % i, [128, 8], F32) for i in range(2)]
        wo = [W("o0"), W("o1")]
        def D_a(i):
            nrows = 128 if i < 16 else DSEQ
            t0 = i * 128
            src = xp[t0:t0 + 128, :] if i < 16 else xsm[:, :]
            g = min(i // 4, 4)
            x, y_ = xt[i % 2], yb[i % 2]
            DMA(x.t[:nrows, :], src, [], [x.k], ("xtD", i % 2))
            for n in range(2):
                b = zbank()
                wt_, wtk = wo[n]
                for kc in range(KC):
                    MM(b.t[:nrows, :], mrg[:, kc, t0:t0 + nrows], wt_[:, kc, :], kc == 0, kc == KC - 1,
                       [mrgk[g], wtk], [b.k])
                STT(y_.t[:nrows, n * 512:(n + 1) * 512], b.t[:nrows, :], 0.25, x.t[:nrows, n * 512:(n + 1) * 512],
                    ALU.mult, ALU.add, [b.k, x.k], [y_.k])

        def D_b(i):
            nrows = 128 if i < 16 else DSEQ
            t0 = i * 128
            dst = y_p[t0:t0 + 128, :] if i < 16 else y_s[:, :]
            y_, yo_, s8 = yb[i % 2], yo[i % 2], st8[i % 2]
            ACT(junk.t[:nrows, :], y_.t[:nrows, :], AF.Square, [y_.k], [junk.k, s8.k], accum=s8.t[:nrows, 0:1])
            ACT(s8.t[:nrows, 1:2], s8.t[:nrows, 0:1], AF.Sqrt, [s8.k], [s8.k], bias=EPS, scale=1.0 / D)
            P.add("dve", lambda e, s8=s8, nrows=nrows: e.reciprocal(out=s8.t[:nrows, 2:3], in_=s8.t[:nrows, 1:2]),
                  reads=[s8.k], writes=[s8.k])
            STT(yo_.t[:nrows, :], y_.t[:nrows, :], s8.t[:nrows, 2:3], gfin.t[:nrows, :], ALU.mult, ALU.mult,
                [y_.k, s8.k, gfin.k], [yo_.k], eng="pool" if i % 2 else "dve")
            DMA(dst, yo_.t[:nrows, :], [yo_.k], [], ("yoD", i % 2), q="act")

        for i in range(18):
            if i < 17:
                D_a(i)
            if i > 0:
                D_b(i - 1)
        ar.release(mD)

    P.run()
    es.close()
    return nc, ar


def _bd(w):
    o = np.zeros((128, 8, 128), np.float32)
    for n in range(16):
        c, q = divmod(n, 2)
        o[q * 64:(q + 1) * 64, c, q * 64:(q + 1) * 64] = w[n]
    return o


def _pm(v):
    return np.ascontiguousarray(np.asarray(v, np.float32).reshape(-1, 128).T)


def make_in_maps(inp):
    f = lambda a: np.ascontiguousarray(np.asarray(a, dtype=np.float32))
    pvec = np.zeros((128, NPV), np.float32)
    pvec[:, PV_GMIX:PV_GMIX + 8] = _pm(inp["g_mix"][0])
    pvec[:, PV_GMEM:PV_GMEM + 8] = _pm(inp["g_mem"][0])
    pvec[:, PV_L0:PV_L0 + 4] = _pm(inp["lb_logits"][0])
    pvec[:, PV_L1:PV_L1 + 4] = _pm(inp["lb_logits"][1])
    pvec[:, PV_GA:PV_GA + 4] = _pm(inp["g_a_out"][0])
    for j in range(4):
        pvec[:, PV_WC + 8 * j:PV_WC + 8 * j + 8] = _pm(inp["w_conv"][0][j])
    pvec[:, PV_BC:PV_BC + 8] = _pm(inp["b_conv"][0])
    pvec[:, PV_BR:PV_BR + 8] = _pm(inp["b_lru_r"][0])
    pvec[:, PV_BI:PV_BI + 8] = _pm(inp["b_lru_i"][0])
    pvec[:, PV_LAM:PV_LAM + 8] = _pm(inp["lru_lambda"][0])
    gfin = np.ascontiguousarray(np.broadcast_to(f(inp["g_final"])[None, :], (128, D)))
    ident = np.eye(128, dtype=np.float32)
    s = np.arange(128)[:, None]
    t = np.arange(128)[None, :]
    amask = ((s // HCH == t // HCH) & (t >= s)).astype(np.float32)
    rmask = np.ones((128, 512), np.float32)
    rmask[:, ::HCH] = 0.0
    shared = {
        "w_in": f(inp["w_in"][0]), "w_ad": f(inp["w_a_down"][0]), "w_bd": f(inp["w_b_down"][0]),
        "w_cd": f(inp["w_c_down"][0]), "w_o": f(inp["w_out"][0]), "w_mk": f(inp["w_mem_k"][0]),
        "w_mv": f(inp["w_mem_v"][0]), "wr_bd": _bd(f(inp["w_lru_r"][0])), "wi_bd": _bd(f(inp["w_lru_i"][0])),
        "pv": pvec, "gfin": gfin, "ident": ident, "amask": amask, "rmask": rmask,
    }
    maps = []
    for b in range(8):
        m = dict(shared)
        m["xp"] = f(inp["x_prompt"][b])
        m["xs"] = f(inp["x_sample"][b])
        m["mem"] = f(inp["mem_prompt"][b])
        m["cmk"] = f(inp["cache_mem_k"][0, b]).reshape(NMEM, 512)
        m["cmv"] = f(inp["cache_mem_v"][0, b]).reshape(NMEM, 512)
        m["s_hg"] = f(inp["state_hgrn"][0, b])
        m["s_cv"] = np.ascontiguousarray(f(inp["state_conv"][0, b]).reshape(3, 8, 128).transpose(2, 1, 0))
        m["s_lr"] = _pm(inp["state_lru"][0, b])
        maps.append(m)
    return maps


_CACHE = {}


def kernel(**inputs):
    if "nc" not in _CACHE:
        _CACHE["nc"] = build()[0]
    nc = _CACHE["nc"]
    maps = make_in_maps(inputs)
    res = run_bass_kernel_spmd(nc, maps, core_ids=list(range(8)))
    R = res.results
    st = lambda k: np.stack([np.asarray(r[k], np.float32) for r in R])
    y_p = st("y_p")
    y_s = st("y_s")
    hg_p = st("o_hg_p")[None]
    cv_p = st("o_cv_p").transpose(0, 3, 2, 1).reshape(8, 3, 1024)[None]
    lr_p = st("o_lr_p").transpose(0, 2, 1).reshape(8, 1024)[None]
    mk = st("o_mk").reshape(8, NMEM, 4, 128)[None]
    mv = st("o_mv").reshape(8, NMEM, 4, 128)[None]
    hg_s = st("o_hg_s")[None]
    cv_s = st("o_cv_s").transpose(0, 3, 2, 1).reshape(8, 3, 1024)[None]
    lr_s = st("o_lr_s").transpose(0, 2, 1).reshape(8, 1024)[None]
    return (y_p, y_s, hg_p, np.ascontiguousarray(cv_p), np.ascontiguousarray(lr_p), mk, mv,
            hg_s, np.ascontiguousarray(cv_s), np.ascontiguousarray(lr_s))
```

```python
import numpy as np
from contextlib import ExitStack
import concourse.bass as bass
import concourse.mybir as mybir
from concourse.bass_utils import run_bass_kernel_spmd

F32 = mybir.dt.float32
BF16 = mybir.dt.bfloat16
AF = mybir.ActivationFunctionType
ALU = mybir.AluOpType

D = 1024
SEQ = 2048
DSEQ = 32
NMEM = 256
EPS = 1e-6
KC = 8
HCH = 128
TOK = SEQ + DSEQ

O_QA, O_FA, O_VA, O_GA, O_XB, O_GB, O_QC, O_GC, O_ZA, O_ZB, O_ZC = (
    0, 512, 1024, 1536, 2048, 3072, 4096, 4608, 5120, 6144, 7168)

PV_GMIX, PV_GMEM, PV_L0, PV_L1, PV_GA, PV_WC, PV_BC, PV_BR, PV_BI, PV_LAM = (
    0, 8, 16, 20, 24, 28, 60, 68, 76, 84)
NPV = 92


class Tok:
    __slots__ = ("name", "w", "rs", "rdma", "pre", "excl")

    def __init__(self, name):
        self.name = name
        self.w = None
        self.rs = {}
        self.rdma = []
        self.pre = []
        self.excl = False


class Op:
    __slots__ = ("idx", "eng", "fn", "deps", "is_dma", "slot", "cum", "need", "ticket", "nofuse", "vc", "waits")


ENGS = ("pe", "act", "dve", "pool", "sp")


WSTAT = {}


class Prog:
    def __init__(self, nc, es):
        self.nc = nc
        self.es = es
        self.ops = []
        self.eng_ops = {e: [] for e in ENGS}
        self.slot_cum = {}
        self.slot_hist = {}
        self.slot_sem = {}
        self.eng_sem = {}

    def add(self, eng, fn, reads=(), writes=(), slot=None, nofuse=False):
        op = Op()
        op.nofuse = nofuse
        op.idx = len(self.ops)
        op.eng = eng
        op.fn = fn
        op.is_dma = slot is not None
        op.slot = slot
        op.need = False
        op.ticket = None
        op.cum = None
        deps = {}

        def dep(o, kind):
            if o is None:
                return
            if kind == "raw" or o not in deps:
                deps[o] = kind

        for t in reads:
            dep(t.w, "raw")
            if t.excl:
                for e2, r in t.rs.items():
                    if e2 != eng:
                        dep(r, "raw")
        for t in writes:
            dep(t.w, "raw")
            for r in t.rs.values():
                dep(r, "war")
            for r in t.rdma:
                dep(r, "raw")
            for r in t.pre:
                dep(r, "raw")
        op.deps = deps
        for t in reads:
            if op.is_dma:
                t.rdma.append(op)
            else:
                t.rs[eng] = op
        for t in writes:
            t.w = op
            t.rs = {}
            t.rdma = []
        if op.is_dma:
            c = self.slot_cum.get(slot, 0) + 16
            self.slot_cum[slot] = c
            op.cum = c
            self.slot_hist.setdefault(slot, []).append((op.idx, c))
        self.ops.append(op)
        self.eng_ops[eng].append(op)
        return op

    def _resolve(self):
        for op in self.ops:
            for d, kind in op.deps.items():
                if d.is_dma:
                    continue
                if d.eng == op.eng and not op.is_dma:
                    if op.eng == "pe" or kind == "war":
                        continue
                d.need = True
        for e in ENGS:
            n = 0
            for op in self.eng_ops[e]:
                if op.need and not op.is_dma:
                    n += 1
                    op.ticket = n

    def _plan_waits(self):
        vcE = {e: {} for e in ENGS}
        for op in self.ops:
            K = vcE[op.eng]
            need = {}
            for d, kind in op.deps.items():
                if d.is_dma:
                    key = ("slot", d.slot)
                    val = self._slot_wait_value(d.slot, op.idx)
                else:
                    if d.eng == op.eng and not op.is_dma and (op.eng == "pe" or kind == "war"):
                        continue
                    key = ("eng", d.eng)
                    val = d.ticket
                if key not in need or val > need[key][0]:
                    need[key] = (val, d)
            out = []
            for key, (val, d) in sorted(need.items(), key=lambda kv: -kv[1][1].idx):
                if K.get(key, 0) >= val:
                    continue
                out.append((key, val))
                K[key] = val
                if d.vc is not None:
                    for k2, v2 in d.vc.items():
                        if v2 > K.get(k2, 0):
                            K[k2] = v2
            op.waits = out
            if op.is_dma:
                op.vc = dict(K)
                op.vc[("slot", op.slot)] = op.cum
            elif op.ticket is not None:
                op.vc = dict(K)
                op.vc[("eng", op.eng)] = op.ticket
            else:
                op.vc = None

    def _slot_wait_value(self, slot, before_idx):
        v = 0
        for idx, c in self.slot_hist[slot]:
            if idx < before_idx:
                v = c
            else:
                break
        return v

    def alloc_sems(self):
        nc = self.nc
        for e in ("pe", "act", "dve", "pool"):
            self.eng_sem[e] = self.es.enter_context(nc.semaphore("s_" + e))
        for i, s in enumerate(self.slot_hist):
            self.slot_sem[s] = self.es.enter_context(nc.semaphore("d%d" % i))

    def emit(self, e, eng):
        for op in self.eng_ops[e]:
            pend = []
            for key, val in op.waits:
                sem = self.slot_sem[key[1]] if key[0] == "slot" else self.eng_sem[key[1]]
                pend.append((sem, val))
                WSTAT[e] = WSTAT.get(e, 0) + 1
            fuse = None
            if pend and e in ("act", "dve", "pool", "pe") and not op.is_dma and not op.nofuse:
                fuse = pend.pop()
            for sem, val in pend:
                eng.wait_ge(sem, val)
            ins = op.fn(eng)
            if fuse is not None:
                ins._wait_ge(fuse[0], fuse[1])
            if op.is_dma:
                ins.then_inc(self.slot_sem[op.slot], 16)
            elif op.need:
                ins.then_inc(self.eng_sem[e], 1)
        if e == "sp":
            for s, c in self.slot_cum.items():
                eng.wait_ge(self.slot_sem[s], c)

    def run(self):
        self._resolve()
        self._plan_waits()
        self.alloc_sems()
        nc = self.nc
        with nc.Block() as block:
            @block.tensor
            def _(eng):
                self.emit("pe", eng)

            @block.scalar
            def _(eng):
                self.emit("act", eng)

            @block.vector
            def _(eng):
                self.emit("dve", eng)

            @block.gpsimd
            def _(eng):
                self.emit("pool", eng)

            @block.sync
            def _(eng):
                self.emit("sp", eng)


class Arena:
    def __init__(self, nc, base=16512, top=229344):
        self.nc = nc
        self.off = base
        self.top = top
        self.n = 0
        self.peak = base

    def alloc(self, name, shape, dtype):
        isz = 2 if dtype == BF16 else 4
        nb = isz
        for s in shape[1:]:
            nb *= s
        nb = (nb + 31) // 32 * 32
        off = self.off
        assert off + nb <= self.top, ("SBUF overflow", name, off + nb - self.top)
        self.off += nb
        self.peak = max(self.peak, self.off)
        self.n += 1
        return self.nc.alloc_sbuf_tensor_at("%s_%d" % (name, self.n), list(shape), dtype, offset=off)

    def mark(self):
        return self.off

    def release(self, m):
        self.off = m


class Buf:
    def __init__(self, t, name):
        self.t = t
        self.k = Tok(name)


def build(phases="0ABMCD", dbg=()):
    nc = bass.Bass("TRN2", target_bir_lowering=False)
    es = ExitStack()
    P = Prog(nc, es)
    ar = Arena(nc)

    def din(name, shape):
        return nc.dram_tensor(name, list(shape), F32, kind="ExternalInput").ap()

    def dout(name, shape):
        return nc.dram_tensor(name, list(shape), F32, kind="ExternalOutput").ap()

    xp = din("xp", [SEQ, D]); xsm = din("xs", [DSEQ, D]); memd = din("mem", [NMEM, D])
    cmk = din("cmk", [NMEM, 512]); cmv = din("cmv", [NMEM, 512])
    s_hg = din("s_hg", [4, 128, 128]); s_cv = din("s_cv", [128, 8, 3]); s_lr = din("s_lr", [128, 8])
    w_in = din("w_in", [D, 8192]); w_ad = din("w_ad", [512, D]); w_bd = din("w_bd", [D, D])
    w_cd = din("w_cd", [512, D]); w_o = din("w_o", [D, D]); w_mk = din("w_mk", [D, 512]); w_mv = din("w_mv", [D, 512])
    wr_d = din("wr_bd", [128, 8, 128]); wi_d = din("wi_bd", [128, 8, 128])
    pv_d = din("pv", [128, NPV]); gf_d = din("gfin", [128, D])
    id_d = din("ident", [128, 128]); mk_d = din("amask", [128, 128]); rm_d = din("rmask", [128, 512])

    y_p = dout("y_p", [SEQ, D]); y_s = dout("y_s", [DSEQ, D])
    o_hg_p = dout("o_hg_p", [4, 128, 128]); o_cv_p = dout("o_cv_p", [128, 8, 3]); o_lr_p = dout("o_lr_p", [128, 8])
    o_mk = dout("o_mk", [NMEM, 512]); o_mv = dout("o_mv", [NMEM, 512])
    o_hg_s = dout("o_hg_s", [4, 128, 128]); o_cv_s = dout("o_cv_s", [128, 8, 3]); o_lr_s = dout("o_lr_s", [128, 8])

    PB = [Buf(nc.alloc_psum_tensor("pb%d" % i, [128, 512], F32), "pb%d" % i) for i in range(8)]
    for b_ in PB:
        b_.k.excl = True
    zrot = [0]
    zlist = [PB[0], PB[1], PB[2]]

    def zbank():
        b = zlist[zrot[0] % len(zlist)]
        zrot[0] += 1
        return b
    TB = PB[3]

    all_bufs = []

    def sb(name, shape, dtype):
        o0 = ar.off
        b = Buf(ar.alloc(name, shape, dtype), name)
        o1 = ar.off
        for (p0, p1, ob) in all_bufs:
            if p0 < o1 and o0 < p1:
                k = ob.k
                for r in [k.w] + list(k.rs.values()) + k.rdma + k.pre:
                    if r is not None and r not in b.k.pre:
                        b.k.pre.append(r)
        all_bufs.append((o0, o1, b))
        return b

    hT = ar.alloc("hT", [128, KC, TOK], BF16)
    hTk = [Tok("hT%d" % i) for i in range(17)]
    mrg = ar.alloc("mrg", [128, KC, TOK], BF16)
    mrgk = [Tok("mrg%d" % i) for i in range(5)]
    ident = sb("ident", [128, 128], BF16)
    ones = sb("ones", [128, 128], BF16)
    amask = sb("amask", [128, 128], F32)
    rmask = sb("rmask", [128, 512], F32)
    pv = sb("pv", [128, NPV], F32)
    dv = sb("dv", [128, 80], F32)
    DEFER_A3 = ("A" in phases) and ("B" in phases)
    oagT_keep = sb("oagT1", [128, 4, 512], BF16) if "A" in phases else None
    DV_LB, DV_OML, DV_CL, DV_T, DV_HBR, DV_HBI, DV_HCL, DV_HOML, DV_LBH, DV_NHOML = 0, 4, 8, 16, 32, 40, 48, 56, 60, 64

    NR = 9
    ring = [sb("ring%d" % i, [128, 4096], BF16) for i in range(NR)]

    def pvc(col):
        return pv.t[:, col:col + 1]

    def dvc(col):
        return dv.t[:, col:col + 1]

    def wv(ap, kc):
        return ap.rearrange("(c p) n -> p c n", p=128)

    units = {
        "mk": wv(w_mk, 8), "mv": wv(w_mv, 8),
        "fa": wv(w_in[:, O_FA:O_FA + 512], 8), "qa": wv(w_in[:, O_QA:O_QA + 512], 8),
        "ga": wv(w_in[:, O_GA:O_GA + 512], 8), "va": wv(w_in[:, O_VA:O_VA + 512], 8),
        "za0": wv(w_in[:, O_ZA:O_ZA + 512], 8), "za1": wv(w_in[:, O_ZA + 512:O_ZA + 1024], 8),
        "ad": wv(w_ad, 4),
        "xb0": wv(w_in[:, O_XB:O_XB + 512], 8), "xb1": wv(w_in[:, O_XB + 512:O_XB + 1024], 8),
        "gb0": wv(w_in[:, O_GB:O_GB + 512], 8), "gb1": wv(w_in[:, O_GB + 512:O_GB + 1024], 8),
        "zb0": wv(w_in[:, O_ZB:O_ZB + 512], 8), "zb1": wv(w_in[:, O_ZB + 512:O_ZB + 1024], 8),
        "bd0": wv(w_bd[:, 0:512], 8), "bd1": wv(w_bd[:, 512:1024], 8),
        "qc": wv(w_in[:, O_QC:O_QC + 512], 8), "gc": wv(w_in[:, O_GC:O_GC + 512], 8),
        "zc0": wv(w_in[:, O_ZC:O_ZC + 512], 8), "zc1": wv(w_in[:, O_ZC + 512:O_ZC + 1024], 8),
        "cd": wv(w_cd, 4),
        "o0": wv(w_o[:, 0:512], 8), "o1": wv(w_o[:, 512:1024], 8),
    }
    plan = []
    if "A" in phases:
        plan += ["fa", "qa", "ga", "va", "za0", "za1", "ad"]
    if "B" in phases:
        plan += ["xb0", "xb1", "gb0", "gb1", "zb0", "zb1", "bd0", "bd1"]
    if "M" in phases:
        plan += ["mk", "mv"]
    if "C" in phases:
        plan += ["qc", "gc", "zc0", "zc1", "cd"]
    if "D" in phases:
        plan += ["o0", "o1"]
    free_slots = list(range(NR))
    where = {}

    def pump(limit=None, gate=()):
        n_ = 0
        while plan and free_slots and (limit is None or n_ < limit):
            n_ += 1
            u = plan.pop(0)
            s = free_slots.pop(0)
            where[u] = s
            src = units[u]
            a = src.shape[1]
            dst = ring[s].t[:, :].rearrange("p (a b) -> p a b", a=a)
            P.add("pool", lambda e, dst=dst, src=src: e.dma_start(out=dst, in_=src),
                  reads=list(gate), writes=[ring[s].k], slot=("ring", s))

    def W(u, a=8):
        s = where[u]
        return ring[s].t[:, :].rearrange("p (a b) -> p a b", a=a), ring[s].k

    def wfree(u):
        free_slots.append(where.pop(u))
        pump()

    def ACT(out, in_, func, reads, writes, bias=0.0, scale=1.0, accum=None):
        if accum is None:
            P.add("act", lambda e: e.activation(out=out, in_=in_, func=func, bias=bias, scale=scale),
                  reads=reads, writes=writes)
        else:
            P.add("act", lambda e: e.activation(out=out, in_=in_, func=func, bias=bias, scale=scale,
                                               accum_out=accum), reads=reads, writes=writes, nofuse=True)

    def TS(out, in0, s1, s2, op0, op1, reads, writes, eng="dve"):
        if s2 is None:
            P.add(eng, lambda e: e.tensor_scalar(out=out, in0=in0, scalar1=s1, scalar2=None, op0=op0),
                  reads=reads, writes=writes)
        else:
            P.add(eng, lambda e: e.tensor_scalar(out=out, in0=in0, scalar1=s1, scalar2=s2, op0=op0, op1=op1),
                  reads=reads, writes=writes)

    def TT(out, in0, in1, op, reads, writes, eng="dve"):
        P.add(eng, lambda e: e.tensor_tensor(out=out, in0=in0, in1=in1, op=op), reads=reads, writes=writes)

    def STT(out, in0, sc, in1, op0, op1, reads, writes, eng="dve"):
        if eng == "pool":
            P.add("pool", lambda e: e.tensor_scalar(out=out, in0=in0, scalar1=sc, scalar2=1.0, op0=op0, op1=ALU.mult),
                  reads=reads, writes=writes)
            P.add("pool", lambda e: e.tensor_tensor(out=out, in0=out, in1=in1, op=op1), reads=reads + writes, writes=writes)
            return
        P.add("dve", lambda e: e.scalar_tensor_tensor(out=out, in0=in0, scalar=sc, in1=in1, op0=op0, op1=op1),
              reads=reads, writes=writes)

    def CP(out, in_, reads, writes, eng="dve"):
        if eng == "act":
            P.add(eng, lambda e: e.activation(out=out, in_=in_, func=AF.Copy), reads=reads, writes=writes)
        else:
            P.add(eng, lambda e: e.tensor_copy(out=out, in_=in_), reads=reads, writes=writes)

    def MM(out, lhsT, rhs, start, stop, reads, writes):
        P.add("pe", lambda e: e.matmul(out, lhsT, rhs, start=start, stop=stop, skip_group_check=True),
              reads=reads, writes=writes)

    def TR(out, in_, idn, reads, writes):
        P.add("pe", lambda e: e.transpose(out, in_, idn), reads=reads, writes=writes)

    def DMA(out, in_, reads, writes, slot, q="sp"):
        P.add(q, lambda e: e.dma_start(out=out, in_=in_), reads=reads, writes=writes, slot=slot)

    def hTtoks(t0, T):
        return [hTk[i] for i in range(t0 // 128, (t0 + T + 127) // 128)]

    def zmm(u, col0, t0, T, bank, ncol=128):
        wt, wk = W(u)
        rd = [wk] + hTtoks(t0, T)
        for kc in range(KC):
            MM(bank.t[:ncol, :T], wt[:, kc, col0:col0 + ncol], hT[:, kc, t0:t0 + T],
               kc == 0, kc == KC - 1, rd, [bank.k])

    dbg_list = []

    def DBG(name, ap, shape, toks, dtype=BF16):
        if name in dbg:
            d = nc.dram_tensor("dbg_" + name, list(shape), dtype, kind="ExternalOutput").ap()
            DMA(d, ap, toks, [], ("dbg", name))

    DMA(pv.t[:, :], pv_d, [], [pv.k], "c_pv")
    DMA(amask.t[:, :], mk_d, [], [amask.k], "c_am")
    DMA(rmask.t[:, :], rm_d, [], [rmask.k], "c_rm")
    P.add("pool", lambda e: e.dma_start(out=ident.t[:, :], in_=id_d), writes=[ident.k], slot="c_id")
    P.add("dve", lambda e: e.memset(ones.t[:, :], 1.0), writes=[ones.k])
    pump(limit=3)
    TT(dv.t[:, DV_T:DV_T + 4], pv.t[:, PV_L0:PV_L0 + 4], pv.t[:, PV_L1:PV_L1 + 4], ALU.subtract, [pv.k], [dv.k])
    ACT(dv.t[:, DV_LB:DV_LB + 4], dv.t[:, DV_T:DV_T + 4], AF.Sigmoid, [dv.k], [dv.k])
    TS(dv.t[:, DV_OML:DV_OML + 4], dv.t[:, DV_LB:DV_LB + 4], -1.0, 1.0, ALU.mult, ALU.add, [dv.k], [dv.k])
    TS(dv.t[:, DV_HOML:DV_HOML + 4], dv.t[:, DV_LB:DV_LB + 4], -0.5, 0.5, ALU.mult, ALU.add, [dv.k], [dv.k])
    TS(dv.t[:, DV_LBH:DV_LBH + 4], dv.t[:, DV_LB:DV_LB + 4], 0.5, 0.5, ALU.mult, ALU.add, [dv.k], [dv.k])
    TS(dv.t[:, DV_NHOML:DV_NHOML + 4], dv.t[:, DV_LB:DV_LB + 4], 0.5, -0.5, ALU.mult, ALU.add, [dv.k], [dv.k])
    ACT(dv.t[:, DV_T:DV_T + 8], pv.t[:, PV_LAM:PV_LAM + 8], AF.Exp, [pv.k, dv.k], [dv.k], scale=-1.0)
    ACT(dv.t[:, DV_T + 8:DV_T + 16], dv.t[:, DV_T:DV_T + 8], AF.Ln, [dv.k], [dv.k], bias=1.0)
    TS(dv.t[:, DV_CL:DV_CL + 8], dv.t[:, DV_T + 8:DV_T + 16], -8.0, None, ALU.mult, None, [dv.k], [dv.k])
    TS(dv.t[:, DV_HCL:DV_HCL + 8], dv.t[:, DV_T + 8:DV_T + 16], -4.0, None, ALU.mult, None, [dv.k], [dv.k])
    TS(dv.t[:, DV_HBR:DV_HBR + 8], pv.t[:, PV_BR:PV_BR + 8], 0.5, None, ALU.mult, None, [pv.k, dv.k], [dv.k])
    TS(dv.t[:, DV_HBI:DV_HBI + 8], pv.t[:, PV_BI:PV_BI + 8], 0.5, None, ALU.mult, None, [pv.k, dv.k], [dv.k])

    groups = [(0, 512), (512, 512), (1024, 512), (1536, 512), (2048, 32)]

    m0 = ar.mark()
    NB = {}

    def alloc_norm(tag, n=2):
        NB["xt"] = [sb("xt%s%d" % (tag, i), [128, D], F32) for i in range(n)]
        NB["junk"] = sb("junk" + tag, [128, D], BF16)
        NB["xn"] = [sb("xn%s%d" % (tag, i), [128, D], BF16) for i in range(n)]
        NB["st8"] = [sb("st8%s_%d" % (tag, i), [128, 8], F32) for i in range(n)]
        NB["tag"] = tag

    alloc_norm("0", 4)
    TBv = TB.t[:, :].bitcast(BF16).rearrange("p (a b) -> p a b", a=8)
    cnt = [0]

    def norm_a(src, nrows):
        i = cnt[0] % len(NB["xt"])
        cnt[0] += 1
        x, xb_, s8, junk = NB["xt"][i], NB["xn"][i], NB["st8"][i], NB["junk"]
        DMA(x.t[:nrows, :], src, [], [x.k], ("xt" + NB["tag"], i))
        ACT(junk.t[:nrows, :], x.t[:nrows, :], AF.Square, [x.k], [junk.k, s8.k], accum=s8.t[:nrows, 0:1])
        ACT(s8.t[:nrows, 1:2], s8.t[:nrows, 0:1], AF.Sqrt, [s8.k], [s8.k], bias=EPS, scale=1.0 / D)
        P.add("dve", lambda e: e.reciprocal(out=s8.t[:nrows, 2:3], in_=s8.t[:nrows, 1:2]), reads=[s8.k], writes=[s8.k])
        ACT(xb_.t[:nrows, :], x.t[:nrows, :], AF.Copy, [x.k, s8.k], [xb_.k], scale=s8.t[:nrows, 2:3])
        return xb_

    def norm_b(xb_, nrows, gcol, dstT, dtok):
        for kc in range(KC):
            TR(TBv[:, kc, 0:nrows], xb_.t[:nrows, kc * 128:(kc + 1) * 128], ident.t[:nrows, :nrows],
               [xb_.k, ident.k], [TB.k])
        gb_ = pv.t[:, gcol:gcol + 8].unsqueeze(2).to_broadcast([128, 8, nrows])
        TT(dstT, TBv[:, :, 0:nrows], gb_, ALU.mult, [TB.k, pv.k], [dtok])

    def norm_T(src, nrows, gcol, dstT, dtok):
        norm_b(norm_a(src, nrows), nrows, gcol, dstT, dtok)

    def norm_many(items):
        prev = None
        for it in items:
            xb_ = norm_a(it[0], it[1])
            if prev is not None:
                norm_b(*prev)
            prev = (xb_, it[1], it[2], it[3], it[4])
        norm_b(*prev)

    norm_many([(xp[i * 128:(i + 1) * 128, :], 128, PV_GMIX, hT[:, :, i * 128:(i + 1) * 128], hTk[i]) for i in range(16)]
              + [(xsm[:, :], DSEQ, PV_GMIX, hT[:, :, SEQ:SEQ + DSEQ], hTk[16])])
    DBG("hT", hT[:, :, :], [128, KC, TOK], hTk)
    pump(gate=[hTk[16]])

    ar.release(m0)

    MKV = {}
    def phase_M():
        alloc_norm("M")
        hmT = sb("hmT", [128, KC, NMEM], BF16)
        norm_many([(memd[j * 128:(j + 1) * 128, :], 128, PV_GMEM, hmT.t[:, :, j * 128:(j + 1) * 128], hmT.k)
                   for j in range(2)])
        wk_, wkk = W("mk")
        wv_, wvk = W("mv")
        for h in range(4):
            b = zbank()
            for kc in range(KC):
                MM(b.t[:, :NMEM], wk_[:, kc, h * 128:(h + 1) * 128], hmT.t[:, kc, :], kc == 0, kc == KC - 1,
                   [wkk, hmT.k], [b.k])
            CP(kT_p.t[:, h, :], b.t[:, :NMEM], [b.k], [kT_p.k], eng="act")
        stg = [sb("stg%d" % i, [128, 512], F32) for i in range(2)]
        n = 0
        for (wt_, wtk, od, isv) in ((wk_, wkk, o_mk, False), (wv_, wvk, o_mv, True)):
            for j in range(2):
                b = zbank()
                for kc in range(KC):
                    MM(b.t[:, :], hmT.t[:, kc, j * 128:(j + 1) * 128], wt_[:, kc, :], kc == 0, kc == KC - 1,
                       [wtk, hmT.k], [b.k])
                s = stg[n % 2]
                n += 1
                CP(s.t[:, :], b.t[:, :], [b.k], [s.k], eng="act")
                if isv:
                    CP(v_p.t[:, j, :], b.t[:, :], [b.k], [v_p.k], eng="dve")
                DMA(od[j * 128:(j + 1) * 128, :], s.t[:, :], [s.k], [], ("stg", n % 2))
        wfree("mk")
        wfree("mv")
        cmkb = sb("cmkb", [128, 2, 512], BF16)
        P.add("pool", lambda e: e.dma_start(out=cmkb.t[:, :, :], in_=cmk.rearrange("(j p) n -> p j n", p=128)),
              writes=[cmkb.k], slot="c_cmk")
        P.add("pool", lambda e: e.dma_start(out=v_s.t[:, :, :], in_=cmv.rearrange("(j p) n -> p j n", p=128)),
              writes=[v_s.k], slot="c_cmv")
        for h in range(4):
            for j in range(2):
                TR(TBv[:, h * 2 + j, :], cmkb.t[:, j, h * 128:(h + 1) * 128], ident.t[:, :], [cmkb.k, ident.k], [TB.k])
        CP(kT_s.t[:, :, :], TB.t[:, :].bitcast(BF16).rearrange("p (a b) -> p a b", a=4), [TB.k], [kT_s.k])
        DBG("kT_p", kT_p.t[:, :, :], [128, 4, 256], [kT_p.k])
        DBG("kT_s", kT_s.t[:, :, :], [128, 4, 256], [kT_s.k])


    if not any(p in phases for p in "ABC"):
        for g in range(5):
            t0, T = groups[g]
            P.add("dve", lambda e, t0=t0, T=T: e.memset(mrg[:, :, t0:t0 + T], 0.0), writes=[mrgk[g]])


    def mkpool(n, name):
        bufs = [sb("%s%d" % (name, i), [128, 512], F32) for i in range(n)]
        c = [0]

        def get():
            b = bufs[c[0] % n]
            c[0] += 1
            return b
        return get

    mrg_init = [False] * 5

    def merge(g, j, t0, T, pbank, tz):
        dst = mrg[:, j, t0:t0 + T]
        if not mrg_init[g]:
            STT(dst, tz.t[:, :T], 1.0, pbank.t[:, :T], ALU.add, ALU.mult, [pbank.k, tz.k], [mrgk[g]])
        else:
            STT(tz.t[:, :T], tz.t[:, :T], 1.0, pbank.t[:, :T], ALU.add, ALU.mult, [pbank.k, tz.k], [tz.k])
            TT(dst, dst, tz.t[:, :T], ALU.add, [tz.k, mrgk[g]], [mrgk[g]], eng="pool")

    if "A" in phases:
        mA = ar.mark()
        ATT, OB, KVB, XB = PB[4], PB[5], PB[6], PB[7]
        poolXa = mkpool(6, "tAx")
        poolYa = mkpool(2, "tAy")
        SETS = []
        for s_ in range(2):
            SETS.append(dict(
                qgT=sb("qgT%d" % s_, [128, 4, 512], BF16), kgT=sb("kgT%d" % s_, [128, 4, 512], BF16),
                sga=sb("sga%d" % s_, [128, 4, 512], BF16), dec=sb("dec%d" % s_, [128, 4, 8], F32)))
        kd_tm1 = sb("kd_tm", [128, 4, 512], BF16); v_tm1 = sb("v_tm", [128, 4, 512], BF16)
        for s_ in range(2):
            SETS[s_]["kd_tm"] = kd_tm1
            SETS[s_]["v_tm"] = v_tm1
        kdT = sb("kdT", [128, 4, 512], BF16)
        attm = [sb("attm0", [128, 4, 128], BF16)] * 2
        S_p = sb("S_p", [128, 4, 128], F32); S_s = S_p
        Sb = [sb("Sb%d" % i, [128, 4, 128], BF16) for i in range(2)]
        sq = sb("sqA", [128, 512], BF16)
        oagTs = [sb("oagT0", [128, 4, 512], BF16), oagT_keep]
        P.add("dve", lambda e: e.memset(S_p.t[:, :, :], 0.0), writes=[S_p.k])
        sbi = [0]

        def geo(g):
            t0, T = groups[g]
            TSZ = min(128, T); CS = min(HCH, T)
            return t0, T, TSZ, T // TSZ, CS, TSZ // CS, T // CS

        def XA(g, h):
            t0, T, TSZ, NT, CS, CPT, NCH = geo(g)
            st = SETS[g % 2]
            last = g == 4
            tmp = poolXa
            bf_ = zbank(); zmm("fa", h * 128, t0, T, bf_)
            A_ = tmp(); ACT(A_.t[:, :T], bf_.t[:, :T], AF.Tanh, [bf_.k], [A_.k], scale=0.5)
            yield
            bq_ = zbank(); zmm("qa", h * 128, t0, T, bq_)
            Q_ = tmp(); ACT(Q_.t[:, :T], bq_.t[:, :T], AF.Tanh, [bq_.k], [Q_.k], scale=0.5)
            STT(Q_.t[:, :T], Q_.t[:, :T], 1.0, bq_.t[:, :T], ALU.add, ALU.mult, [Q_.k, bq_.k], [Q_.k])
            yield
            bg_ = zbank(); zmm("ga", h * 128, t0, T, bg_)
            G_ = tmp(); ACT(G_.t[:, :T], bg_.t[:, :T], AF.Tanh, [bg_.k], [G_.k], scale=0.5)
            STT(st["sga"].t[:, h, :T], G_.t[:, :T], 1.0, bg_.t[:, :T], ALU.add, ALU.mult, [G_.k, bg_.k], [st["sga"].k])
            if last and h == 3:
                wfree("fa"); wfree("qa"); wfree("ga")
            yield
            C_ = tmp(); ACT(C_.t[:, :T], A_.t[:, :T], AF.Ln, [A_.k, dv.k], [C_.k], bias=dvc(DV_LBH + h), scale=dvc(DV_HOML + h))
            K_ = G_
            TS(K_.t[:, :T], A_.t[:, :T], dvc(DV_NHOML + h), dvc(DV_HOML + h), ALU.mult, ALU.add, [A_.k, G_.k, dv.k], [K_.k])
            yield
            D_ = tmp()
            P.add("dve", lambda e, D_=D_, C_=C_, T=T: e.tensor_tensor_scan(
                out=D_.t[:, :T], data0=rmask.t[:, :T], data1=C_.t[:, :T], initial=0.0, op0=ALU.mult, op1=ALU.add),
                reads=[rmask.k, C_.k], writes=[D_.k])
            yield
            E_ = A_
            ACT(E_.t[:, :T], D_.t[:, :T], AF.Exp, [D_.k, A_.k], [E_.k])
            F_ = tmp(); ACT(F_.t[:, :T], D_.t[:, :T], AF.Exp, [D_.k], [F_.k], scale=-1.0)
            yield
            e3 = E_.t[:, :T].rearrange("p (c s) -> p c s", s=CS)
            CP(st["dec"].t[:, h, 0:NCH].unsqueeze(2), e3[:, :, CS - 1:CS], [E_.k], [st["dec"].k])
            TT(F_.t[:, :T], K_.t[:, :T], F_.t[:, :T], ALU.mult, [K_.k, F_.k], [F_.k], eng="pool")
            TT(st["qgT"].t[:, h, :T], Q_.t[:, :T], E_.t[:, :T], ALU.mult, [Q_.k, E_.k], [st["qgT"].k], eng="pool")
            yield
            CP(st["kgT"].t[:, h, :T], F_.t[:, :T], [F_.k], [st["kgT"].k], eng="act")
            f3 = F_.t[:, :T].rearrange("p (c s) -> p c s", s=CS)
            TT(kdT.t[:, h, :T].rearrange("p (c s) -> p c s", s=CS), f3, e3[:, :, CS - 1:CS].to_broadcast([128, NCH, CS]),
               ALU.mult, [F_.k, E_.k], [kdT.k], eng="pool")
            yield
            if last and h == 3:
                pass

        def XA_fin(g):
            t0, T, TSZ, NT, CS, CPT, NCH = geo(g)
            st = SETS[g % 2]
            last = g == 4
            wva, wvak = W("va")
            for h in range(NT):
                b = zbank()
                c0 = t0 + h * TSZ
                for kc in range(KC):
                    MM(b.t[:TSZ, :], hT[:, kc, c0:c0 + TSZ], wva[:, kc, :], kc == 0, kc == KC - 1,
                       [wvak] + hTtoks(c0, TSZ), [b.k])
                CP(st["v_tm"].t[:TSZ, h, :], b.t[:TSZ, :], [b.k], [st["v_tm"].k], eng="act")
            if last:
                wfree("va")
            for i in range(NT):
                for h in range(4):
                    TR(TBv[:TSZ, h, :], kdT.t[:, h, i * TSZ:(i + 1) * TSZ], ident.t[:, :], [kdT.k, ident.k], [TB.k])
                CP(st["kd_tm"].t[:TSZ, i, :].rearrange("p (a b) -> p a b", a=4), TBv[:TSZ, 0:4, :], [TB.k], [st["kd_tm"].k])

        def YA(g, i):
            t0, T, TSZ, NT, CS, CPT, NCH = geo(g)
            st = SETS[g % 2]
            qgT, kgT, kd_tm, v_tm, sga, dec = st["qgT"], st["kgT"], st["kd_tm"], st["v_tm"], st["sga"], st["dec"]
            S = S_p if g < 4 else S_s
            oagT = oagTs[g % 2]
            if i == 0 and (g == 0 or g == 4):
                CP(Sb[sbi[0]].t[:, :, :], S.t[:, :, :], [S.k], [Sb[sbi[0]].k])
            am = attm[i % 2]
            c0 = i * TSZ
            for h in range(4):
                MM(ATT.t[:TSZ, h * 128:h * 128 + TSZ], kgT.t[:, h, c0:c0 + TSZ], qgT.t[:, h, c0:c0 + TSZ], True, True,
                   [kgT.k, qgT.k], [ATT.k])
            for h in range(4):
                TT(am.t[:TSZ, h, :TSZ], ATT.t[:TSZ, h * 128:h * 128 + TSZ], amask.t[:TSZ, :TSZ], ALU.mult,
                   [ATT.k, amask.k], [am.k])
            yield
            yield
            for h in range(4):
                MM(OB.t[:, h * TSZ:(h + 1) * TSZ], v_tm.t[:TSZ, i, h * 128:(h + 1) * 128], am.t[:TSZ, h, :TSZ],
                   h == 0, False, [v_tm.k, am.k], [OB.k])
            for c in range(CPT):
                gc = i * CPT + c
                cur = Sb[sbi[0]]
                for h in range(4):
                    MM(OB.t[:, h * TSZ + c * CS:h * TSZ + (c + 1) * CS], cur.t[:, h, :],
                       qgT.t[:, h, c0 + c * CS:c0 + (c + 1) * CS], False, (c == CPT - 1 and h == 3),
                       [cur.k, qgT.k], [OB.k])
                for h in range(4):
                    MM(KVB.t[:, h * 128:(h + 1) * 128], kd_tm.t[c * CS:(c + 1) * CS, i, h * 128:(h + 1) * 128],
                       v_tm.t[c * CS:(c + 1) * CS, i, h * 128:(h + 1) * 128], True, True,
                       [kd_tm.k, v_tm.k], [KVB.k])
                for h in range(4):
                    STT(S.t[:, h, :], S.t[:, h, :], dec.t[:, h, gc:gc + 1],
                        KVB.t[:, h * 128:(h + 1) * 128], ALU.mult, ALU.add, [S.k, dec.k, KVB.k], [S.k])
                sbi[0] ^= 1
                CP(Sb[sbi[0]].t[:, :, :], S.t[:, :, :], [S.k], [Sb[sbi[0]].k])
                yield
                yield
            W4 = 4 * TSZ
            ACT(sq.t[:, :W4], OB.t[:, :W4], AF.Square, [OB.k], [sq.k])
            yield
            MM(XB.t[:, :W4], ones.t[:, :], sq.t[:, :W4], True, True, [ones.k, sq.k], [XB.k])
            l_ = poolYa(); ACT(l_.t[:, :W4], XB.t[:, :W4], AF.Ln, [XB.k], [l_.k], bias=4.0 * EPS, scale=1.0 / 128)
            ACT(l_.t[:, :W4], l_.t[:, :W4], AF.Exp, [l_.k], [l_.k], scale=-0.5)
            yield
            for h in range(4):
                STT(OB.t[:, h * TSZ:(h + 1) * TSZ], OB.t[:, h * TSZ:(h + 1) * TSZ], pvc(PV_GA + h),
                    l_.t[:, h * TSZ:(h + 1) * TSZ], ALU.mult, ALU.mult, [OB.k, pv.k, l_.k], [OB.k])
            TT(oagT.t[:, :, c0:c0 + TSZ], OB.t[:, :W4].rearrange("p (a b) -> p a b", a=4), sga.t[:, :, c0:c0 + TSZ],
               ALU.mult, [OB.k, sga.k], [oagT.k])

        def downA(g, j, pool=None):
            t0, T = groups[g]
            last = g == (3 if DEFER_A3 else 4)
            wad, wadk = W("ad", 4)
            oagT = oagTs[g % 2]
            b = zbank()
            for c in range(4):
                MM(b.t[:, :T], wad[:, c, j * 128:(j + 1) * 128], oagT.t[:, c, :T], c == 0, c == 3, [wadk, oagT.k], [b.k])
            bz = zbank(); zmm("za%d" % (j // 4), (j % 4) * 128, t0, T, bz)
            sgz = (pool or poolYa)(); ACT(sgz.t[:, :T], bz.t[:, :T], AF.Tanh, [bz.k], [sgz.k], scale=0.5)
            merge(g, j, t0, T, b, sgz)
            if last and j == 3:
                wfree("za0")
            if last and j == 7:
                wfree("za1"); wfree("ad")
            if j == 7:
                mrg_init[g] = True

        def run2(*gs):
            gs = [x for x in gs if x is not None]
            while gs:
                for x in list(gs):
                    try:
                        next(x)
                    except StopIteration:
                        gs.remove(x)

        def Y2(g, p):
            NT = geo(g)[3]
            for i in (2 * p, 2 * p + 1):
                if i < NT:
                    yield from YA(g, i)

        def DA(g, js):
            yield
            downA(g, js[0])
            yield
            yield
            yield
            downA(g, js[1])
            yield

        for h in range(4):
            run2(XA(0, h))
        XA_fin(0)
        for g in range(6):
            NT = geo(g)[3] if g < 5 else 0
            for i in range(4):
                run2(XA(g + 1, i) if g + 1 < 5 else None,
                     YA(g, i) if i < NT else None,
                     DA(g - 1, (2 * i, 2 * i + 1)) if (g >= 1 and not (DEFER_A3 and g - 1 == 3)) else None)
            if g + 1 < 5:
                XA_fin(g + 1)
            if g == 3:
                DMA(o_hg_p.rearrange("h k v -> k h v"), S_p.t[:, :, :], [S_p.k], [], "o_hgp")
                DMA(S_p.t[:, :, :], s_hg.rearrange("h k v -> k h v"), [S_p.k], [S_p.k], "c_shg")
            if g == 4:
                DMA(o_hg_s.rearrange("h k v -> k h v"), S_s.t[:, :, :], [S_s.k], [], "o_hgs")
        ar.release(mA)
    else:
        pass

    if "B" in phases:
        mB = ar.mark()
        wr = sb("wr", [128, 8, 128], BF16)
        wi = sb("wi", [128, 8, 128], BF16)
        P.add("pool", lambda e: e.dma_start(out=wr.t[:, :, :], in_=wr_d), writes=[wr.k], slot="c_wr")
        P.add("pool", lambda e: e.dma_start(out=wi.t[:, :, :], in_=wi_d), writes=[wi.k], slot="c_wi")
        def mkpool2(n, name, dtype):
            bufs = [sb("%s%d" % (name, i), [128, 512], dtype) for i in range(n)]
            c = [0]

            def get():
                b_ = bufs[c[0] % n]
                c[0] += 1
                return b_
            return get
        poolX = mkpool2(4, "pX", F32)
        poolG = mkpool2(4, "pG", BF16)
        poolY = mkpool2(8, "pY", F32)
        poolD = mkpool2(2, "pD", BF16)
        xsb = [sb("xsb%d" % i, [128, 3 + 512], F32) for i in range(2)]
        xcb = [sb("xcb%d" % i, [128, 512], BF16) for i in range(4)]
        hbgs = [sb("hbg%d" % i, [128, 8, 512], BF16) for i in range(2)]
        cst_p = sb("cst_p", [128, 8, 3], F32); cst_s = sb("cst_s", [128, 8, 3], F32)
        hl_p = sb("hl_p", [128, 8], F32); hl_s = sb("hl_s", [128, 8], F32)
        P.add("dve", lambda e: e.memset(cst_p.t[:, :, :], 0.0), writes=[cst_p.k])
        P.add("dve", lambda e: e.memset(hl_p.t[:, :], 0.0), writes=[hl_p.k])
        DMA(cst_s.t[:, :, :], s_cv, [], [cst_s.k], "c_scv")
        DMA(hl_s.t[:, :], s_lr, [], [hl_s.k], "c_slr")
        zlist[:] = [PB[2], PB[3]]
        XS = {}

        def stageX(g, pr):
            t0, T = groups[g]
            last = g == 4
            cst = cst_p if g < 4 else cst_s
            cs = (2 * pr, 2 * pr + 1)
            Zx = {}; Zg = {}; X = {}; G = {}; XC = {}
            for c in cs:
                Zx[c] = PB[c % 2]; zmm("xb%d" % (c // 4), (c % 4) * 128, t0, T, Zx[c])
            if last and pr % 2 == 1:
                wfree("xb%d" % (pr // 2))
            for c in cs:
                xs_ = xsb[c % 2]
                CP(xs_.t[:, 0:3], cst.t[:, c, :], [cst.k], [xs_.k])
                CP(xs_.t[:, 3:3 + T], Zx[c].t[:, :T], [Zx[c].k], [xs_.k], eng="act")
                CP(cst.t[:, c, :], xs_.t[:, T:T + 3], [xs_.k], [cst.k])
            yield
            for c in cs:
                Zg[c] = zbank(); zmm("gb%d" % (c // 4), (c % 4) * 128, t0, T, Zg[c])
                tg = poolY(); ACT(tg.t[:, :T], Zg[c].t[:, :T], AF.Tanh, [Zg[c].k], [tg.k], scale=0.5)
                G[c] = poolG()
                STT(G[c].t[:, :T], tg.t[:, :T], 1.0, Zg[c].t[:, :T], ALU.add, ALU.mult, [tg.k, Zg[c].k], [G[c].k])
            if last and pr % 2 == 1:
                wfree("gb%d" % (pr // 2))
            yield
            for c in cs:
                Z = Zx[c]
                TS(Z.t[:, :T], Z.t[:, :T], pvc(PV_WC + 24 + c), pvc(PV_BC + c), ALU.mult, ALU.add, [Z.k, pv.k], [Z.k])
            yield
            for tap, off in ((2, 16), (1, 8)):
                for c in cs:
                    Z = Zx[c]; xs_ = xsb[c % 2]
                    STT(Z.t[:, :T], xs_.t[:, tap:tap + T], pvc(PV_WC + off + c), Z.t[:, :T], ALU.mult, ALU.add,
                        [xs_.k, Z.k, pv.k], [Z.k])
                yield
            for c in cs:
                Z = Zx[c]; xs_ = xsb[c % 2]
                X[c] = poolX()
                STT(X[c].t[:, :T], xs_.t[:, 0:T], pvc(PV_WC + c), Z.t[:, :T], ALU.mult, ALU.add, [xs_.k, Z.k, pv.k], [X[c].k])
            for c in cs:
                XC[c] = xcb[(2 * pr + (c % 2)) % 4]
                CP(XC[c].t[:, :T], X[c].t[:, :T], [X[c].k], [XC[c].k], eng="act")
            XS[(g, pr)] = (X, G, XC)
            yield

        def stageY(g, pr):
            t0, T = groups[g]
            hbg = hbgs[g % 2]
            hl = hl_p if g < 4 else hl_s
            cs = (2 * pr, 2 * pr + 1)
            X, G, XC = XS.pop((g, pr))
            R = {}; A2 = {}; I_ = {}
            for c in cs:
                Rb, Ib = PB[4 + 2 * (c % 2)], PB[5 + 2 * (c % 2)]
                MM(Rb.t[:, :T], wr.t[:, c, :], XC[c].t[:, :T], True, True, [wr.k, XC[c].k], [Rb.k])
                MM(Ib.t[:, :T], wi.t[:, c, :], XC[c].t[:, :T], True, True, [wi.k, XC[c].k], [Ib.k])
            for c in cs:
                Rb, Ib = PB[4 + 2 * (c % 2)], PB[5 + 2 * (c % 2)]
                R[c] = poolY(); ACT(R[c].t[:, :T], Rb.t[:, :T], AF.Tanh, [Rb.k, dv.k], [R[c].k], bias=dvc(DV_HBR + c), scale=0.5)
                I_[c] = poolY(); ACT(I_[c].t[:, :T], Ib.t[:, :T], AF.Tanh, [Ib.k, dv.k], [I_[c].k], bias=dvc(DV_HBI + c), scale=0.5)
            yield
            for c in cs:
                ACT(R[c].t[:, :T], R[c].t[:, :T], AF.Exp, [R[c].k, dv.k], [R[c].k], bias=dvc(DV_HCL + c), scale=dvc(DV_HCL + c))
            for c in cs:
                TS(I_[c].t[:, :T], I_[c].t[:, :T], 1.0, 1.0, ALU.add, ALU.mult, [I_[c].k], [I_[c].k], eng="pool")
            yield
            for c in cs:
                A2[c] = poolY()
                TT(A2[c].t[:, :T], R[c].t[:, :T], R[c].t[:, :T], ALU.mult, [R[c].k], [A2[c].k], eng="pool")
            yield
            for c in cs:
                TT(I_[c].t[:, :T], I_[c].t[:, :T], X[c].t[:, :T], ALU.mult, [I_[c].k, X[c].k], [I_[c].k], eng="pool")
            for c in cs:
                ACT(A2[c].t[:, :T], A2[c].t[:, :T], AF.Sqrt, [A2[c].k], [A2[c].k], bias=0.25, scale=-0.25)
                if g == 0:
                    P.add("dve", lambda e, b_=A2[c]: e.memset(b_.t[:, 0:1], 0.5), writes=[A2[c].k])
            yield
            for c in cs:
                TT(I_[c].t[:, :T], I_[c].t[:, :T], A2[c].t[:, :T], ALU.mult, [I_[c].k, A2[c].k], [I_[c].k])
            yield
            for c in cs:
                P.add("dve", lambda e, hb=A2[c], a=R[c], u=I_[c], hl=hl, c=c, T=T: e.tensor_tensor_scan(
                    out=hb.t[:, :T], data0=a.t[:, :T], data1=u.t[:, :T], initial=hl.t[:, c:c + 1],
                    op0=ALU.mult, op1=ALU.add), reads=[R[c].k, I_[c].k, hl.k], writes=[A2[c].k])
                CP(hl.t[:, c:c + 1], A2[c].t[:, T - 1:T], [A2[c].k], [hl.k])
            yield
            for c in cs:
                TT(hbg.t[:, c, :T], A2[c].t[:, :T], G[c].t[:, :T], ALU.mult, [A2[c].k, G[c].k], [hbg.k], eng="pool")
            yield

        def downB(g, j):
            t0, T = groups[g]
            last = g == 4
            hbg = hbgs[g % 2]
            b = zbank()
            wbd, wbdk = W("bd%d" % (j // 4))
            jc = (j % 4) * 128
            for c in range(8):
                MM(b.t[:, :T], wbd[:, c, jc:jc + 128], hbg.t[:, c, :T], c == 0, c == 7, [wbdk, hbg.k], [b.k])
            bz = zbank(); zmm("zb%d" % (j // 4), jc, t0, T, bz)
            sgz = poolD(); ACT(sgz.t[:, :T], bz.t[:, :T], AF.Tanh, [bz.k], [sgz.k], scale=0.5)
            merge(g, j, t0, T, b, sgz)
            if last and j % 4 == 3:
                wfree("zb%d" % (j // 4)); wfree("bd%d" % (j // 4))
            if j == 7:
                mrg_init[g] = True

        def runB(*gs):
            gs = [x for x in gs if x is not None]
            while gs:
                for x in list(gs):
                    try:
                        next(x)
                    except StopIteration:
                        gs.remove(x)

        def DB(g, pr):
            yield
            yield
            yield
            yield
            downB(g, 2 * pr + 0)
            yield
            yield
            yield
            downB(g, 2 * pr + 1)
            yield

        def DA3(k):
            yield
            yield
            yield
            yield
            downA(3, 2 * k, poolD)
            yield
            yield
            yield
            downA(3, 2 * k + 1, poolD)
            yield

        seq = [(g, pr) for g in range(5) for pr in range(4)]
        runB(stageX(*seq[0]))
        for k in range(len(seq) + 4):
            runB(stageX(*seq[k + 1]) if k + 1 < len(seq) else None,
                 stageY(*seq[k]) if k < len(seq) else None,
                 DB(*seq[k - 4]) if k >= 4 else (DA3(k) if DEFER_A3 else None))
            if k < len(seq):
                g, pr = seq[k]
                if pr == 3 and g == 3:
                    DMA(o_cv_p, cst_p.t[:, :, :], [cst_p.k], [], "o_cvp")
                    DMA(o_lr_p, hl_p.t[:, :], [hl_p.k], [], "o_lrp")
                if pr == 3 and g == 4:
                    DMA(o_cv_s, cst_s.t[:, :, :], [cst_s.k], [], "o_cvs")
                    DMA(o_lr_s, hl_s.t[:, :], [hl_s.k], [], "o_lrs")
        zlist[:] = [PB[0], PB[1], PB[2]]
        ar.release(mB)

    if "M" in phases:
        mC = ar.mark()
        kT_p = sb("kT_p", [128, 4, 256], BF16); v_p = sb("v_p", [128, 2, 512], BF16)
        kT_s = sb("kT_s", [128, 4, 256], BF16); v_s = sb("v_s", [128, 2, 512], BF16)
        mM = ar.mark()
        phase_M()
        ar.release(mM)
    if "C" in phases:
        USE_RCP = False
        poolCx = mkpool(2, "tCx")
        poolCy = mkpool(4, "tCy")
        poolCd = mkpool(2, "tCd")
        qcTs = [sb("qcT%d" % i, [128, 4, 512], BF16) for i in range(2)]
        sgcs = [sb("sgc%d" % i, [128, 4, 512], BF16) for i in range(2)]
        ocgs = [sb("ocg%d" % i, [128, 4, 512], BF16) for i in range(2)]
        eT = [sb("eT%d" % i, [128, 2, 512], BF16) for i in range(2)]
        SC = [PB[4], PB[5]]; OC = PB[6]; DEN = PB[7]

        def XC(g, h):
            t0, T = groups[g]
            last = g == 4
            Z = zbank(); zmm("qc", h * 128, t0, T, Z)
            CP(qcTs[g % 2].t[:, h, :T], Z.t[:, :T], [Z.k], [qcTs[g % 2].k], eng="act")
            if last and h == 3:
                wfree("qc")
            yield
            Z = zbank(); zmm("gc", h * 128, t0, T, Z)
            tg = poolCx(); ACT(tg.t[:, :T], Z.t[:, :T], AF.Tanh, [Z.k], [tg.k], scale=0.5)
            STT(sgcs[g % 2].t[:, h, :T], tg.t[:, :T], 1.0, Z.t[:, :T], ALU.add, ALU.mult, [tg.k, Z.k], [sgcs[g % 2].k])
            if last and h == 3:
                wfree("gc")
            yield

        def YC(g, h):
            t0, T = groups[g]
            kT, vv = (kT_p, v_p) if g < 4 else (kT_s, v_s)
            qcT, sgc, ocg = qcTs[g % 2], sgcs[g % 2], ocgs[g % 2]
            e_ = eT[h % 2]
            for mj in range(2):
                s = SC[mj]
                MM(s.t[:, :T], kT.t[:, h, mj * 128:(mj + 1) * 128], qcT.t[:, h, :T], True, True, [kT.k, qcT.k], [s.k])
                ACT(e_.t[:, mj, :T], s.t[:, :T], AF.Exp, [s.k], [e_.k], scale=float(128 ** -0.5))
            yield
            for mj in range(2):
                MM(OC.t[:, :T], vv.t[:, mj, h * 128:(h + 1) * 128], e_.t[:, mj, :T], mj == 0, mj == 1, [vv.k, e_.k], [OC.k])
            for mj in range(2):
                MM(DEN.t[:, :T], ones.t[:, :], e_.t[:, mj, :T], mj == 0, mj == 1, [ones.k, e_.k], [DEN.k])
            rd = poolCy()
            if USE_RCP:
                P.add("dve", lambda e, rd=rd, T=T: e.reciprocal_approx_fast(out=rd.t[:, :T], in_=DEN.t[:, :T]),
                      reads=[DEN.k], writes=[rd.k])
            else:
                ACT(rd.t[:, :T], DEN.t[:, :T], AF.Ln, [DEN.k], [rd.k])
                ACT(rd.t[:, :T], rd.t[:, :T], AF.Exp, [rd.k], [rd.k], scale=-1.0)
            yield
            t_ = poolCy(); TT(t_.t[:, :T], OC.t[:, :T], rd.t[:, :T], ALU.mult, [OC.k, rd.k], [t_.k])
            TT(ocg.t[:, h, :T], t_.t[:, :T], sgc.t[:, h, :T], ALU.mult, [t_.k, sgc.k], [ocg.k], eng="pool")
            yield

        def downC(g, j):
            t0, T = groups[g]
            last = g == 4
            ocg = ocgs[g % 2]
            wcd, wcdk = W("cd", 4)
            b = zbank()
            for c in range(4):
                MM(b.t[:, :T], wcd[:, c, j * 128:(j + 1) * 128], ocg.t[:, c, :T], c == 0, c == 3, [wcdk, ocg.k], [b.k])
            jc = (j % 4) * 128
            bz = zbank(); zmm("zc%d" % (j // 4), jc, t0, T, bz)
            sgz = poolCd(); ACT(sgz.t[:, :T], bz.t[:, :T], AF.Tanh, [bz.k], [sgz.k], scale=0.5)
            merge(g, j, t0, T, b, sgz)
            if last and j % 4 == 3:
                wfree("zc%d" % (j // 4))
            if last and j == 7:
                wfree("cd")
            if j == 7:
                mrg_init[g] = True

        def DC(g, js):
            downC(g, js[0])
            yield
            yield
            downC(g, js[1])
            yield

        def runC(*gs):
            gs = [x for x in gs if x is not None]
            while gs:
                for x in list(gs):
                    try:
                        next(x)
                    except StopIteration:
                        gs.remove(x)

        for h in range(4):
            runC(XC(0, h))
        for g in range(6):
            for h in range(4):
                runC(XC(g + 1, h) if g + 1 < 5 else None,
                     YC(g, h) if g < 5 else None,
                     DC(g - 1, (2 * h, 2 * h + 1)) if g >= 1 else None)
            if g < 5:
                DBG("ocg%d" % g, ocgs[g % 2].t[:, :, :], [128, 4, 512], [ocgs[g % 2].k])
    if "M" in phases:
        ar.release(mC)

    if "D" in phases:
        mD = ar.mark()
        xt = [sb("xtD%d" % i, [128, D], F32) for i in range(2)]
        yb = [sb("ybD%d" % i, [128, D], F32) for i in range(2)]
        yo = [sb("yoD%d" % i, [128, D], F32) for i in range(2)]
        junk = sb("junkD", [128, D], BF16)
        gfin = sb("gfin", [128, D], F32)
        DMA(gfin.t[:, :], gf_d, [], [gfin.k], "c_gf")
        st8 = [sb("st8D%d" % i, [128, 8], F32) for i in range(2)]
        wo = [W("o0"), W("o1")]
        def D_a(i):
            nrows = 128 if i < 16 else DSEQ
            t0 = i * 128
            src = xp[t0:t0 + 128, :] if i < 16 else xsm[:, :]
            g = min(i // 4, 4)
            x, y_ = xt[i % 2], yb[i % 2]
            DMA(x.t[:nrows, :], src, [], [x.k], ("xtD", i % 2))
            for n in range(2):
                b = zbank()
                wt_, wtk = wo[n]
                for kc in range(KC):
                    MM(b.t[:nrows, :], mrg[:, kc, t0:t0 + nrows], wt_[:, kc, :], kc == 0, kc == KC - 1,
                       [mrgk[g], wtk], [b.k])
                STT(y_.t[:nrows, n * 512:(n + 1) * 512], b.t[:nrows, :], 0.25, x.t[:nrows, n * 512:(n + 1) * 512],
                    ALU.mult, ALU.add, [b.k, x.k], [y_.k])

        def D_b(i):
            nrows = 128 if i < 16 else DSEQ
            t0 = i * 128
            dst = y_p[t0:t0 + 128, :] if i < 16 else y_s[:, :]
            y_, yo_, s8 = yb[i % 2], yo[i % 2], st8[i % 2]
            ACT(junk.t[:nrows, :], y_.t[:nrows, :], AF.Square, [y_.k], [junk.k, s8.k], accum=s8.t[:nrows, 0:1])
            ACT(s8.t[:nrows, 1:2], s8.t[:nrows, 0:1], AF.Sqrt, [s8.k], [s8.k], bias=EPS, scale=1.0 / D)
            P.add("dve", lambda e, s8=s8, nrows=nrows: e.reciprocal(out=s8.t[:nrows, 2:3], in_=s8.t[:nrows, 1:2]),
                  reads=[s8.k], writes=[s8.k])
            STT(yo_.t[:nrows, :], y_.t[:nrows, :], s8.t[:nrows, 2:3], gfin.t[:nrows, :], ALU.mult, ALU.mult,
                [y_.k, s8.k, gfin.k], [yo_.k], eng="pool" if i % 2 else "dve")
            DMA(dst, yo_.t[:nrows, :], [yo_.k], [], ("yoD", i % 2), q="act")

        for i in range(18):
            if i < 17:
                D_a(i)
            if i > 0:
                D_b(i - 1)
        ar.release(mD)

    P.run()
    es.close()
    return nc, ar


def _bd(w):
    o = np.zeros((128, 8, 128), np.float32)
    for n in range(16):
        c, q = divmod(n, 2)
        o[q * 64:(q + 1) * 64, c, q * 64:(q + 1) * 64] = w[n]
    return o


def _pm(v):
    return np.ascontiguousarray(np.asarray(v, np.float32).reshape(-1, 128).T)


def make_in_maps(inp):
    f = lambda a: np.ascontiguousarray(np.asarray(a, dtype=np.float32))
    pvec = np.zeros((128, NPV), np.float32)
    pvec[:, PV_GMIX:PV_GMIX + 8] = _pm(inp["g_mix"][0])
    pvec[:, PV_GMEM:PV_GMEM + 8] = _pm(inp["g_mem"][0])
    pvec[:, PV_L0:PV_L0 + 4] = _pm(inp["lb_logits"][0])
    pvec[:, PV_L1:PV_L1 + 4] = _pm(inp["lb_logits"][1])
    pvec[:, PV_GA:PV_GA + 4] = _pm(inp["g_a_out"][0])
    for j in range(4):
        pvec[:, PV_WC + 8 * j:PV_WC + 8 * j + 8] = _pm(inp["w_conv"][0][j])
    pvec[:, PV_BC:PV_BC + 8] = _pm(inp["b_conv"][0])
    pvec[:, PV_BR:PV_BR + 8] = _pm(inp["b_lru_r"][0])
    pvec[:, PV_BI:PV_BI + 8] = _pm(inp["b_lru_i"][0])
    pvec[:, PV_LAM:PV_LAM + 8] = _pm(inp["lru_lambda"][0])
    gfin = np.ascontiguousarray(np.broadcast_to(f(inp["g_final"])[None, :], (128, D)))
    ident = np.eye(128, dtype=np.float32)
    s = np.arange(128)[:, None]
    t = np.arange(128)[None, :]
    amask = ((s // HCH == t // HCH) & (t >= s)).astype(np.float32)
    rmask = np.ones((128, 512), np.float32)
    rmask[:, ::HCH] = 0.0
    shared = {
        "w_in": f(inp["w_in"][0]), "w_ad": f(inp["w_a_down"][0]), "w_bd": f(inp["w_b_down"][0]),
        "w_cd": f(inp["w_c_down"][0]), "w_o": f(inp["w_out"][0]), "w_mk": f(inp["w_mem_k"][0]),
        "w_mv": f(inp["w_mem_v"][0]), "wr_bd": _bd(f(inp["w_lru_r"][0])), "wi_bd": _bd(f(inp["w_lru_i"][0])),
        "pv": pvec, "gfin": gfin, "ident": ident, "amask": amask, "rmask": rmask,
    }
    maps = []
    for b in range(8):
        m = dict(shared)
        m["xp"] = f(inp["x_prompt"][b])
        m["xs"] = f(inp["x_sample"][b])
        m["mem"] = f(inp["mem_prompt"][b])
        m["cmk"] = f(inp["cache_mem_k"][0, b]).reshape(NMEM, 512)
        m["cmv"] = f(inp["cache_mem_v"][0, b]).reshape(NMEM, 512)
        m["s_hg"] = f(inp["state_hgrn"][0, b])
        m["s_cv"] = np.ascontiguousarray(f(inp["state_conv"][0, b]).reshape(3, 8, 128).transpose(2, 1, 0))
        m["s_lr"] = _pm(inp["state_lru"][0, b])
        maps.append(m)
    return maps


_CACHE = {}


def kernel(**inputs):
    if "nc" not in _CACHE:
        _CACHE["nc"] = build()[0]
    nc = _CACHE["nc"]
    maps = make_in_maps(inputs)
    res = run_bass_kernel_spmd(nc, maps, core_ids=list(range(8)))
    R = res.results
    st = lambda k: np.stack([np.asarray(r[k], np.float32) for r in R])
    y_p = st("y_p")
    y_s = st("y_s")
    hg_p = st("o_hg_p")[None]
    cv_p = st("o_cv_p").transpose(0, 3, 2, 1).reshape(8, 3, 1024)[None]
    lr_p = st("o_lr_p").transpose(0, 2, 1).reshape(8, 1024)[None]
    mk = st("o_mk").reshape(8, NMEM, 4, 128)[None]
    mv = st("o_mv").reshape(8, NMEM, 4, 128)[None]
    hg_s = st("o_hg_s")[None]
    cv_s = st("o_cv_s").transpose(0, 3, 2, 1).reshape(8, 3, 1024)[None]
    lr_s = st("o_lr_s").transpose(0, 2, 1).reshape(8, 1024)[None]
    return (y_p, y_s, hg_p, np.ascontiguousarray(cv_p), np.ascontiguousarray(lr_p), mk, mv,
            hg_s, np.ascontiguousarray(cv_s), np.ascontiguousarray(lr_s))
```
